# Optimizing a Trainium2 kernel written in Bass

```python
import math
import jax
import jax.numpy as jnp
from jax import lax
import numpy as np

D_MODEL = 2048
BATCH = 2
SEQ = 4096
DEPTH = 2

CHUNK = 64
EPS = 1e-6
MIX_WIDTH = D_MODEL
GROUP_WIDTH = MIX_WIDTH // 2

RET_HEADS = 4
RET_DK = GROUP_WIDTH // RET_HEADS
RET_DV = GROUP_WIDTH // RET_HEADS
ROPE_BASE = 10000.0
SGU_WINDOW = 128
SGU_GROUPS = 4
SGU_DG = GROUP_WIDTH // SGU_GROUPS
HG_HEADS = 8
HG_DK = GROUP_WIDTH // HG_HEADS
HG_DV = GROUP_WIDTH // HG_HEADS
DSA_HEADS = 8
DSA_DV = GROUP_WIDTH // DSA_HEADS
DSA_Q_RANK = 384
DSA_KV_RANK = 256
IDX_HEADS = 16
IDX_DIM = 64
TOPK_MAX = 256
Q_BLOCK = 128
REL_BUCKETS = 32
REL_MAX_DIST = 256
D_FF = ((-(-8 * D_MODEL // 3)) + 255) // 256 * 256

EVEN_IN = 4 * GROUP_WIDTH + 2 * GROUP_WIDTH
ODD_IN = 4 * GROUP_WIDTH + DSA_Q_RANK + DSA_KV_RANK + IDX_DIM + IDX_HEADS

kernel_name = "hybrid_retention_sgu_hgrn2_dsa_trunk"

F32 = jnp.float32


def rms_norm(x, g=None):
    xf = x.astype(F32)
    y = xf * lax.rsqrt(jnp.mean(xf * xf, axis=-1, keepdims=True) + EPS)
    if g is not None:
        y = y * g.astype(F32)
    return y.astype(x.dtype)


def layer_norm(x, g, b):
    xf = x.astype(F32)
    mu = jnp.mean(xf, axis=-1, keepdims=True)
    var = jnp.mean(jnp.square(xf - mu), axis=-1, keepdims=True)
    return ((xf - mu) * lax.rsqrt(var + EPS) * g.astype(F32) + b.astype(F32)).astype(x.dtype)


def rotary(x, pos):
    d = x.shape[-1]
    inv = ROPE_BASE ** (-jnp.arange(0, d, 2, dtype=F32) / d)
    ang = pos.astype(F32)[:, None] * inv[None, :]
    cos = jnp.cos(ang)[None, :, None, :]
    sin = jnp.sin(ang)[None, :, None, :]
    xf = x.astype(F32)
    x1, x2 = xf[..., : d // 2], xf[..., d // 2:]
    return jnp.concatenate([x1 * cos - x2 * sin, x1 * sin + x2 * cos], axis=-1)


def to_chunks(t, nc):
    b, s, h, d = t.shape
    return t.astype(F32).reshape(b, nc, CHUNK, h, d).transpose(1, 0, 3, 2, 4)


def from_chunks(t):
    nc, b, h, c, d = t.shape
    return t.transpose(1, 0, 3, 2, 4).reshape(b, nc * c, h, d)


def retention(q, k, v):
    b, s, h, dk = q.shape
    nc = s // CHUNK
    log_gamma = jnp.log(1.0 - 2.0 ** (-5.0 - jnp.arange(h, dtype=F32)))
    qc = to_chunks(q, nc)
    kc = to_chunks(k * (dk ** -0.5), nc)
    vc = to_chunks(v, nc)
    pos = jnp.arange(CHUNK, dtype=F32)
    d_intra = jnp.exp(log_gamma[:, None, None] * jnp.abs(pos[:, None] - pos[None, :]))
    scores = jnp.einsum('nbhid,nbhjd->nbhij', qc, kc) * d_intra[None, None]
    intra = jnp.einsum('nbhij,nbhje->nbhie', scores, vc)
    xi = jnp.exp(log_gamma[:, None] * (pos + 1.0))[None, :, :, None]
    zeta = jnp.exp(log_gamma[:, None] * (CHUNK - 1.0 - pos))[None, :, :, None]
    g_chunk = jnp.exp(log_gamma * CHUNK)[None, :, None, None]

    def step(state, inp):
        qi, ki, vi = inp
        cross = jnp.einsum('bhid,bhde->bhie', qi, state) * xi
        state = state * g_chunk + jnp.einsum('bhjd,bhje->bhde', ki * zeta, vi)
        return state, cross

    s0 = jnp.zeros((b, h, dk, v.shape[-1]), F32)
    _, cross = lax.scan(step, s0, (qc, kc, vc))
    return from_chunks(intra + cross)


def spatial_gating(u, v, ln_g, ln_b, w_s, b_s):
    b, s, _ = v.shape
    v = layer_norm(v, ln_g, ln_b).astype(F32)
    ch = jnp.arange(SGU_WINDOW) // CHUNK
    mask = ch[None, :] <= ch[:, None]
    w = jnp.where(mask[None], w_s.astype(F32), 0.0)
    vw = v.reshape(b, s // SGU_WINDOW, SGU_WINDOW, SGU_GROUPS, SGU_DG)
    mixed = jnp.einsum('gij,bnjgc->bnigc', w, vw) + b_s.astype(F32).T[None, None, :, :, None]
    return u.astype(F32) * mixed.reshape(b, s, GROUP_WIDTH)


def hgrn2(q, f_logits, i, lower_bound):
    b, s, h, dk = q.shape
    nc = s // CHUNK
    lb = lower_bound.astype(F32).reshape(h, dk)
    f = lb + (1.0 - lb) * jax.nn.sigmoid(f_logits.astype(F32))
    log_f = jnp.log(f)
    k = 1.0 - f
    qa = jax.nn.silu(q.astype(F32))
    qc, kc, lfc, vc = to_chunks(qa, nc), to_chunks(k, nc), to_chunks(log_f, nc), to_chunks(i, nc)
    causal = jnp.tril(jnp.ones((CHUNK, CHUNK), dtype=bool))

    def step(state, inp):
        qi, ki, lfi, vi = inp
        bcum = jnp.cumsum(lfi, axis=2)
        diff = bcum[:, :, :, None, :] - bcum[:, :, None, :, :]
        decay = jnp.exp(jnp.where(causal[:, :, None], diff, -jnp.inf))
        attn = jnp.einsum('bhtd,bhsd,bhtsd->bhts', qi, ki, decay)
        intra = jnp.einsum('bhts,bhse->bhte', attn, vi)
        cross = jnp.einsum('bhtd,bhde->bhte', qi * jnp.exp(bcum), state)
        blast = bcum[:, :, -1, :]
        state = state * jnp.exp(blast)[..., None] + jnp.einsum(
            'bhsd,bhse->bhde', ki * jnp.exp(blast[:, :, None, :] - bcum), vi)
        return state, intra + cross

    s0 = jnp.zeros((b, h, dk, i.shape[-1]), F32)
    _, out = lax.scan(step, s0, (qc, kc, lfc, vc))
    return from_chunks(out)


def rel_bucket(rel):
    nb = REL_BUCKETS // 2
    max_exact = nb // 2
    ret = jnp.where(rel > 0, nb, 0)
    n = jnp.abs(rel)
    nf = jnp.maximum(n, 1).astype(F32)
    large = max_exact + (jnp.log(nf / max_exact) / math.log(REL_MAX_DIST / max_exact)
                         * (nb - max_exact)).astype(jnp.int32)
    large = jnp.minimum(large, nb - 1)
    return ret + jnp.where(n < max_exact, n, large)


def dsa_attention(c_q, c_kv, k_idx, w_idx, cq_g, ckv_g, w_uq, qn_g, w_qidx, w_uv, rel_bias):
    b, s, _ = c_q.shape
    n_blk = s // Q_BLOCK
    k_sel = min(TOPK_MAX, s // 4)
    cq = rms_norm(c_q, cq_g).astype(F32)
    q = (cq @ w_uq.astype(F32)).reshape(b, s, DSA_HEADS, DSA_KV_RANK)
    q = rms_norm(q, qn_g)
    kv = rms_norm(c_kv, ckv_g).astype(F32)
    q_idx = (cq @ w_qidx.astype(F32)).reshape(b, s, IDX_HEADS, IDX_DIM)
    kix = k_idx.astype(F32)
    w_h = w_idx.astype(F32) * (IDX_HEADS ** -0.5)
    key_chunk = jnp.arange(s, dtype=jnp.int32) // CHUNK

    def blocks(t):
        return t.reshape((b, n_blk, Q_BLOCK) + t.shape[2:]).swapaxes(0, 1)

    def one_block(inp):
        qb, qib, wb, blk = inp
        t = blk * Q_BLOCK + jnp.arange(Q_BLOCK, dtype=jnp.int32)
        sc = jnp.einsum('bqhd,bsd->bqhs', qib, kix) * (IDX_DIM ** -0.5)
        sc = jnp.einsum('bqh,bqhs->bqs', wb, jax.nn.relu(sc))
        admissible = key_chunk[None, :] <= (t // CHUNK)[:, None]
        sc = jnp.where(admissible[None], sc, -jnp.inf)
        _, idx = lax.top_k(sc, k_sel)
        valid = (idx // CHUNK) <= (t // CHUNK)[None, :, None]
        kv_sel = jnp.take_along_axis(kv, idx.reshape(b, -1)[..., None], axis=1)
        kv_sel = kv_sel.reshape(b, Q_BLOCK, k_sel, DSA_KV_RANK)
        logits = jnp.einsum('bqhr,bqkr->bqhk', qb.astype(F32), kv_sel) * (DSA_KV_RANK ** -0.5)
        bias = rel_bias.astype(F32)[rel_bucket(idx - t[None, :, None])]
        logits = logits + bias.transpose(0, 1, 3, 2)
        logits = jnp.where(valid[:, :, None, :], logits, -jnp.inf)
        p = jax.nn.softmax(logits, axis=-1)
        return jnp.einsum('bqhk,bqkr->bqhr', p, kv_sel)

    o = lax.map(one_block, (blocks(q), blocks(q_idx), blocks(w_h), jnp.arange(n_blk, dtype=jnp.int32)))
    o = o.swapaxes(0, 1).reshape(b, s, DSA_HEADS, DSA_KV_RANK)
    return jnp.einsum('bshr,hrd->bshd', o, w_uv.astype(F32)).reshape(b, s, GROUP_WIDTH)


def even_mixer(h, pos, w_in, w_out, sgu_ln_g, sgu_ln_b, sgu_w_s, sgu_b_s):
    b, s, _ = h.shape
    z = h @ w_in
    q, k, v, g, u, vs = jnp.split(z, 6, axis=-1)
    q = rotary(q.reshape(b, s, RET_HEADS, RET_DK), pos)
    k = rotary(k.reshape(b, s, RET_HEADS, RET_DK), pos)
    ret = retention(q, k, v.reshape(b, s, RET_HEADS, RET_DV))
    ret = rms_norm(ret).reshape(b, s, GROUP_WIDTH) * jax.nn.silu(g.astype(F32))
    sgu = spatial_gating(jax.nn.gelu(u, approximate=False), jax.nn.gelu(vs, approximate=False),
                         sgu_ln_g, sgu_ln_b, sgu_w_s, sgu_b_s)
    mixed = jnp.concatenate([ret.astype(h.dtype), sgu.astype(h.dtype)], axis=-1)
    return mixed @ w_out


def odd_mixer(h, lb, w_in, w_out, hgrn_norm_g, cq_g, ckv_g, w_uq, qn_g, w_qidx, w_uv, rel_bias):
    b, s, _ = h.shape
    z = h @ w_in
    offs = [int(o) for o in np.cumsum([GROUP_WIDTH] * 4 + [DSA_Q_RANK, DSA_KV_RANK, IDX_DIM])]
    hq, hf, hi, hg, c_q, c_kv, k_idx, w_idx = jnp.split(z, offs, axis=-1)
    hg_out = hgrn2(hq.reshape(b, s, HG_HEADS, HG_DK), hf.reshape(b, s, HG_HEADS, HG_DK),
                   hi.reshape(b, s, HG_HEADS, HG_DV), lb)
    hg_out = rms_norm(hg_out, hgrn_norm_g.reshape(HG_HEADS, HG_DV)).reshape(b, s, GROUP_WIDTH)
    hg_out = hg_out * jax.nn.silu(hg.astype(F32))
    attn = dsa_attention(c_q, c_kv, k_idx, w_idx, cq_g, ckv_g, w_uq, qn_g, w_qidx, w_uv, rel_bias)
    mixed = jnp.concatenate([hg_out.astype(h.dtype), attn.astype(h.dtype)], axis=-1)
    return mixed @ w_out


def swiglu(h, wg, wu, wd):
    return (jax.nn.silu(h @ wg) * (h @ wu)) @ wd


def setup_inputs(seed: int = 0) -> dict:
    key = jax.random.key(seed)
    ks = iter(jax.random.split(key, 32))
    n_even = (DEPTH + 1) // 2
    n_odd = DEPTH // 2

    def nrm(shape, scale):
        return jax.random.normal(next(ks), shape, F32) * scale

    def gain(shape):
        return 1.0 + 0.01 * jax.random.normal(next(ks), shape, F32)

    return {
        "x": nrm((BATCH, SEQ, D_MODEL), 1.0),
        "ln_mix_g": gain((DEPTH, D_MODEL)),
        "ln_ffn_g": gain((DEPTH, D_MODEL)),
        "w_ffn_gate": nrm((DEPTH, D_MODEL, D_FF), D_MODEL ** -0.5),
        "w_ffn_up": nrm((DEPTH, D_MODEL, D_FF), D_MODEL ** -0.5),
        "w_ffn_down": nrm((DEPTH, D_FF, D_MODEL), D_FF ** -0.5),
        "rel_bias": nrm((REL_BUCKETS, DSA_HEADS), 0.2),
        "ev_w_in": nrm((n_even, D_MODEL, EVEN_IN), D_MODEL ** -0.5),
        "ev_w_out": nrm((n_even, MIX_WIDTH, D_MODEL), MIX_WIDTH ** -0.5),
        "sgu_ln_g": gain((n_even, GROUP_WIDTH)),
        "sgu_ln_b": nrm((n_even, GROUP_WIDTH), 0.01),
        "sgu_w_s": nrm((n_even, SGU_GROUPS, SGU_WINDOW, SGU_WINDOW), SGU_WINDOW ** -0.5),
        "sgu_b_s": gain((n_even, SGU_GROUPS, SGU_WINDOW)),
        "od_w_in": nrm((n_odd, D_MODEL, ODD_IN), D_MODEL ** -0.5),
        "od_w_out": nrm((n_odd, MIX_WIDTH, D_MODEL), MIX_WIDTH ** -0.5),
        "hgrn_lb": nrm((DEPTH, GROUP_WIDTH), 0.1),
        "hgrn_norm_g": gain((n_odd, GROUP_WIDTH)),
        "dsa_cq_g": gain((n_odd, DSA_Q_RANK)),
        "dsa_ckv_g": gain((n_odd, DSA_KV_RANK)),
        "dsa_w_uq": nrm((n_odd, DSA_Q_RANK, DSA_HEADS * DSA_KV_RANK), DSA_Q_RANK ** -0.5),
        "dsa_qnorm_g": gain((n_odd, DSA_KV_RANK)),
        "dsa_w_qidx": nrm((n_odd, DSA_Q_RANK, IDX_HEADS * IDX_DIM), DSA_Q_RANK ** -0.5),
        "dsa_w_uv": nrm((n_odd, DSA_HEADS, DSA_KV_RANK, DSA_DV), DSA_KV_RANK ** -0.5),
    }


def reference(x, ln_mix_g, ln_ffn_g, w_ffn_gate, w_ffn_up, w_ffn_down, rel_bias,
              ev_w_in, ev_w_out, sgu_ln_g, sgu_ln_b, sgu_w_s, sgu_b_s,
              od_w_in, od_w_out, hgrn_lb, hgrn_norm_g, dsa_cq_g, dsa_ckv_g,
              dsa_w_uq, dsa_qnorm_g, dsa_w_qidx, dsa_w_uv):
    s = x.shape[1]
    pos = jnp.arange(s, dtype=jnp.int32)
    lb_soft = jax.nn.softmax(hgrn_lb.astype(F32), axis=0)
    lb_layers = jnp.cumsum(lb_soft, axis=0) - lb_soft[0]
    for layer in range(DEPTH):
        h = rms_norm(x, ln_mix_g[layer])
        j = layer // 2
        if layer % 2 == 0:
            mix = even_mixer(h, pos, ev_w_in[j], ev_w_out[j], sgu_ln_g[j], sgu_ln_b[j],
                             sgu_w_s[j], sgu_b_s[j])
        else:
            mix = odd_mixer(h, lb_layers[layer], od_w_in[j], od_w_out[j], hgrn_norm_g[j],
                            dsa_cq_g[j], dsa_ckv_g[j], dsa_w_uq[j], dsa_qnorm_g[j],
                            dsa_w_qidx[j], dsa_w_uv[j], rel_bias)
        x = x + mix.astype(x.dtype)
        h = rms_norm(x, ln_ffn_g[layer])
        x = x + swiglu(h, w_ffn_gate[layer], w_ffn_up[layer], w_ffn_down[layer]).astype(x.dtype)
    return x
```

```python
import math
from contextlib import ExitStack

import numpy as np
import concourse.bass as bass
import concourse.mybir as mybir
from concourse.bass_utils import run_bass_kernel_spmd

F32 = mybir.dt.float32
F32R = mybir.dt.float32r
ALU = mybir.AluOpType
AF = mybir.ActivationFunctionType
AX = mybir.AxisListType

NCORES = 8
T = 1024
D = 2048
KC = D // 128
DFF = 5632
FC = DFF // 128
EPS = 1e-6
SELF_SYNC = True


class Lane:
    __slots__ = ("name", "sem", "cnt", "step")

    def __init__(self, name, sem, step):
        self.name, self.sem, self.cnt, self.step = name, sem, 0, step


class Buf:
    __slots__ = ("name", "lw", "rd", "dlane")

    def __init__(self, name):
        self.name = name
        self.lw = None
        self.rd = {}
        self.dlane = None


class Prog:
    ENGS = ("pe", "act", "dve", "pool", "sp")

    def __init__(self, nc, stack, self_sync=True):
        self.nc = nc
        self.stack = stack
        self.q = {e: [] for e in self.ENGS}
        self.lanes = {}
        for e in ("pe", "act", "dve", "pool"):
            self.lanes[e] = Lane(e, stack.enter_context(nc.semaphore("s_" + e)), 1)
        self.alllanes = list(self.lanes.values())
        self.waited = {e: {} for e in self.ENGS}
        self.self_sync = self_sync
        self.nbuf = 0
        self.ndl = 0
        self.bufmap = {}
        self.shared = {}

    def buf(self, name=None):
        self.nbuf += 1
        name = name or f"b{self.nbuf}"
        if name not in self.bufmap:
            self.bufmap[name] = Buf(name)
        return self.bufmap[name]

    def bufs(self, n, name="b"):
        return [self.buf(f"{name}{i}") for i in range(n)]

    def dlane(self, b):
        if b.dlane is None:
            self.ndl += 1
            b.dlane = Lane("d_" + b.name, self.stack.enter_context(self.nc.semaphore(f"sd{self.ndl}")), 16)
            self.alllanes.append(b.dlane)
        return b.dlane

    def _deps(self, eng, reads, writes):
        deps = {}
        for b in reads:
            if b.lw is not None and deps.get(b.lw[0], 0) < b.lw[1]:
                deps[b.lw[0]] = b.lw[1]
        for b in writes:
            if b.lw is not None and deps.get(b.lw[0], 0) < b.lw[1]:
                deps[b.lw[0]] = b.lw[1]
            for ln, v in b.rd.items():
                if deps.get(ln, 0) < v:
                    deps[ln] = v
        out = []
        w = self.waited[eng]
        for ln, v in deps.items():
            if ln.name == eng and (eng == "pe" or not self.self_sync):
                continue
            if w.get(ln, 0) >= v:
                continue
            w[ln] = v
            out.append((ln, v))
        return out

    def _mark(self, lane, val, reads, writes):
        for b in reads:
            if b.rd.get(lane, 0) < val:
                b.rd[lane] = val
        for b in writes:
            b.lw = (lane, val)
            b.rd = {}

    def op(self, eng, fn, reads=(), writes=()):
        waits = self._deps(eng, reads, writes)
        lane = self.lanes[eng]
        lane.cnt += 1
        self.q[eng].append((fn, waits, lane, 1))
        self._mark(lane, lane.cnt, reads, writes)

    def dma(self, q, out_ap, in_ap, reads, writes, lane_buf=None):
        waits = self._deps(q, reads, writes)
        if lane_buf is None:
            if q not in self.shared:
                self.shared[q] = self.buf("shared_" + q)
            lane = self.dlane(self.shared[q])
            if lane.cnt > 0 and self.waited[q].get(lane, 0) < lane.cnt:
                self.waited[q][lane] = lane.cnt
                waits.append((lane, lane.cnt))
        else:
            lane = self.dlane(lane_buf)
        lane.cnt += 16

        def fn(e, out_ap=out_ap, in_ap=in_ap):
            return e.dma_start(out=out_ap, in_=in_ap)
        self.q[q].append((fn, waits, lane, 16))
        self._mark(lane, lane.cnt, reads, writes)

    def barrier(self):
        for e in self.ENGS:
            waits = []
            w = self.waited[e]
            for ln in self.alllanes:
                if ln.cnt > 0 and w.get(ln, 0) < ln.cnt and not (ln.name == e and e == "pe"):
                    w[ln] = ln.cnt
                    waits.append((ln, ln.cnt))
            if waits:
                self.q[e].append((None, waits, None, 0))

    def emit(self):
        nc = self.nc
        engmap = {"pe": "tensor", "act": "scalar", "dve": "vector", "pool": "gpsimd", "sp": "sync"}
        with nc.Block() as block:
            for e in self.ENGS:
                items = self.q[e]

                def body(engine, items=items):
                    for fn, waits, lane, step in items:
                        for ln, v in waits:
                            engine.wait_ge(ln.sem, v)
                        if fn is not None:
                            fn(engine).then_inc(lane.sem, step)
                if items:
                    getattr(block, engmap[e])(body)


class Ctx:
    def __init__(self, name):
        self.nc = bass.Bass("TRN2", target_bir_lowering=False)
        self.st = ExitStack()
        self.P = Prog(self.nc, self.st, self_sync=SELF_SYNC)
        self.name = name
        self.nt = 0
        nc, P = self.nc, self.P
        self.ps = [self.st.enter_context(nc.psum_tensor(f"ps{i}", [128, 512], F32)) for i in range(8)]
        self.ps_b = P.bufs(8, "ps")
        self.ones = self.sb("ones", [128, 128])
        self.ones_b = P.buf("ones")
        P.op("dve", lambda e: e.memset(self.ones[:], 1.0), [], [self.ones_b])
        self.eps = self.sb("epsc", [128, 1])
        self.eps_b = P.buf("eps")
        P.op("dve", lambda e: e.memset(self.eps[:], EPS), [], [self.eps_b])
        self.psrot = 0

    def sb(self, name, shape, dt=F32):
        if not hasattr(self, "_sbmap"):
            self._sbmap = {}
        if name not in self._sbmap:
            self._sbmap[name] = self.st.enter_context(self.nc.sbuf_tensor(name, shape, dt))
        return self._sbmap[name]

    def din(self, name, shape, dt=F32):
        return self.nc.dram_tensor(name, list(shape), dt, kind="ExternalInput").ap()

    def dout(self, name, shape, dt=F32):
        return self.nc.dram_tensor(name, list(shape), dt, kind="ExternalOutput").ap()

    def dscr(self, name, shape, dt=F32):
        return self.nc.dram_tensor(name, list(shape), dt, kind="Internal").ap()

    def finish(self, final_bufs):
        P = self.P
        waits = P._deps("sp", final_bufs, final_bufs)
        P.q["sp"].append((None, waits, None, 0))
        P.emit()
        self.st.close()
        return self.nc

    def nextps(self, lo=4, n=4):
        i = lo + self.psrot % n
        self.psrot += 1
        return i


def load_X(c, X, X_b, xT_d, q="sp"):
    P = c.P
    for g in range(4):
        P.dma(q, X[:, 4 * g:4 * g + 4, :], xT_d[:, 4 * g:4 * g + 4, :], [], X_b[8 * g:8 * g + 8], X_b[8 * g])


def store_X(c, X, X_b, out_d, q="sp"):
    P = c.P
    for g in range(4):
        P.dma(q, out_d[:, 4 * g:4 * g + 4, :], X[:, 4 * g:4 * g + 4, :], X_b[8 * g:8 * g + 8], [], X_b[8 * g])


def rmsnorm_fm(c, X, X_b, Hr, H_b, g_sb, g_b, sq, sq_b, rstd, rstd_b):
    P = c.P
    for half in range(2):
        hs = slice(half * 512, (half + 1) * 512)
        pb = half
        for k in range(KC):
            sl = (half * KC + k) % 2
            ss = slice(sl * 512, (sl + 1) * 512)
            P.op("act", lambda e, k=k, hs=hs, ss=ss: e.activation(out=sq[:, ss], in_=X[:, k, hs], func=AF.Square),
                 [X_b[2 * k + half]], [sq_b[sl]])
            P.op("pe", lambda e, k=k, ss=ss, pb=pb: e.matmul(c.ps[pb][:], c.ones[:], sq[:, ss], start=(k == 0), stop=(k == KC - 1)),
                 [c.ones_b, sq_b[sl]], [c.ps_b[pb]])
        P.op("act", lambda e, hs=hs, pb=pb: e.activation(out=rstd[:, hs], in_=c.ps[pb][:], func=AF.Sqrt, bias=c.eps[:, 0:1], scale=1.0 / D),
             [c.ps_b[pb], c.eps_b], [rstd_b[half]])
        P.op("dve", lambda e, hs=hs: e.reciprocal(out=rstd[:, hs], in_=rstd[:, hs]), [rstd_b[half]], [rstd_b[half]])
        for k in range(KC):
            P.op("dve", lambda e, k=k, hs=hs: e.scalar_tensor_tensor(out=Hr[:, k, hs], in0=X[:, k, hs], scalar=g_sb[:, k:k + 1], in1=rstd[:, hs],
                                                                     op0=ALU.mult, op1=ALU.mult),
                 [X_b[2 * k + half], g_b, rstd_b[half]], [H_b])


def ffn_phase(c, X, X_b, Hr, H_b, wg_d, wu_d, wd_d, WR, tmp=None, tmp2=None):
    P = c.P
    wgu = [(WR[:, s * 4096:s * 4096 + 2048].bitcast(F32R), WR[:, s * 4096 + 2048:(s + 1) * 4096].bitcast(F32R)) for s in range(2)]
    wgu_b = [(P.buf(f"wg{s}"), P.buf(f"wu{s}")) for s in range(2)]
    wd = [WR[:, 8192 + s * 2048:8192 + (s + 1) * 2048].bitcast(F32R) for s in range(2)]
    wd_b = P.bufs(2, "wd")
    act = [WR[:, 12288 + s * 1024:12288 + (s + 1) * 1024] for s in range(2)]
    act_b = P.bufs(2, "actt")
    if tmp is None:
        tmp = c.sb("ffn_tmp", [128, 1024])
    tmp_b = P.bufs(2, "ffntmp")
    tmp2_b = P.bufs(2, "ffntmp2")

    def issue_loads(f):
        s = f % 2
        P.dma("pool", wgu[s][0], wg_d[f], [], [wgu_b[s][0]], wgu_b[s][0])
        P.dma("pool", wgu[s][1], wu_d[f], [], [wgu_b[s][1]], wgu_b[s][1])
        P.dma("pool", wd[s], wd_d[f * 128:(f + 1) * 128, :], [], [wd_b[s]], wd_b[s])

    issue_loads(0)
    for f in range(FC):
        s = f % 2
        if f + 1 < FC:
            issue_loads(f + 1)
        wg3 = wgu[s][0].rearrange("p (k j) -> p k j", k=KC)
        wu3 = wgu[s][1].rearrange("p (k j) -> p k j", k=KC)
        for half in range(2):
            hs = slice(half * 512, (half + 1) * 512)
            for (w3, wb, pb) in ((wg3, wgu_b[s][0], half), (wu3, wgu_b[s][1], 2 + half)):
                for k in range(KC):
                    P.op("pe", lambda e, w3=w3, k=k, hs=hs, pb=pb: e.matmul(c.ps[pb][:], w3[:, k, :], Hr[:, k, hs], start=(k == 0), stop=(k == KC - 1)),
                         [wb, H_b], [c.ps_b[pb]])
        for half in range(2):
            hs = slice(half * 512, (half + 1) * 512)
            P.op("act", lambda e, hs=hs, half=half: e.activation(out=tmp[:, hs], in_=c.ps[half][:], func=AF.Silu),
                 [c.ps_b[half]], [tmp_b[half]])
            P.op("dve", lambda e, hs=hs, half=half, s=s: e.tensor_tensor(out=act[s][:, hs].bitcast(F32R), in0=tmp[:, hs], in1=c.ps[2 + half][:], op=ALU.mult),
                 [tmp_b[half], c.ps_b[2 + half]], [act_b[s]])
        for n in range(KC):
            for half in range(2):
                hs = slice(half * 512, (half + 1) * 512)
                pb = c.nextps()
                P.op("pe", lambda e, n=n, hs=hs, pb=pb, s=s: e.matmul(c.ps[pb][:], wd[s][:, n * 128:(n + 1) * 128], act[s][:, hs].bitcast(F32R), start=True, stop=True),
                     [wd_b[s], act_b[s]], [c.ps_b[pb]])
                idx = 2 * n + half
                if tmp2 is not None and idx % 3 == 2:
                    s2 = (idx // 3) % 2
                    P.op("act", lambda e, pb=pb, s2=s2: e.copy(out=tmp2[s2], in_=c.ps[pb][:]), [c.ps_b[pb]], [tmp2_b[s2]])
                    P.op("pool", lambda e, n=n, hs=hs, s2=s2: e.tensor_tensor(out=X[:, n, hs], in0=X[:, n, hs], in1=tmp2[s2], op=ALU.add),
                         [X_b[2 * n + half], tmp2_b[s2]], [X_b[2 * n + half]])
                else:
                    P.op("dve", lambda e, n=n, hs=hs, pb=pb: e.tensor_tensor(out=X[:, n, hs], in0=X[:, n, hs], in1=c.ps[pb][:], op=ALU.add),
                         [X_b[2 * n + half], c.ps_b[pb]], [X_b[2 * n + half]])


def load_small(c, name, dram_ap, shape, q="sp", dt=F32):
    t = c.sb(name, shape, dt)
    b = c.P.buf(name)
    c.P.dma(q, t[:], dram_ap, [], [b], None)
    return t, b


def build_ffn_test():
    c = Ctx("ffn")
    P = c.P
    xT = c.din("xT", [128, KC, T])
    g = c.din("g", [128, KC])
    wg = c.din("wg", [FC, 128, 2048])
    wu = c.din("wu", [FC, 128, 2048])
    wd = c.din("wd", [DFF, D])
    out = c.dout("out", [128, KC, T])
    X = c.sb("X", [128, KC, T]); X_b = P.bufs(2 * KC, "X")
    H = c.sb("H", [128, KC * T]); H_b = P.buf("H")
    WR = c.sb("WR", [128, 14336])
    Hr = H[:, :].bitcast(F32R).rearrange("p (k t) -> p k t", k=KC)
    g_sb, g_b = load_small(c, "g_sb", g, [128, KC])
    sq = c.sb("sq", [128, T]); sq_b = P.bufs(2, "sq")
    rstd = c.sb("rstd", [128, T]); rstd_b = P.bufs(2, "rstd")
    load_X(c, X, X_b, xT)
    rmsnorm_fm(c, X, X_b, Hr, H_b, g_sb, g_b, sq, sq_b, rstd, rstd_b)
    t2 = c.sb("ffn_tmp2", [128, 1024])
    ffn_phase(c, X, X_b, Hr, H_b, wg, wu, wd, WR, None, [t2[:, 0:512], t2[:, 512:1024]])
    store_X(c, X, X_b, out)
    return c.finish(X_b)


def tile_fm(W):
    K, N = W.shape
    return np.ascontiguousarray(W.reshape(K // 128, 128, N // 128, 128).transpose(2, 1, 0, 3)).reshape(N // 128, 128, (K // 128) * 128)


def tile_tm(W, cw=256):
    K, N = W.shape
    return np.ascontiguousarray(W.reshape(K // 128, 128, N // cw, cw).transpose(2, 1, 0, 3)).reshape(N // cw, 128, (K // 128) * cw)


def x_to_fm(xs):
    return np.ascontiguousarray(xs.T.reshape(KC, 128, xs.shape[0]).transpose(1, 0, 2))


def fm_to_x(o):
    return np.ascontiguousarray(o.transpose(1, 0, 2).reshape(D, o.shape[2]).T)


def vec_fm(g):
    return np.ascontiguousarray(g.reshape(-1, 128).T)


def proj_phase(c, Hr, H_b, kc, wfm_d, n_fm, zfm_d, wtm_d, n_tm, ztm_d, WR, tag, fm_list=None, tm_list=None):
    P = c.P
    fm_list = list(range(n_fm)) if fm_list is None else fm_list
    tm_list = list(range(n_tm)) if tm_list is None else tm_list
    zfm_b = P.buf(tag + "zfm")
    ztm_b = P.buf(tag + "ztm")
    wf = [WR[:, s * 2048:s * 2048 + kc * 128].bitcast(F32R).rearrange("p (k j) -> p k j", k=kc) for s in range(2)]
    wf_b = P.bufs(2, "pjwf")
    wt = [WR[:, 4096 + s * 4096:4096 + s * 4096 + kc * 256].bitcast(F32R).rearrange("p (k j) -> p k j", k=kc) for s in range(2)]
    wt_b = P.bufs(2, "pjwt")
    stf = [WR[:, 12288 + s * 1024:12288 + (s + 1) * 1024] for s in range(2)]
    stf_b = P.bufs(2, "pjstf")
    stt_t = c.sb("pjstt", [128, 2, 256])
    stt_b = P.bufs(2, "pjstt")
    if fm_list:
        P.dma("pool", wf[0], wfm_d[fm_list[0]].rearrange("p (k j) -> p k j", k=kc), [], [wf_b[0]], wf_b[0])
    for ni, n in enumerate(fm_list):
        s = ni % 2
        if ni + 1 < len(fm_list):
            P.dma("pool", wf[1 - s], wfm_d[fm_list[ni + 1]].rearrange("p (k j) -> p k j", k=kc), [], [wf_b[1 - s]], wf_b[1 - s])
        for half in range(2):
            hs = slice(half * 512, (half + 1) * 512)
            pb = c.nextps(0, 8)
            for k in range(kc):
                P.op("pe", lambda e, s=s, k=k, hs=hs, pb=pb: e.matmul(c.ps[pb][:], wf[s][:, k, :], Hr[:, k, hs], start=(k == 0), stop=(k == kc - 1)),
                     [wf_b[s], H_b], [c.ps_b[pb]])
            P.op("act", lambda e, s=s, hs=hs, pb=pb: e.copy(out=stf[s][:, hs].bitcast(F32R), in_=c.ps[pb][:]), [c.ps_b[pb]], [stf_b[s]])
        P.dma("sp", zfm_d[n], stf[s], [stf_b[s]], [zfm_b], stf_b[s])
    if tm_list:
        P.dma("pool", wt[0], wtm_d[tm_list[0]].rearrange("p (k j) -> p k j", k=kc), [], [wt_b[0]], wt_b[0])
    cnt = 0
    for gi, g in enumerate(tm_list):
        s = gi % 2
        if gi + 1 < len(tm_list):
            P.dma("pool", wt[1 - s], wtm_d[tm_list[gi + 1]].rearrange("p (k j) -> p k j", k=kc), [], [wt_b[1 - s]], wt_b[1 - s])
        for j in range(8):
            pb = c.nextps(0, 8)
            for k in range(kc):
                P.op("pe", lambda e, s=s, k=k, j=j, pb=pb: e.matmul(c.ps[pb][:, 0:256], Hr[:, k, j * 128:(j + 1) * 128], wt[s][:, k, :], start=(k == 0), stop=(k == kc - 1)),
                     [wt_b[s], H_b], [c.ps_b[pb]])
            ss = cnt % 2
            cnt += 1
            P.op("dve", lambda e, ss=ss, pb=pb: e.tensor_copy(out=stt_t[:, ss, :], in_=c.ps[pb][:, 0:256]), [c.ps_b[pb]], [stt_b[ss]])
            P.dma("sp", ztm_d[j, :, g * 256:(g + 1) * 256], stt_t[:, ss, :], [stt_b[ss]], [ztm_b], stt_b[ss])
    return zfm_b, ztm_b


def rotary_tm(c, raw, raw_b, outk, outk_b, cos_tm, sin_tm, tab_bs, t3, t2, tmp_b):
    P = c.P
    A, B = raw[:, :, 0:128], raw[:, :, 128:256]
    P.op("dve", lambda e: e.tensor_tensor(out=t3, in0=A, in1=cos_tm, op=ALU.mult), [raw_b] + tab_bs, [tmp_b[0]])
    P.op("dve", lambda e: e.tensor_tensor(out=t2, in0=B, in1=sin_tm, op=ALU.mult), [raw_b] + tab_bs, [tmp_b[1]])
    P.op("dve", lambda e: e.tensor_tensor(out=outk[:, :, 0:128], in0=t3, in1=t2, op=ALU.subtract), [tmp_b[0], tmp_b[1]], [outk_b])
    P.op("dve", lambda e: e.tensor_tensor(out=t3, in0=A, in1=sin_tm, op=ALU.mult), [raw_b] + tab_bs, [tmp_b[0]])
    P.op("dve", lambda e: e.tensor_tensor(out=t2, in0=B, in1=cos_tm, op=ALU.mult), [raw_b] + tab_bs, [tmp_b[1]])
    P.op("dve", lambda e: e.tensor_tensor(out=outk[:, :, 128:256], in0=t3, in1=t2, op=ALU.add), [tmp_b[0], tmp_b[1]], [outk_b])


RET_GAMMA = [1.0 - 2.0 ** (-5.0 - h) for h in range(4)]


def build_k1():
    c = Ctx("k1")
    P = c.P
    xT = c.din("xT", [128, KC, T])
    g = c.din("g", [128, KC])
    wtm = c.din("wtm", [8, 128, KC * 256])
    cos_d = c.din("cos_tm", [128, 8, 128])
    sin_d = c.din("sin_tm", [128, 8, 128])
    kz1_d = c.din("kz1", [128, 32])
    sloc = c.dout("sloc", [4, 2, 128, 256])
    ztm = c.dscr("ztm1", [8, 128, 2048])
    X = c.sb("X", [128, KC, T]); X_b = P.bufs(2 * KC, "X")
    H = c.sb("H", [128, KC * T]); H_b = P.buf("H")
    WR = c.sb("WR", [128, 14336])
    Hr = H[:, :].bitcast(F32R).rearrange("p (k t) -> p k t", k=KC)
    g_sb, g_b = load_small(c, "g_sb", g, [128, KC])
    cos_tm, cb = load_small(c, "cos_sb", cos_d, [128, 8, 128])
    sin_tm, sbb = load_small(c, "sin_sb", sin_d, [128, 8, 128])
    kz1, kz1_b = load_small(c, "kz1_sb", kz1_d, [128, 32])
    tab_b = P.buf("tabs")
    sq = c.sb("sq", [128, T]); sq_b = P.bufs(2, "sq")
    rstd = c.sb("rstd", [128, T]); rstd_b = P.bufs(2, "rstd")
    load_X(c, X, X_b, xT)
    rmsnorm_fm(c, X, X_b, Hr, H_b, g_sb, g_b, sq, sq_b, rstd, rstd_b)
    _, ztm_b = proj_phase(c, Hr, H_b, KC, None, 0, None, wtm, 8, ztm, WR, "k1")
    P.barrier()
    Xf = X[:, :, :].rearrange("p k t -> p (k t)")
    raw = Xf[:, 0:2048].rearrange("p (j c) -> p j c", j=8); raw_b = P.buf("raw")
    t3 = Xf[:, 6144:7168].rearrange("p (j c) -> p j c", j=8)
    t2 = Xf[:, 7168:8192].rearrange("p (j c) -> p j c", j=8)
    tmp_b = P.bufs(2, "rt")
    so = Xf[:, 8192:8704].rearrange("p (s c) -> p s c", s=2); so_b = P.bufs(2, "so")
    ktr = H[:, 0:2048].bitcast(F32R).rearrange("p (j c) -> p j c", j=8); kt_b = P.buf("kt")
    vt = H[:, 2048:4096].bitcast(F32R).rearrange("p (j c) -> p j c", j=8); vt_b = P.buf("vt")
    for h in range(4):
        P.dma("sp", raw, ztm[:, :, h * 256:(h + 1) * 256].rearrange("j p c -> p j c"), [ztm_b], [raw_b], raw_b)
        P.dma("pool", vt, ztm[:, :, 1024 + h * 256:1024 + (h + 1) * 256].rearrange("j p c -> p j c"), [ztm_b], [vt_b], vt_b)
        for j in range(8):
            P.op("dve", lambda e, j=j, h=h: e.tensor_scalar(out=raw[:, j, :], in0=raw[:, j, :], scalar1=kz1[:, j * 4 + h:j * 4 + h + 1], scalar2=None, op0=ALU.mult),
                 [raw_b, kz1_b], [raw_b])
        rotary_tm(c, raw, raw_b, ktr, kt_b, cos_tm[:], sin_tm[:], [cb, sbb], t3, t2, tmp_b)
        for dc in range(2):
            pb = c.nextps(0, 8)
            for j in range(8):
                P.op("pe", lambda e, j=j, dc=dc, pb=pb: e.matmul(c.ps[pb][:, 0:256], ktr[:, j, dc * 128:(dc + 1) * 128], vt[:, j, :], start=(j == 0), stop=(j == 7)),
                     [kt_b, vt_b], [c.ps_b[pb]])
            P.op("act", lambda e, dc=dc, pb=pb: e.copy(out=so[:, dc, :], in_=c.ps[pb][:, 0:256]), [c.ps_b[pb]], [so_b[dc]])
            P.dma("sp", sloc[h, dc], so[:, dc, :], [so_b[dc]], [], so_b[dc])
    return c.finish(so_b)


def rope_tables(qtr):
    pos = (np.arange(T, dtype=np.float32) + np.float32(qtr * T))
    inv = (np.float32(10000.0) ** (-(np.arange(0, 256, 2, dtype=np.float32) / np.float32(256)))).astype(np.float32)
    ang = (pos[:, None] * inv[None, :]).astype(np.float32)
    cos, sin = np.cos(ang).astype(np.float32), np.sin(ang).astype(np.float32)
    fm = (np.ascontiguousarray(cos.T), np.ascontiguousarray(sin.T))
    tm = (np.ascontiguousarray(cos.reshape(8, 128, 128).transpose(1, 0, 2)), np.ascontiguousarray(sin.reshape(8, 128, 128).transpose(1, 0, 2)))
    return fm, tm


def k1_tables():
    kz1 = np.zeros((128, 8, 4), np.float64)
    tb = np.arange(128)
    for j in range(8):
        for h in range(4):
            kz1[:, j, h] = RET_GAMMA[h] ** (1023 - (128 * j + tb)) / 16.0
    return kz1.reshape(128, 32).astype(np.float32)


def R_(ap):
    return ap.bitcast(F32R)


def outproj_partial(c, X, X_b, mixedT, mixed_b, wout_d, rows, wo, wo_b):
    P = c.P
    for mi, r in enumerate(rows):
        P.dma("pool", wo[mi], wout_d[r * 128:(r + 1) * 128, :], [], [wo_b[mi]], wo_b[mi])
    for n in range(KC):
        for half in range(2):
            hs = slice(half * 512, (half + 1) * 512)
            pb = c.nextps(0, 8)
            for mi in range(len(rows)):
                P.op("pe", lambda e, mi=mi, n=n, hs=hs, pb=pb: e.matmul(c.ps[pb][:], wo[mi][:, n * 128:(n + 1) * 128], mixedT[:, mi, hs],
                                                                        start=(mi == 0), stop=(mi == len(rows) - 1)),
                     [wo_b[mi], mixed_b], [c.ps_b[pb]])
            P.op("dve", lambda e, n=n, hs=hs, pb=pb: e.tensor_tensor(out=X[:, n, hs], in0=X[:, n, hs], in1=c.ps[pb][:], op=ALU.add),
                 [X_b[2 * n + half], c.ps_b[pb]], [X_b[2 * n + half]])


def groupnorm_gate(c, src, src_b, nchunk, width, gate, gate_b, gain, gain_b, outT, out_b, sq, sq_b, rstd, rstd_b):
    P = c.P
    for half in range(2):
        hs = slice(half * 512, (half + 1) * 512)
        pb = c.nextps(0, 8)
        for i in range(nchunk):
            sl = i % 2
            ss = slice(sl * 512, (sl + 1) * 512)
            P.op("act", lambda e, i=i, hs=hs, ss=ss: e.activation(out=sq[:, ss], in_=src[:, i, hs], func=AF.Square), [src_b], [sq_b[sl]])
            P.op("pe", lambda e, i=i, ss=ss, pb=pb: e.matmul(c.ps[pb][:], c.ones[:], sq[:, ss], start=(i == 0), stop=(i == nchunk - 1)),
                 [c.ones_b, sq_b[sl]], [c.ps_b[pb]])
        P.op("act", lambda e, hs=hs, pb=pb: e.activation(out=rstd[:, hs], in_=c.ps[pb][:], func=AF.Sqrt, bias=c.eps[:, 0:1], scale=1.0 / width),
             [c.ps_b[pb], c.eps_b], [rstd_b[half]])
        P.op("dve", lambda e, hs=hs: e.reciprocal(out=rstd[:, hs], in_=rstd[:, hs]), [rstd_b[half]], [rstd_b[half]])
    for i in range(nchunk):
        P.op("act", lambda e, i=i: e.activation(out=R_(gate[:, i, :]), in_=gate[:, i, :], func=AF.Silu), [gate_b], [gate_b])
    for i in range(nchunk):
        if gain is None:
            P.op("dve", lambda e, i=i: e.tensor_tensor(out=R_(outT[:, i, :]), in0=src[:, i, :], in1=rstd[:, :], op=ALU.mult),
                 [src_b, rstd_b[0], rstd_b[1]], [out_b])
        else:
            P.op("dve", lambda e, i=i: e.scalar_tensor_tensor(out=R_(outT[:, i, :]), in0=src[:, i, :], scalar=gain[:, i:i + 1], in1=rstd[:, :], op0=ALU.mult, op1=ALU.mult),
                 [src_b, rstd_b[0], rstd_b[1], gain_b], [out_b])
        P.op("dve", lambda e, i=i: e.tensor_tensor(out=R_(outT[:, i, :]), in0=outT[:, i, :], in1=gate[:, i, :], op=ALU.mult),
             [out_b, gate_b], [out_b])


def retention_phase(c, X, X_b, H, WR, FS, zfm, zfm_b, ztm, ztm_b, d, wout_d, sq, sq_b, rstd, rstd_b, sstate=None, first=False):
    P = c.P
    qT = H[:, 0:2048].rearrange("p (a t) -> p a t", a=2); qT_b = P.buf("qT")
    kT = H[:, 2048:4096].rearrange("p (a t) -> p a t", a=2); kT_b = P.buf("kT")
    qs = H[:, 4096:6144].rearrange("p (a t) -> p a t", a=2); qs_b = P.buf("qs")
    ktm = H[:, 6144:8192].rearrange("p (j c) -> p j c", j=8); ktm_b = P.buf("ktm")
    vtm = H[:, 8192:10240].rearrange("p (j c) -> p j c", j=8); vtm_b = P.buf("vtm")
    S = [H[:, 10240 + s * 512:10240 + (s + 1) * 512].rearrange("p (a e) -> p a e", a=2) for s in range(2)] + [H[:, 15616:16128].rearrange("p (a e) -> p a e", a=2)]
    S_b = P.bufs(3, "S")
    PT = [H[:, 11264 + s * 128:11264 + (s + 1) * 128] for s in range(2)]; PT_b = P.bufs(2, "PT")
    mixedT = H[:, 11520:13568].rearrange("p (a t) -> p a t", a=2); mixed_b = P.buf("mixedT")
    retT = H[:, 13568:15616].rearrange("p (a t) -> p a t", a=2); retT_b = P.buf("retT")
    tS = [H[:, 15616 + s * 256:15616 + (s + 1) * 256] for s in range(2)]; tS_b = P.bufs(2, "tS")
    cosT = WR[:, 0:1024]; sinT = WR[:, 1024:2048]
    cos_tm = WR[:, 2048:3072].rearrange("p (j i) -> p j i", j=8); sin_tm = WR[:, 3072:4096].rearrange("p (j i) -> p j i", j=8)
    dmask = WR[:, 4096:4608].rearrange("p (h i) -> p h i", h=4); xi = WR[:, 4608:5120].rearrange("p (h i) -> p h i", h=4)
    tab_b = P.bufs(6, "rtab")
    for i, (dst, key) in enumerate(((cosT, "cosT"), (sinT, "sinT"), (cos_tm, "cos_tm"), (sin_tm, "sin_tm"), (dmask, "dmask"), (xi, "xi"))):
        P.dma("pool", R_(dst), d[key], [], [tab_b[i]], None)
    sst_b = P.buf("sstate")
    rawA = WR[:, 5120:6144]; rawB = WR[:, 6144:7168]; raw_b = P.buf("rraw")
    rawtm = WR[:, 5120:7168].rearrange("p (j c) -> p j c", j=8)
    t3 = WR[:, 7168:8192]; t2 = WR[:, 8192:9216]; tmp_b = P.bufs(2, "rtmp")
    wo = [R_(WR[:, 9216 + s * 2048:9216 + (s + 1) * 2048]) for s in range(2)]; wo_b = P.bufs(2, "wo")
    gT = FS[:, 2048:4096].rearrange("p (a t) -> p a t", a=2); gT_b = P.buf("gT")
    kz, kz_b = load_small(c, "kz_sb", d["kz"], [128, 4])
    if sstate is None:
        sw, sw_b = load_small(c, "sw_sb", d["sw"], [128, 12])

    def rot_fm(chunk0, out3, out_b):
        P.dma("pool", R_(rawA), zfm[chunk0], [zfm_b], [raw_b], raw_b)
        P.dma("pool", R_(rawB), zfm[chunk0 + 1], [zfm_b], [raw_b], raw_b)
        tb = [tab_b[0], tab_b[1]]
        P.op("dve", lambda e: e.tensor_tensor(out=R_(t3), in0=rawA, in1=cosT, op=ALU.mult), [raw_b] + tb, [tmp_b[0]])
        P.op("dve", lambda e: e.tensor_tensor(out=R_(t2), in0=rawB, in1=sinT, op=ALU.mult), [raw_b] + tb, [tmp_b[1]])
        P.op("dve", lambda e: e.tensor_tensor(out=R_(out3[:, 0, :]), in0=t3, in1=t2, op=ALU.subtract), [tmp_b[0], tmp_b[1]], [out_b])
        P.op("dve", lambda e: e.tensor_tensor(out=R_(t3), in0=rawA, in1=sinT, op=ALU.mult), [raw_b] + tb, [tmp_b[0]])
        P.op("dve", lambda e: e.tensor_tensor(out=R_(t2), in0=rawB, in1=cosT, op=ALU.mult), [raw_b] + tb, [tmp_b[1]])
        P.op("dve", lambda e: e.tensor_tensor(out=R_(out3[:, 1, :]), in0=t3, in1=t2, op=ALU.add), [tmp_b[0], tmp_b[1]], [out_b])

    for h in range(4):
        g128 = float(RET_GAMMA[h] ** 128)
        rot_fm(2 * h, qT, qT_b)
        for a in range(2):
            P.op("dve", lambda e, a=a, h=h: e.tensor_tensor(out=R_(qs[:, a, :]).rearrange("p (j i) -> p j i", j=8), in0=qT[:, a, :].rearrange("p (j i) -> p j i", j=8),
                                                          in1=xi[:, h:h + 1, :].to_broadcast([128, 8, 128]), op=ALU.mult), [qT_b, tab_b[5]], [qs_b])
        rot_fm(8 + 2 * h, kT, kT_b)
        P.dma("pool", R_(rawtm), ztm[:, :, h * 256:(h + 1) * 256].rearrange("j p c -> p j c"), [ztm_b], [raw_b], raw_b)
        P.dma("pool", R_(vtm), ztm[:, :, 1024 + h * 256:1024 + (h + 1) * 256].rearrange("j p c -> p j c"), [ztm_b], [vtm_b], vtm_b)
        for a in range(2):
            P.dma("pool", R_(gT[:, a, :]), zfm[16 + 2 * h + a], [zfm_b], [gT_b], gT_b)
        P.op("dve", lambda e, h=h: e.tensor_scalar(out=R_(rawtm), in0=rawtm, scalar1=kz[:, h:h + 1], scalar2=None, op0=ALU.mult), [raw_b, kz_b], [raw_b])
        t3j = t3.rearrange("p (j i) -> p j i", j=8); t2j = t2.rearrange("p (j i) -> p j i", j=8)
        tb = [tab_b[2], tab_b[3]]
        A, B = rawtm[:, :, 0:128], rawtm[:, :, 128:256]
        P.op("dve", lambda e: e.tensor_tensor(out=R_(t3j), in0=A, in1=cos_tm, op=ALU.mult), [raw_b] + tb, [tmp_b[0]])
        P.op("dve", lambda e: e.tensor_tensor(out=R_(t2j), in0=B, in1=sin_tm, op=ALU.mult), [raw_b] + tb, [tmp_b[1]])
        P.op("dve", lambda e: e.tensor_tensor(out=R_(ktm[:, :, 0:128]), in0=t3j, in1=t2j, op=ALU.subtract), [tmp_b[0], tmp_b[1]], [ktm_b])
        P.op("dve", lambda e: e.tensor_tensor(out=R_(t3j), in0=A, in1=sin_tm, op=ALU.mult), [raw_b] + tb, [tmp_b[0]])
        P.op("dve", lambda e: e.tensor_tensor(out=R_(t2j), in0=B, in1=cos_tm, op=ALU.mult), [raw_b] + tb, [tmp_b[1]])
        P.op("dve", lambda e: e.tensor_tensor(out=R_(ktm[:, :, 128:256]), in0=t3j, in1=t2j, op=ALU.add), [tmp_b[0], tmp_b[1]], [ktm_b])
        if sstate is not None:
            for dc in range(2):
                if first:
                    P.op("dve", lambda e, dc=dc: e.tensor_scalar(out=R_(S[0][:, dc, :]), in0=dmask[:, 0:2, :].rearrange("p a i -> p (a i)"), scalar1=0.0, scalar2=None, op0=ALU.mult),
                         [tab_b[4]], [S_b[0]])
                else:
                    P.dma("pool", R_(S[0][:, dc, :]), sstate[h, dc], [sst_b], [S_b[0]], S_b[0])
        else:
            for dc in range(2):
                for i in range(3):
                    sl = (dc * 3 + i) % 2
                    P.dma("pool", R_(tS[sl]), d["sprev"][i, h, dc], [], [tS_b[sl]], tS_b[sl])
                    col = sw[:, i * 4 + h:i * 4 + h + 1]
                    if i == 0:
                        P.op("dve", lambda e, sl=sl, dc=dc, col=col: e.tensor_scalar(out=R_(S[0][:, dc, :]), in0=tS[sl], scalar1=col, scalar2=None, op0=ALU.mult),
                             [tS_b[sl], sw_b], [S_b[0]])
                    else:
                        P.op("dve", lambda e, sl=sl, dc=dc, col=col: e.scalar_tensor_tensor(out=R_(S[0][:, dc, :]), in0=tS[sl], scalar=col, in1=S[0][:, dc, :], op0=ALU.mult, op1=ALU.add),
                             [tS_b[sl], sw_b, S_b[0]], [S_b[0]])
        def stage1(j, h=h, g128=g128):
            js = slice(j * 128, (j + 1) * 128)
            pb = c.nextps(0, 8)
            for a in range(2):
                P.op("pe", lambda e, a=a, js=js, pb=pb: e.matmul(c.ps[pb][:, 0:128], R_(kT[:, a, js]), R_(qT[:, a, js]), start=(a == 0), stop=(a == 1)),
                     [kT_b, qT_b], [c.ps_b[pb]])
            P.op("dve", lambda e, pb=pb, j=j, h=h: e.tensor_tensor(out=R_(PT[j % 2]), in0=c.ps[pb][:, 0:128], in1=dmask[:, h, :], op=ALU.mult),
                 [c.ps_b[pb], tab_b[4]], [PT_b[j % 2]])
            if j < 7 or sstate is not None:
                cur, nxt = j % 3, (j + 1) % 3
                for a in range(2):
                    pb3 = c.nextps(0, 8)
                    P.op("pe", lambda e, a=a, j=j, pb3=pb3: e.matmul(c.ps[pb3][:, 0:256], R_(ktm[:, j, a * 128:(a + 1) * 128]), R_(vtm[:, j, :]), start=True, stop=True),
                         [ktm_b, vtm_b], [c.ps_b[pb3]])
                    P.op("dve", lambda e, a=a, pb3=pb3, cur=cur, nxt=nxt, g128=g128: e.scalar_tensor_tensor(out=R_(S[nxt][:, a, :]), in0=S[cur][:, a, :], scalar=g128, in1=c.ps[pb3][:, 0:256],
                                                                                                      op0=ALU.mult, op1=ALU.add),
                         [S_b[cur], c.ps_b[pb3]], [S_b[nxt]])

        def stage2(j):
            js = slice(j * 128, (j + 1) * 128)
            cur = j % 3
            for ec in range(2):
                es = slice(ec * 128, (ec + 1) * 128)
                pb2 = c.nextps(0, 8)
                P.op("pe", lambda e, j=j, es=es, pb2=pb2: e.matmul(c.ps[pb2][:, 0:128], R_(vtm[:, j, es]), R_(PT[j % 2]), start=True, stop=False),
                     [vtm_b, PT_b[j % 2]], [c.ps_b[pb2]])
                for a in range(2):
                    P.op("pe", lambda e, a=a, es=es, js=js, pb2=pb2, cur=cur: e.matmul(c.ps[pb2][:, 0:128], R_(S[cur][:, a, es]), R_(qs[:, a, js]), start=False, stop=(a == 1)),
                         [S_b[cur], qs_b], [c.ps_b[pb2]])
                P.op("act", lambda e, ec=ec, js=js, pb2=pb2: e.copy(out=R_(retT[:, ec, js]), in_=c.ps[pb2][:, 0:128]), [c.ps_b[pb2]], [retT_b])

        stage1(0)
        for j in range(8):
            if j + 1 < 8:
                stage1(j + 1)
            stage2(j)
        if sstate is not None:
            for dc in range(2):
                P.dma("sp", sstate[h, dc], S[2][:, dc, :], [S_b[2]], [sst_b], S_b[2])
        groupnorm_gate(c, retT, retT_b, 2, 256, gT, gT_b, None, None, mixedT, mixed_b, sq, sq_b, rstd, rstd_b)
        outproj_partial(c, X, X_b, R_(mixedT), mixed_b, wout_d, [2 * h, 2 * h + 1], wo, wo_b)


def sgu_phase(c, X, X_b, H, WR, FS, zfm, zfm_b, ztm, ztm_b, d, wout_d, u_chunk0, vs_col0):
    P = c.P
    vs = H[:, 0:8192].rearrange("p (j c) -> p j c", j=8); vs_b = P.bufs(8, "vs")
    uT = H[:, 8192:10240].rearrange("p (a t) -> p a t", a=2); uT_b = P.buf("uT")
    mixedT = H[:, 11520:13568].rearrange("p (a t) -> p a t", a=2); mixed_b = P.buf("smixedT")
    lng = WR[:, 0:1024]; lnb = WR[:, 1024:2048]
    wsT = WR[:, 2048:2560].rearrange("p (g i) -> p g i", g=4); bsb = WR[:, 2560:3072].rearrange("p (g i) -> p g i", g=4)
    junk = WR[:, 3072:4096]; junk_b = P.buf("sjunk")
    tmp = [WR[:, 4096 + s * 512:4096 + (s + 1) * 512] for s in range(2)]; tmp_b = P.bufs(2, "stmp")
    wo = [R_(WR[:, 9216 + s * 2048:9216 + (s + 1) * 2048]) for s in range(2)]; wo_b = P.bufs(2, "wo")
    tb = P.bufs(4, "stab")
    for i, (dst, key) in enumerate(((lng, "lng_b"), (lnb, "lnb_b"), (wsT, "wsT"), (bsb, "bsb"))):
        P.dma("pool", R_(dst), d[key], [], [tb[i]], None)
    P.op("dve", lambda e: e.tensor_scalar(out=R_(wsT[64:128, :, 0:64]), in0=wsT[64:128, :, 0:64], scalar1=0.0, scalar2=None, op0=ALU.mult), [tb[2]], [tb[2]])
    st = c.sb("sgu_st", [128, 8, 4]); st_b = P.bufs(8, "sgst")
    P.op("dve", lambda e: e.memset(st[:], 0.0), [], st_b)
    for j in range(8):
        P.dma("pool", R_(vs[:, j, :]), ztm[j, :, vs_col0:vs_col0 + 1024], [ztm_b], [vs_b[j]], None)
    for j in range(8):
        v = vs[:, j, :]
        P.op("act", lambda e, v=v, j=j: e.activation(out=R_(v), in_=v, func=AF.Gelu, accum_out=st[:, j, 0:1]), [vs_b[j], st_b[j]], [vs_b[j], st_b[j]])
        P.op("dve", lambda e, j=j: e.tensor_scalar(out=st[:, j, 1:2], in0=st[:, j, 0:1], scalar1=-1.0 / 1024, scalar2=None, op0=ALU.mult), [st_b[j]], [st_b[j]])
        P.op("act", lambda e, v=v, j=j: e.activation(out=R_(junk), in_=v, func=AF.Square, bias=st[:, j, 1:2], accum_out=st[:, j, 2:3]), [vs_b[j], st_b[j], junk_b], [junk_b, st_b[j]])
        P.op("act", lambda e, j=j: e.activation(out=st[:, j, 3:4], in_=st[:, j, 2:3], func=AF.Sqrt, bias=c.eps[:, 0:1], scale=1.0 / 1024), [st_b[j], c.eps_b], [st_b[j]])
        P.op("dve", lambda e, j=j: e.reciprocal(out=st[:, j, 3:4], in_=st[:, j, 3:4]), [st_b[j]], [st_b[j]])
        P.op("dve", lambda e, v=v, j=j: e.tensor_scalar(out=R_(v), in0=v, scalar1=st[:, j, 1:2], scalar2=st[:, j, 3:4], op0=ALU.add, op1=ALU.mult), [vs_b[j], st_b[j]], [vs_b[j]])
        P.op("dve", lambda e, v=v: e.tensor_tensor(out=R_(v), in0=v, in1=lng, op=ALU.mult), [vs_b[j], tb[0]], [vs_b[j]])
        P.op("dve", lambda e, v=v: e.tensor_tensor(out=R_(v), in0=v, in1=lnb, op=ALU.add), [vs_b[j], tb[1]], [vs_b[j]])
    for g in range(4):
        for a in range(2):
            P.dma("pool", R_(uT[:, a, :]), zfm[u_chunk0 + 2 * g + a], [zfm_b], [uT_b], uT_b)
        for a in range(2):
            P.op("act", lambda e, a=a: e.activation(out=R_(uT[:, a, :]), in_=uT[:, a, :], func=AF.Gelu), [uT_b], [uT_b])
        for a in range(2):
            cc = 2 * g + a
            for wq in range(2):
                pb = c.nextps(0, 8)
                for w in range(4):
                    j = 4 * wq + w
                    P.op("pe", lambda e, j=j, w=w, cc=cc, g=g, pb=pb: e.matmul(c.ps[pb][:, w * 128:(w + 1) * 128], R_(vs[:, j, cc * 128:(cc + 1) * 128]), R_(wsT[:, g, :]), start=True, stop=True),
                         [vs_b[j], tb[2]], [c.ps_b[pb]])
                sl = (a * 2 + wq) % 2
                P.op("dve", lambda e, pb=pb, sl=sl, g=g: e.tensor_tensor(out=R_(tmp[sl]).rearrange("p (w i) -> p w i", w=4), in0=c.ps[pb][:].rearrange("p (w i) -> p w i", w=4),
                                                                       in1=bsb[:, g:g + 1, :].to_broadcast([128, 4, 128]), op=ALU.add), [c.ps_b[pb], tb[3]], [tmp_b[sl]])
                P.op("dve", lambda e, a=a, wq=wq, sl=sl: e.tensor_tensor(out=R_(mixedT[:, a, wq * 512:(wq + 1) * 512]), in0=tmp[sl], in1=uT[:, a, wq * 512:(wq + 1) * 512], op=ALU.mult),
                     [tmp_b[sl], uT_b], [mixed_b])
        outproj_partial(c, X, X_b, R_(mixedT), mixed_b, wout_d, [8 + 2 * g, 8 + 2 * g + 1], wo, wo_b)


def build_k2():
    c = Ctx("k2")
    P = c.P
    xT = c.din("xT", [128, KC, T])
    g_mix = c.din("g_mix", [128, KC]); g_ffn = c.din("g_ffn", [128, KC])
    wfm = c.din("wfm", [32, 128, KC * 128]); wtm = c.din("wtm", [12, 128, KC * 256])
    d = {"cosT": c.din("cosT", [128, 1024]), "sinT": c.din("sinT", [128, 1024]),
         "cos_tm": c.din("cos_tm", [128, 8, 128]), "sin_tm": c.din("sin_tm", [128, 8, 128]),
         "dmask": c.din("dmask", [128, 4, 128]), "xi": c.din("xi", [128, 4, 128]), "kz": c.din("kz", [128, 4]),
         "sw": c.din("sw", [128, 12]), "sprev": c.din("sprev", [3, 4, 2, 128, 256]),
         "lng_b": c.din("lng_b", [128, 1024]), "lnb_b": c.din("lnb_b", [128, 1024]),
         "wsT": c.din("wsT", [128, 4, 128]), "bsb": c.din("bsb", [128, 4, 128])}
    wout = c.din("wout", [D, D])
    wg = c.din("wg", [FC, 128, 2048]); wu = c.din("wu", [FC, 128, 2048]); wd = c.din("wd", [DFF, D])
    out = c.dout("out", [128, KC, T])
    zfm = c.dscr("zfm", [32, 128, 1024])
    ztm = c.dscr("ztm", [8, 128, 3072])
    X = c.sb("X", [128, KC, T]); X_b = P.bufs(2 * KC, "X")
    H = c.sb("H", [128, KC * T]); H_b = P.buf("H")
    WR = c.sb("WR", [128, 14336])
    FS = c.sb("FS", [128, 4096])
    Hr = H[:, :].bitcast(F32R).rearrange("p (k t) -> p k t", k=KC)
    gm_sb, gm_b = load_small(c, "gm_sb", g_mix, [128, KC])
    gf_sb, gf_b = load_small(c, "gf_sb", g_ffn, [128, KC])
    sq = FS[:, 0:1024]; sq_b = P.bufs(2, "sq")
    rstd = FS[:, 1024:2048]; rstd_b = P.bufs(2, "rstd")
    load_X(c, X, X_b, xT)
    rmsnorm_fm(c, X, X_b, Hr, H_b, gm_sb, gm_b, sq, sq_b, rstd, rstd_b)
    zfm_b, ztm_b = proj_phase(c, Hr, H_b, KC, wfm, 32, zfm, wtm, 12, ztm, WR, "k2")
    P.barrier()
    retention_phase(c, X, X_b, H, WR, FS, zfm, zfm_b, ztm, ztm_b, d, wout, sq, sq_b, rstd, rstd_b)
    P.barrier()
    sgu_phase(c, X, X_b, H, WR, FS, zfm, zfm_b, ztm, ztm_b, d, wout, 24, 2048)
    P.barrier()
    rmsnorm_fm(c, X, X_b, Hr, H_b, gf_sb, gf_b, sq, sq_b, rstd, rstd_b)
    ffn_phase(c, X, X_b, Hr, H_b, wg, wu, wd, WR, FS[:, 2048:3072])
    store_X(c, X, X_b, out)
    return c.finish(X_b)


def k2_tables(qtr):
    tb = np.arange(128)
    dmask = np.zeros((128, 4, 128), np.float64)
    xi = np.zeros((128, 4, 128), np.float64)
    kz = np.zeros((128, 4), np.float64)
    sw = np.zeros((128, 12), np.float64)
    ch = tb // 64
    for h in range(4):
        gm = RET_GAMMA[h]
        dmask[:, h, :] = (gm ** np.abs(tb[None, :] - tb[:, None])) * (ch[:, None] <= ch[None, :]) / 16.0
        xi[:, h, :] = (gm ** (tb + 1.0))[None, :]
        kz[:, h] = gm ** (127.0 - tb) / 16.0
        for i in range(3):
            if i < qtr:
                sw[:, i * 4 + h] = gm ** (1024.0 * (qtr - i - 1))
    return dmask.astype(np.float32), xi.astype(np.float32), kz.astype(np.float32), sw.astype(np.float32)


def hgrn_phase(c, X, X_b, H, WR, FS, zfm, zfm_b, ztm, ztm_b, d, wout_d, sq, sq_b, rstd, rstd_b, outputs, dbg=None, hstate=None, first=False):
    P = c.P
    qhat = H[:, 0:1024]; qhat_b = P.buf("qhat")
    khat = H[:, 1024:2048]; khat_b = P.buf("khat")
    eb = H[:, 2048:3072]; eb_b = P.buf("eb")
    enb = H[:, 3072:4096]; enb_b = P.buf("enb")
    vtm = H[:, 4096:5120].rearrange("p (j e) -> p j e", j=8); vtm_b = P.buf("hvtm")
    rawq = H[:, 5120:6144]; rawq_b = P.buf("rawq")
    rawf = H[:, 6144:7168]; rawf_b = P.buf("rawf")
    kk = H[:, 7168:7296]; kk_b = P.buf("kk")
    ktl = H[:, 7296:7424]; ktl_b = P.buf("ktl")
    am = [H[:, 7424 + s * 128:7424 + (s + 1) * 128] for s in range(2)]; am_b = P.bufs(2, "am")
    St = [H[:, 7680 + s * 128:7680 + (s + 1) * 128] for s in range(2)]; St_b = P.bufs(2, "St")
    outT = H[:, 7936:8960].rearrange("p (a t) -> p a t", a=1); outT_b = P.buf("houtT")
    mixedT = H[:, 8960:9984].rearrange("p (a t) -> p a t", a=1); mixed_b = P.buf("hmixedT")
    ident = H[:, 9984:10112]; ident_b = P.buf("ident")
    tS = H[:, 10112:10240]; tS_b = P.buf("htS")
    lbt = WR[:, 0:1024]; omlt = WR[:, 1024:2048]; lbt_b = P.buf("lbt")
    rawft = WR[:, 2048:3072].rearrange("p (j e) -> p j e", j=8); rawft_b = P.buf("rawft")
    l1t = WR[:, 3072:4096]
    wo = [R_(WR[:, 9216 + s * 2048:9216 + (s + 1) * 2048]) for s in range(2)]; wo_b = P.bufs(2, "wo")
    logf = FS[:, 2048:3072].rearrange("p (j e) -> p j e", j=8); logf_b = P.buf("logf")
    gate = FS[:, 3072:4096].rearrange("p (a t) -> p a t", a=1); gate_b = P.buf("hgate")
    ltri, ltri_b = load_small(c, "ltri_sb", d["ltri"], [128, 128])
    mtri, mtri_b = load_small(c, "mtri_sb", d["mtri"], [128, 128])
    P.dma("pool", R_(ident), d["ident"], [], [ident_b], None)
    hst_b = P.buf("hstate")
    lbf = c.sb("lbf", [128, 4, 8]); lbf_b = P.buf("lbf")
    P.dma("sp", lbf[:, 3, :], d["lb0f"], [], [lbf_b], None)
    P.dma("sp", lbf[:, 0, :], d["lb1f"], [], [lbf_b], None)
    P.op("dve", lambda e: e.tensor_tensor(out=lbf[:, 0, :], in0=lbf[:, 0, :], in1=lbf[:, 3, :], op=ALU.subtract), [lbf_b], [lbf_b])
    P.op("act", lambda e: e.activation(out=lbf[:, 0, :], in_=lbf[:, 0, :], func=AF.Sigmoid), [lbf_b], [lbf_b])
    P.op("dve", lambda e: e.tensor_scalar(out=lbf[:, 1, :], in0=lbf[:, 0, :], scalar1=-1.0, scalar2=1.0, op0=ALU.mult, op1=ALU.add), [lbf_b], [lbf_b])
    P.op("dve", lambda e: e.tensor_scalar(out=lbf[:, 2, :], in0=lbf[:, 1, :], scalar1=-1.0, scalar2=None, op0=ALU.mult), [lbf_b], [lbf_b])
    P.dma("pool", R_(lbt), d["lb1t"], [], [lbt_b], None)
    P.dma("pool", R_(l1t), d["lb0t"], [], [lbt_b], None)
    P.op("dve", lambda e: e.tensor_tensor(out=R_(lbt), in0=lbt, in1=l1t, op=ALU.subtract), [lbt_b], [lbt_b])
    P.op("act", lambda e: e.activation(out=R_(lbt), in_=lbt, func=AF.Sigmoid), [lbt_b], [lbt_b])
    P.op("dve", lambda e: e.tensor_scalar(out=R_(omlt), in0=lbt, scalar1=-1.0, scalar2=1.0, op0=ALU.mult, op1=ALU.add), [lbt_b], [lbt_b])
    fused = hstate is not None
    if outputs:
        ngain, ngain_b = load_small(c, "ngain_sb", d["ngain"], [128, 8])
        if not fused:
            aprev, aprev_b = load_small(c, "aprev_sb", d["aprev"], [128, 24])
    if not outputs and not fused:
        aprod = c.sb("aprod", [128, 8]); aprod_b = P.buf("aprod")
        P.op("dve", lambda e: e.memset(aprod[:], 1.0), [], [aprod_b])
    if not outputs:
        sout = c.sb("hsout", [128, 128]); sout_b = P.buf("hsout")

    for h in range(8):
        hs_ = slice(h * 128, (h + 1) * 128)
        if outputs:
            P.dma("pool", R_(rawq), zfm[h], [zfm_b], [rawq_b], rawq_b)
        P.dma("pool", R_(rawf), zfm[8 + h], [zfm_b], [rawf_b], rawf_b)
        P.dma("pool", R_(rawft), ztm[:, :, h * 128:(h + 1) * 128].rearrange("j p e -> p j e"), [ztm_b], [rawft_b], rawft_b)
        P.dma("pool", R_(vtm), ztm[:, :, 1024 + h * 128:1024 + (h + 1) * 128].rearrange("j p e -> p j e"), [ztm_b], [vtm_b], vtm_b)
        if outputs:
            P.dma("pool", R_(gate[:, 0, :]), zfm[16 + h], [zfm_b], [gate_b], gate_b)
        P.op("act", lambda e: e.activation(out=R_(rawft), in_=rawft, func=AF.Sigmoid), [rawft_b], [rawft_b])
        P.op("dve", lambda e, hs_=hs_: e.tensor_tensor(out=R_(rawft), in0=rawft, in1=omlt[:, hs_].rearrange("p (o e) -> p o e", o=1).to_broadcast([128, 8, 128]), op=ALU.mult),
             [rawft_b, lbt_b], [rawft_b])
        P.op("dve", lambda e, hs_=hs_: e.tensor_tensor(out=R_(rawft), in0=rawft, in1=lbt[:, hs_].rearrange("p (o e) -> p o e", o=1).to_broadcast([128, 8, 128]), op=ALU.add),
             [rawft_b, lbt_b], [rawft_b])
        P.op("act", lambda e: e.activation(out=logf, in_=rawft, func=AF.Ln), [rawft_b], [logf_b])
        for half in range(2):
            pb = c.nextps(0, 8)
            for jj in range(4):
                j = half * 4 + jj
                P.op("pe", lambda e, j=j, jj=jj, pb=pb: e.matmul(c.ps[pb][:, jj * 128:(jj + 1) * 128], logf[:, j, :], ltri[:], start=True, stop=True),
                     [logf_b, ltri_b], [c.ps_b[pb]])
            hsl = slice(half * 512, (half + 1) * 512)
            P.op("act", lambda e, hsl=hsl, pb=pb: e.activation(out=R_(eb[:, hsl]), in_=c.ps[pb][:], func=AF.Exp), [c.ps_b[pb]], [eb_b])
            P.op("act", lambda e, hsl=hsl, pb=pb: e.activation(out=R_(enb[:, hsl]), in_=c.ps[pb][:], func=AF.Exp, scale=-1.0), [c.ps_b[pb]], [enb_b])
        if outputs:
            P.op("act", lambda e: e.activation(out=R_(rawq), in_=rawq, func=AF.Silu), [rawq_b], [rawq_b])
            P.op("dve", lambda e: e.tensor_tensor(out=R_(qhat), in0=rawq, in1=eb, op=ALU.mult), [rawq_b, eb_b], [qhat_b])
        P.op("act", lambda e: e.activation(out=R_(rawf), in_=rawf, func=AF.Sigmoid), [rawf_b], [rawf_b])
        P.op("dve", lambda e, h=h: e.tensor_scalar(out=R_(rawf), in0=rawf, scalar1=lbf[:, 2, h:h + 1], scalar2=lbf[:, 1, h:h + 1], op0=ALU.mult, op1=ALU.add), [rawf_b, lbf_b], [rawf_b])
        P.op("dve", lambda e: e.tensor_tensor(out=R_(khat), in0=rawf, in1=enb, op=ALU.mult), [rawf_b, enb_b], [khat_b])
        if fused:
            if first:
                P.op("dve", lambda e: e.tensor_scalar(out=R_(St[0]), in0=ident, scalar1=0.0, scalar2=None, op0=ALU.mult), [ident_b], [St_b[0]])
            else:
                P.dma("pool", R_(St[0]), hstate[h], [hst_b], [St_b[0]], St_b[0])
        elif outputs:
            for i in range(3):
                P.dma("pool", R_(tS), d["sprev"][i, h], [], [tS_b], tS_b)
                if i == 0:
                    P.op("dve", lambda e: e.tensor_copy(out=R_(St[0]), in_=tS), [tS_b], [St_b[0]])
                else:
                    P.op("dve", lambda e, i=i, h=h: e.scalar_tensor_tensor(out=R_(St[0]), in0=St[0], scalar=aprev[:, i * 8 + h:i * 8 + h + 1], in1=tS, op0=ALU.mult, op1=ALU.add),
                         [St_b[0], aprev_b, tS_b], [St_b[0]])
        else:
            P.op("dve", lambda e: e.tensor_scalar(out=R_(St[0]), in0=ident, scalar1=0.0, scalar2=None, op0=ALU.mult), [ident_b], [St_b[0]])
        for j in range(8):
            js = slice(j * 128, (j + 1) * 128)
            ja = slice(j * 128, j * 128 + 64); jb = slice(j * 128 + 64, (j + 1) * 128)
            ca = j * 128 + 63; cb_ = j * 128 + 127
            cur = j % 2
            P.op("dve", lambda e, ja=ja, ca=ca: e.tensor_scalar(out=R_(kk[:, 0:64]), in0=khat[:, ja], scalar1=eb[:, ca:ca + 1], scalar2=None, op0=ALU.mult), [khat_b, eb_b], [kk_b])
            P.op("dve", lambda e, jb=jb, cb_=cb_: e.tensor_scalar(out=R_(kk[:, 64:128]), in0=khat[:, jb], scalar1=eb[:, cb_:cb_ + 1], scalar2=None, op0=ALU.mult), [khat_b, eb_b], [kk_b])
            pt = c.nextps(0, 8)
            P.op("pe", lambda e, pt=pt: e.matmul(c.ps[pt][:, 0:128], R_(kk), R_(ident), start=True, stop=True), [kk_b, ident_b], [c.ps_b[pt]])
            P.op("act", lambda e, pt=pt: e.copy(out=R_(ktl), in_=c.ps[pt][:, 0:128]), [c.ps_b[pt]], [ktl_b])
            pa = c.nextps(0, 8)
            P.op("pe", lambda e, j=j, pa=pa: e.matmul(c.ps[pa][:, 0:128], R_(ktl[0:64, :]), R_(vtm[0:64, j, :]), start=True, stop=True), [ktl_b, vtm_b], [c.ps_b[pa]])
            P.op("dve", lambda e, pa=pa, ca=ca: e.scalar_tensor_tensor(out=R_(St[1]), in0=St[0], scalar=eb[:, ca:ca + 1], in1=c.ps[pa][:, 0:128], op0=ALU.mult, op1=ALU.add),
                 [St_b[0], eb_b, c.ps_b[pa]], [St_b[1]])
            if outputs:
                pb = c.nextps(0, 8)
                P.op("pe", lambda e, js=js, pb=pb: e.matmul(c.ps[pb][:, 0:128], R_(khat[:, js]), R_(qhat[:, js]), start=True, stop=True), [khat_b, qhat_b], [c.ps_b[pb]])
                P.op("dve", lambda e, pb=pb, cur=cur: e.tensor_tensor(out=R_(am[cur]), in0=c.ps[pb][:, 0:128], in1=mtri[:], op=ALU.mult), [c.ps_b[pb], mtri_b], [am_b[cur]])
                po = c.nextps(0, 8)
                for (lo, qsl, stt) in ((0, ja, 0), (64, jb, 1)):
                    P.op("pe", lambda e, j=j, lo=lo, po=po, cur=cur: e.matmul(c.ps[po][:, lo:lo + 64], R_(vtm[:, j, :]), R_(am[cur][:, lo:lo + 64]), start=True, stop=False),
                         [vtm_b, am_b[cur]], [c.ps_b[po]])
                    P.op("pe", lambda e, lo=lo, po=po, qsl=qsl, stt=stt: e.matmul(c.ps[po][:, lo:lo + 64], R_(St[stt]), R_(qhat[:, qsl]), start=False, stop=True),
                         [St_b[stt], qhat_b], [c.ps_b[po]])
                P.op("act", lambda e, js=js, po=po: e.copy(out=R_(outT[:, 0, js]), in_=c.ps[po][:, 0:128]), [c.ps_b[po]], [outT_b])
            elif not fused:
                P.op("dve", lambda e, ca=ca, h=h: e.tensor_tensor(out=aprod[:, h:h + 1], in0=aprod[:, h:h + 1], in1=eb[:, ca:ca + 1], op=ALU.mult), [aprod_b, eb_b], [aprod_b])
                P.op("dve", lambda e, cb_=cb_, h=h: e.tensor_tensor(out=aprod[:, h:h + 1], in0=aprod[:, h:h + 1], in1=eb[:, cb_:cb_ + 1], op=ALU.mult), [aprod_b, eb_b], [aprod_b])
            if j < 7 or not outputs:
                pc = c.nextps(0, 8)
                P.op("pe", lambda e, j=j, pc=pc: e.matmul(c.ps[pc][:, 0:128], R_(ktl[64:128, :]), R_(vtm[64:128, j, :]), start=True, stop=True), [ktl_b, vtm_b], [c.ps_b[pc]])
                P.op("dve", lambda e, pc=pc, cb_=cb_: e.scalar_tensor_tensor(out=R_(St[0]), in0=St[1], scalar=eb[:, cb_:cb_ + 1], in1=c.ps[pc][:, 0:128], op0=ALU.mult, op1=ALU.add),
                     [St_b[1], eb_b, c.ps_b[pc]], [St_b[0]])
        if outputs:
            groupnorm_gate(c, outT, outT_b, 1, 128, gate, gate_b, ngain[:, h:h + 1], ngain_b, mixedT, mixed_b, sq, sq_b, rstd, rstd_b)
            if dbg is not None:
                P.dma("sp", dbg[h], mixedT[:, 0, :], [mixed_b], [], mixed_b)
            outproj_partial(c, X, X_b, R_(mixedT), mixed_b, wout_d, [h], wo, wo_b)
        else:
            P.op("dve", lambda e: e.tensor_copy(out=sout[:], in_=St[0]), [St_b[0]], [sout_b])
            if fused:
                P.dma("sp", hstate[h], sout[:], [sout_b], [hst_b], sout_b)
            else:
                P.dma("sp", d["sloc"][h], sout[:], [sout_b], [], sout_b)
    if not outputs and not fused:
        P.dma("sp", d["aout"], aprod[:], [aprod_b], [], aprod_b)
        return [sout_b, aprod_b]
    return []


def rstd_fm(c, src, src_b, nchunk, width, sq, sq_b, rstd, rstd_b):
    P = c.P
    for half in range(2):
        hs = slice(half * 512, (half + 1) * 512)
        pb = c.nextps(0, 8)
        for i in range(nchunk):
            sl = i % 2
            ss = slice(sl * 512, (sl + 1) * 512)
            P.op("act", lambda e, i=i, hs=hs, ss=ss: e.activation(out=sq[:, ss], in_=src[:, i, hs], func=AF.Square), [src_b], [sq_b[sl]])
            P.op("pe", lambda e, i=i, ss=ss, pb=pb: e.matmul(c.ps[pb][:], c.ones[:], sq[:, ss], start=(i == 0), stop=(i == nchunk - 1)),
                 [c.ones_b, sq_b[sl]], [c.ps_b[pb]])
        P.op("act", lambda e, hs=hs, pb=pb: e.activation(out=rstd[:, hs], in_=c.ps[pb][:], func=AF.Sqrt, bias=c.eps[:, 0:1], scale=1.0 / width),
             [c.ps_b[pb], c.eps_b], [rstd_b[half]])
        P.op("dve", lambda e, hs=hs: e.reciprocal(out=rstd[:, hs], in_=rstd[:, hs]), [rstd_b[half]], [rstd_b[half]])


L1_FM = 30
L1_TM = 10


def l1_common_inputs(c):
    d = {"lb0f": c.din("lb0f", [128, 8]), "lb1f": c.din("lb1f", [128, 8]), "lb0t": c.din("lb0t", [128, 1024]), "lb1t": c.din("lb1t", [128, 1024]),
         "ltri": c.din("ltri", [128, 128]), "mtri": c.din("mtri", [128, 128]), "ident": c.din("ident", [128, 128])}
    return d


def build_k3():
    c = Ctx("k3")
    P = c.P
    xT = c.din("xT", [128, KC, T])
    g_mix = c.din("g_mix", [128, KC])
    wfm = c.din("wfm", [L1_FM, 128, KC * 128]); wtm = c.din("wtm", [L1_TM, 128, KC * 256])
    d = l1_common_inputs(c)
    ckvg_f = c.din("ckvg_f", [128, 2]); ckvg_t = c.din("ckvg_t", [128, 256])
    d["sloc"] = c.dout("sloc", [8, 128, 128]); d["aout"] = c.dout("aout", [128, 8])
    kvtm_o = c.dout("kvtm", [8, 128, 256]); kvT_o = c.dout("kvT", [2, 128, 1024]); kixT_o = c.dout("kixT", [128, 1024])
    zfm = c.dscr("zfm1", [L1_FM, 128, 1024]); ztm = c.dscr("ztm1", [8, 128, L1_TM * 256])
    X = c.sb("X", [128, KC, T]); X_b = P.bufs(2 * KC, "X")
    H = c.sb("H", [128, KC * T]); H_b = P.buf("H")
    WR = c.sb("WR", [128, 14336])
    FS = c.sb("FS", [128, 4096])
    Hr = H[:, :].bitcast(F32R).rearrange("p (k t) -> p k t", k=KC)
    gm_sb, gm_b = load_small(c, "gm_sb", g_mix, [128, KC])
    sq = FS[:, 0:1024]; sq_b = P.bufs(2, "sq")
    rstd = FS[:, 1024:2048]; rstd_b = P.bufs(2, "rstd")
    load_X(c, X, X_b, xT)
    rmsnorm_fm(c, X, X_b, Hr, H_b, gm_sb, gm_b, sq, sq_b, rstd, rstd_b)
    zfm_b, ztm_b = proj_phase(c, Hr, H_b, KC, wfm, L1_FM, zfm, wtm, L1_TM, ztm, WR, "k3")
    P.barrier()
    fin = hgrn_phase(c, X, X_b, H, WR, FS, zfm, zfm_b, ztm, ztm_b, d, None, sq, sq_b, rstd, rstd_b, outputs=False)
    P.barrier()
    Xf = X[:, :, :].rearrange("p k t -> p (k t)")
    gf, gf_b = load_small(c, "ckvgf_sb", ckvg_f, [128, 2])
    gt, gt_b = load_small(c, "ckvgt_sb", ckvg_t, [128, 256])
    ckT = Xf[:, 0:2048].rearrange("p (a t) -> p a t", a=2); ckT_b = P.buf("ckT")
    for a in range(2):
        P.dma("sp", ckT[:, a, :], zfm[27 + a], [zfm_b], [ckT_b], ckT_b)
    rstd_fm(c, ckT, ckT_b, 2, 256, sq, sq_b, rstd, rstd_b)
    for a in range(2):
        P.op("dve", lambda e, a=a: e.scalar_tensor_tensor(out=ckT[:, a, :], in0=ckT[:, a, :], scalar=gf[:, a:a + 1], in1=rstd[:, :], op0=ALU.mult, op1=ALU.mult),
             [ckT_b, gf_b, rstd_b[0], rstd_b[1]], [ckT_b])
        P.dma("sp", kvT_o[a], ckT[:, a, :], [ckT_b], [], ckT_b)
    kx = Xf[:, 2048:3072]; kx_b = P.buf("kx")
    P.dma("sp", kx, zfm[29], [zfm_b], [kx_b], kx_b)
    P.dma("sp", kixT_o, kx, [kx_b], [], kx_b)
    ck = Xf[:, 4096:6144].rearrange("p (j r) -> p j r", j=8); ck_b = P.bufs(8, "ck")
    junk = Xf[:, 6144:6400]; junk_b = P.buf("ckjunk")
    st = c.sb("ck_st", [128, 8, 2]); st_b = P.bufs(8, "ckst")
    P.op("dve", lambda e: e.memset(st[:], 0.0), [], st_b)
    for j in range(8):
        P.dma("sp", ck[:, j, :], ztm[j, :, 2048:2304], [ztm_b], [ck_b[j]], ck_b[j])
        P.op("act", lambda e, j=j: e.activation(out=junk, in_=ck[:, j, :], func=AF.Square, accum_out=st[:, j, 0:1]), [ck_b[j], st_b[j], junk_b], [junk_b, st_b[j]])
        P.op("act", lambda e, j=j: e.activation(out=st[:, j, 1:2], in_=st[:, j, 0:1], func=AF.Sqrt, bias=c.eps[:, 0:1], scale=1.0 / 256), [st_b[j], c.eps_b], [st_b[j]])
        P.op("dve", lambda e, j=j: e.reciprocal(out=st[:, j, 1:2], in_=st[:, j, 1:2]), [st_b[j]], [st_b[j]])
        P.op("dve", lambda e, j=j: e.scalar_tensor_tensor(out=ck[:, j, :], in0=ck[:, j, :], scalar=st[:, j, 1:2], in1=gt[:], op0=ALU.mult, op1=ALU.mult),
             [ck_b[j], st_b[j], gt_b], [ck_b[j]])
        P.dma("sp", kvtm_o[j], ck[:, j, :], [ck_b[j]], [], ck_b[j])
    return c.finish(fin + [ckT_b, kx_b] + ck_b)


def l1_weights(w_in):
    pad_fm = np.zeros((D, 128), np.float32); pad_fm[:, 0:80] = w_in[:, 4736:4816]
    fm = np.concatenate([w_in[:, 0:2048], w_in[:, 3072:4096], w_in[:, 4096:4480], w_in[:, 4480:4736], pad_fm], 1)
    pad_tm = np.zeros((D, 256), np.float32); pad_tm[:, 0:80] = w_in[:, 4736:4816]
    tm = np.concatenate([w_in[:, 1024:3072], w_in[:, 4480:4736], pad_tm], 1)
    return tile_fm(np.ascontiguousarray(fm)), tile_tm(np.ascontiguousarray(tm))


def l1_tables():
    s = np.arange(128)
    same = (s[:, None] // 64) == (s[None, :] // 64)
    tri = (same & (s[:, None] <= s[None, :])).astype(np.float32)
    return tri.copy(), tri.copy(), np.eye(128, dtype=np.float32)


def bcast_rows(v, n=128):
    return np.ascontiguousarray(np.broadcast_to(np.asarray(v)[None], (n,) + np.asarray(v).shape))


def build_k4(dsa=True, dbg=False):
    c = Ctx("k4")
    P = c.P
    xT = c.din("xT", [128, KC, T])
    g_mix = c.din("g_mix", [128, KC]); g_ffn = c.din("g_ffn", [128, KC])
    wfm = c.din("wfm", [L1_FM, 128, KC * 128]); wtm = c.din("wtm", [L1_TM, 128, KC * 256])
    d = l1_common_inputs(c)
    d["ngain"] = c.din("ngain", [128, 8]); d["sprev"] = c.din("sprev", [3, 8, 128, 128]); d["aprev"] = c.din("aprev", [128, 24])
    wout = c.din("wout", [D, D])
    wg = c.din("wg", [FC, 128, 2048]); wu = c.din("wu", [FC, 128, 2048]); wd = c.din("wd", [DFF, D])
    out = c.dout("out", [128, KC, T])
    dbg_o = c.dout("dbg", [16, 128, 1024]) if dbg else None
    zfm = c.dscr("zfm1", [L1_FM, 128, 1024]); ztm = c.dscr("ztm1", [8, 128, L1_TM * 256])
    X = c.sb("X", [128, KC, T]); X_b = P.bufs(2 * KC, "X")
    H = c.sb("H", [128, KC * T]); H_b = P.buf("H")
    WR = c.sb("WR", [128, 14336])
    FS = c.sb("FS", [128, 4096])
    Hr = H[:, :].bitcast(F32R).rearrange("p (k t) -> p k t", k=KC)
    gm_sb, gm_b = load_small(c, "gm_sb", g_mix, [128, KC])
    gf_sb, gf_b = load_small(c, "gf_sb", g_ffn, [128, KC])
    sq = FS[:, 0:1024]; sq_b = P.bufs(2, "sq")
    rstd = FS[:, 1024:2048]; rstd_b = P.bufs(2, "rstd")
    load_X(c, X, X_b, xT)
    rmsnorm_fm(c, X, X_b, Hr, H_b, gm_sb, gm_b, sq, sq_b, rstd, rstd_b)
    zfm_b, ztm_b = proj_phase(c, Hr, H_b, KC, wfm, L1_FM, zfm, wtm, L1_TM, ztm, WR, "k4")
    P.barrier()
    hgrn_phase(c, X, X_b, H, WR, FS, zfm, zfm_b, ztm, ztm_b, d, wout, sq, sq_b, rstd, rstd_b, outputs=True, dbg=dbg_o)
    P.barrier()
    if dsa:
        dsa_phase(c, X, X_b, H, WR, FS, zfm, zfm_b, ztm, ztm_b, wout, sq, sq_b, rstd, rstd_b, dbg_o)
        P.barrier()
    rmsnorm_fm(c, X, X_b, Hr, H_b, gf_sb, gf_b, sq, sq_b, rstd, rstd_b)
    ffn_phase(c, X, X_b, Hr, H_b, wg, wu, wd, WR, FS[:, 2048:3072])
    store_X(c, X, X_b, out)
    return c.finish(X_b)


BIG = 1.0e30
REPL = -3.0e38


def dsa_phase(c, X, X_b, H, WR, FS, zfm, zfm_b, ztm, ztm_b, wout_d, sq, sq_b, rstd, rstd_b, dbg=None, keys=None, prep=None):
    P = c.P
    d = {} if keys is not None else {"kvT_all": c.din("kvT_all", [2, 128, 4096]), "kvtm_all": c.din("kvtm_all", [32, 128, 256]), "kixT_all": c.din("kixT_all", [128, 4096])}
    keys_b = P.buf("dsakeys")
    if keys is not None:
        d.update(keys)
    d.update({"slotadd": c.din("slotadd", [128, 3]), "slotok": c.din("slotok", [128, 3]),
         "admadd": c.din("admadd", [128, 128]), "admok": c.din("admok", [128, 128]),
         "wuq": c.din("wuq", [16, 128, 384]), "wqi": c.din("wqi", [8, 128, 384]),
         "cqg": c.din("cqg", [128, 3]), "qng": c.din("qng", [128, 2]), "wuv": c.din("wuv", [128, 8, 2, 128]),
         "bias_g": c.din("bias_g", [128, 8, 3, 128]), "crow": c.din("crow", [128, 8]), "ident2": c.din("ident2", [128, 128])})
    xpark = c.dscr("xpark", [128, KC, T])
    qT_s = c.dscr("qT_s", [16, 128, 1024]); qiT_s = c.dscr("qiT_s", [8, 128, 1024]); at_s = c.dscr("at_s", [8, 128, 1024])
    qT_sb = P.buf("qT_s"); qiT_sb = P.buf("qiT_s"); at_sb = P.buf("at_s"); xpark_b = P.buf("xpark")
    for k in range(KC):
        P.dma("sp", xpark[:, k, :], X[:, k, :], [X_b[2 * k], X_b[2 * k + 1]], [xpark_b], X_b[8 * (k // 4)])
    P.barrier()
    if prep is not None:
        prep()
        P.barrier()
    Xf = X[:, :, :].rearrange("p k t -> p (k t)")
    sc = Xf[:, 0:4096]; sc_b = P.buf("sc")
    maskT = Xf[:, 4096:8192].rearrange("p (k t) -> p k t", k=32); maskT_b = P.buf("maskT")
    biasT = Xf[:, 8192:11264].rearrange("p (h n t) -> p h n t", h=8, n=3); biasT_b = P.buf("biasT")
    wi = Xf[:, 11264:11392].rearrange("p (j h) -> p j h", j=8)
    aw = Xf[:, 11392:11520].rearrange("p (j h) -> p j h", j=8)
    sg = Xf[:, 11520:11648].rearrange("p (j h) -> p j h", j=8); wi_b = P.buf("wi")
    admadd = Xf[:, 11648:11776]; admok = Xf[:, 11776:11904]; adm_b = P.buf("adm")
    m8 = Xf[:, 11904:11912]; m8_b = P.buf("m8")
    qraw = Xf[:, 12288:14336].rearrange("p (a t) -> p a t", a=2); qraw_b = P.buf("qraw")
    qout = Xf[:, 14336:16384].rearrange("p (a t) -> p a t", a=2); qout_b = P.buf("qout")
    kvT = H[:, 0:8192].rearrange("p (a s) -> p a s", a=2); kvT_b = P.buf("kvTall")
    kvtm = H[:, 8192:16384].rearrange("p (k r) -> p k r", k=32); kvtm_b = P.buf("kvtmall")
    kix = WR[:, 0:4096]; kix_b = P.buf("kixall")
    sel = WR[:, 4096:8192]; sel_b = P.buf("sel")
    E = [WR[:, 4096 + s * 512:4096 + (s + 1) * 512] for s in range(3)]; E_b = P.bufs(3, "E")
    onT = WR[:, 6144:7168].rearrange("p (a t) -> p a t", a=2); onT_b = P.buf("onT")
    qblk = WR[:, 8192:10240].rearrange("p (n t) -> p n t", n=16); qblk_b = P.buf("qblk")
    qiblk = WR[:, 10240:11264].rearrange("p (n t) -> p n t", n=8); qiblk_b = P.buf("qiblk")
    wuv = WR[:, 11264:13312].rearrange("p (h a v) -> p h a v", h=8, a=2); wuv_b = P.buf("wuv")
    ident = WR[:, 13312:13440]; onesR = WR[:, 13440:13568]; id_b = P.buf("ident2")
    cqT = WR[:, 8192:11264].rearrange("p (a t) -> p a t", a=3); cqT_b = P.buf("cqT")
    wq = [WR[:, 11264 + s * 384:11264 + (s + 1) * 384].rearrange("p (k j) -> p k j", k=3) for s in range(2)]; wq_b = P.bufs(2, "wq")
    rtmp = [FS[:, 2048 + s * 512:2048 + (s + 1) * 512] for s in range(2)]; rtmp_b = P.bufs(2, "rtmp")
    ltmp = [FS[:, 2560 + s * 512:2560 + (s + 1) * 512] for s in range(3)]; ltmp_b = P.bufs(3, "ltmp")
    rden = FS[:, 2048:2560]
    cqg, cqg_b = load_small(c, "cqg_sb", d["cqg"], [128, 3])
    qng, qng_b = load_small(c, "qng_sb", d["qng"], [128, 2])
    slotadd, sla_b = load_small(c, "slotadd_sb", d["slotadd"], [128, 3])
    slotok, slo_b = load_small(c, "slotok_sb", d["slotok"], [128, 3])
    crow, crow_b = load_small(c, "crow_sb", d["crow"], [128, 8])
    P.dma("pool", R_(ident), d["ident2"], [], [id_b], None)
    P.op("dve", lambda e: e.tensor_scalar(out=R_(onesR), in0=ident, scalar1=0.0, scalar2=1.0, op0=ALU.mult, op1=ALU.add), [id_b], [id_b])
    P.dma("sp", admadd, d["admadd"], [], [adm_b], None)
    P.dma("sp", admok, d["admok"], [], [adm_b], None)
    for a in range(3):
        P.dma("pool", R_(cqT[:, a, :]), zfm[24 + a], [zfm_b], [cqT_b], cqT_b)
    rstd_fm(c, cqT, cqT_b, 3, 384, sq, sq_b, rstd, rstd_b)
    for a in range(3):
        P.op("dve", lambda e, a=a: e.scalar_tensor_tensor(out=R_(cqT[:, a, :]), in0=cqT[:, a, :], scalar=cqg[:, a:a + 1], in1=rstd[:, :], op0=ALU.mult, op1=ALU.mult),
             [cqT_b, cqg_b, rstd_b[0], rstd_b[1]], [cqT_b])
    nw = 0

    def small_proj(w_d, n, dst, dst_b, act_eng):
        nonlocal nw
        s = nw % 2
        nw += 1
        P.dma("pool", R_(wq[s]), w_d[n].rearrange("p (k j) -> p k j", k=3), [], [wq_b[s]], wq_b[s])
        for half in range(2):
            hs = slice(half * 512, (half + 1) * 512)
            pb = c.nextps(0, 8)
            for k in range(3):
                P.op("pe", lambda e, s=s, k=k, hs=hs, pb=pb: e.matmul(c.ps[pb][:], R_(wq[s][:, k, :]), R_(cqT[:, k, hs]), start=(k == 0), stop=(k == 2)),
                     [wq_b[s], cqT_b], [c.ps_b[pb]])
            P.op(act_eng, (lambda e, hs=hs, pb=pb: e.copy(out=dst[:, hs], in_=c.ps[pb][:])) if act_eng == "act" else
                 (lambda e, hs=hs, pb=pb: e.tensor_copy(out=dst[:, hs], in_=c.ps[pb][:])), [c.ps_b[pb]], [dst_b])

    for h in range(8):
        for a in range(2):
            small_proj(d["wuq"], 2 * h + a, qraw[:, a, :], qraw_b, "act")
        rstd_fm(c, qraw, qraw_b, 2, 256, sq, sq_b, rstd, rstd_b)
        for a in range(2):
            P.op("dve", lambda e, a=a: e.scalar_tensor_tensor(out=qout[:, a, :], in0=qraw[:, a, :], scalar=qng[:, a:a + 1], in1=rstd[:, :], op0=ALU.mult, op1=ALU.mult),
                 [qraw_b, qng_b, rstd_b[0], rstd_b[1]], [qout_b])
            P.dma("sp", qT_s[2 * h + a], qout[:, a, :], [qout_b], [qT_sb], qout_b)
    for n in range(8):
        a = n % 2
        small_proj(d["wqi"], n, qout[:, a, :], qout_b, "dve")
        P.dma("sp", qiT_s[n], qout[:, a, :], [qout_b], [qiT_sb], qout_b)
    P.barrier()
    for a in range(2):
        P.dma("pool", R_(kvT[:, a, :]), d["kvT_all"][a], [keys_b], [kvT_b], None)
    for k4 in range(4):
        P.dma("pool", R_(kvtm[:, k4 * 8:(k4 + 1) * 8, :]), d["kvtm_all"][k4 * 8:(k4 + 1) * 8].rearrange("k p r -> p k r"), [keys_b], [kvtm_b], None)
    P.dma("pool", R_(kix), d["kixT_all"], [keys_b], [kix_b], None)
    P.dma("pool", R_(wuv), d["wuv"], [], [wuv_b], None)
    P.dma("sp", wi, ztm[:, :, 2304 + 64:2304 + 80].rearrange("j p h -> p j h"), [ztm_b], [wi_b], None)
    P.op("act", lambda e: e.activation(out=sg, in_=wi, func=AF.Sign), [wi_b], [wi_b])
    P.op("dve", lambda e: e.tensor_tensor(out=aw, in0=wi, in1=sg, op=ALU.mult), [wi_b], [wi_b])
    P.op("dve", lambda e: e.tensor_scalar(out=aw, in0=aw, scalar1=1.0 / 32.0, scalar2=None, op0=ALU.mult), [wi_b], [wi_b])
    P.dma("sp", biasT, d["bias_g"], [], [biasT_b], None)
    for h in range(8):
        P.op("dve", lambda e, h=h: e.tensor_scalar(out=biasT[:, h], in0=biasT[:, h], scalar1=crow[:, h:h + 1], scalar2=16.0, op0=ALU.subtract, op1=ALU.mult),
             [biasT_b, crow_b], [biasT_b])
    for qb in range(8):
        qs_ = slice(qb * 128, (qb + 1) * 128)
        nown = (qb + 1) * 128
        P.dma("pool", R_(qblk), qT_s[:, :, qs_].rearrange("n p t -> p n t"), [qT_sb], [qblk_b], qblk_b)
        P.dma("pool", R_(qiblk), qiT_s[:, :, qs_].rearrange("n p t -> p n t"), [qiT_sb], [qiblk_b], qiblk_b)
        tiles = [(lo, min(lo + 512, nown)) for lo in range(0, nown, 512)] + [(lo, lo + 512) for lo in range(1024, 4096, 512)]
        cnt = 0
        for (lo, hi_) in tiles:
            n = hi_ - lo
            for hd in range(16):
                pr = slice((hd % 2) * 64, (hd % 2) * 64 + 64)
                pb = c.nextps(3, 5)
                P.op("pe", lambda e, hd=hd, pr=pr, lo=lo, hi_=hi_, n=n, pb=pb: e.matmul(c.ps[pb][:, 0:n], R_(qiblk[pr, hd // 2, :]), R_(kix[pr, lo:hi_]), start=True, stop=True),
                     [qiblk_b, kix_b], [c.ps_b[pb]])
                s = cnt % 2
                cnt += 1
                P.op("act", lambda e, hd=hd, n=n, pb=pb, s=s, qb=qb: e.activation(out=rtmp[s][:, 0:n], in_=c.ps[pb][:, 0:n], func=AF.Relu, scale=aw[:, qb, hd:hd + 1]),
                     [c.ps_b[pb], wi_b], [rtmp_b[s]])
                if hd == 0:
                    P.op("dve", lambda e, lo=lo, hi_=hi_, n=n, s=s, qb=qb, hd=hd: e.tensor_scalar(out=sc[:, lo:hi_], in0=rtmp[s][:, 0:n], scalar1=sg[:, qb, hd:hd + 1], scalar2=None, op0=ALU.mult),
                         [rtmp_b[s], wi_b], [sc_b])
                else:
                    P.op("dve", lambda e, lo=lo, hi_=hi_, n=n, s=s, qb=qb, hd=hd: e.scalar_tensor_tensor(out=sc[:, lo:hi_], in0=rtmp[s][:, 0:n], scalar=sg[:, qb, hd:hd + 1], in1=sc[:, lo:hi_],
                                                                                                  op0=ALU.mult, op1=ALU.add), [rtmp_b[s], wi_b, sc_b], [sc_b])
        dsl = slice(qb * 128, (qb + 1) * 128)
        P.op("dve", lambda e, dsl=dsl: e.tensor_tensor(out=sc[:, dsl], in0=sc[:, dsl], in1=admadd, op=ALU.add), [sc_b, adm_b], [sc_b])
        if nown < 1024:
            P.op("dve", lambda e, nown=nown: e.memset(sc[:, nown:1024], -BIG), [sc_b], [sc_b])
        for i in range(3):
            P.op("dve", lambda e, i=i: e.tensor_scalar(out=sc[:, 1024 * (i + 1):1024 * (i + 2)], in0=sc[:, 1024 * (i + 1):1024 * (i + 2)], scalar1=slotadd[:, i:i + 1], scalar2=None, op0=ALU.add),
                 [sc_b, sla_b], [sc_b])
        for r in range(32):
            P.op("dve", lambda e: e.max(out=m8, in_=sc), [sc_b], [m8_b])
            P.op("dve", lambda e: e.match_replace(out=sc, in_to_replace=m8, in_values=sc, imm_value=REPL), [sc_b, m8_b], [sc_b])
        P.barrier()
        P.op("dve", lambda e: e.tensor_scalar(out=R_(sel), in0=sc, scalar1=REPL, scalar2=None, op0=ALU.is_equal), [sc_b], [sel_b])
        P.op("dve", lambda e, dsl=dsl: e.tensor_tensor(out=R_(sel[:, dsl]), in0=sel[:, dsl], in1=admok, op=ALU.mult), [sel_b, adm_b], [sel_b])
        if nown < 1024:
            P.op("dve", lambda e, nown=nown: e.tensor_scalar(out=R_(sel[:, nown:1024]), in0=sel[:, nown:1024], scalar1=0.0, scalar2=None, op0=ALU.mult), [sel_b], [sel_b])
        for i in range(3):
            P.op("dve", lambda e, i=i: e.tensor_scalar(out=R_(sel[:, 1024 * (i + 1):1024 * (i + 2)]), in0=sel[:, 1024 * (i + 1):1024 * (i + 2)], scalar1=slotok[:, i:i + 1], scalar2=None, op0=ALU.mult),
                 [sel_b, slo_b], [sel_b])
        kbs = list(range(qb + 1)) + list(range(8, 32))
        for kb in kbs:
            pb = c.nextps(3, 5)
            P.op("pe", lambda e, kb=kb, pb=pb: e.matmul(c.ps[pb][:, 0:128], R_(sel[:, kb * 128:(kb + 1) * 128]), R_(ident), start=True, stop=True), [sel_b, id_b], [c.ps_b[pb]])
            P.op("dve", lambda e, kb=kb, pb=pb: e.tensor_scalar(out=maskT[:, kb, :], in0=c.ps[pb][:, 0:128], scalar1=-1.0, scalar2=BIG, op0=ALU.add, op1=ALU.mult), [c.ps_b[pb]], [maskT_b])
        P.barrier()
        near = {}
        for g in (0, -1, -2):
            kb = qb + g
            near[kb if kb >= 0 else 16 + kb] = 2 + g
        NS = 3
        for hg in range(2):
            def emit_logits(ki, kb, hg=hg):
                ks = slice(kb * 128, (kb + 1) * 128)
                pb = c.nextps(3, 5)
                for a in range(2):
                    P.op("pe", lambda e, a=a, ks=ks, pb=pb, hg=hg: e.matmul(c.ps[pb][:], R_(kvT[:, a, ks]), R_(qblk[:, 8 * hg + a:8 * hg + 8:2, :]), start=(a == 0), stop=(a == 1)),
                         [kvT_b, qblk_b], [c.ps_b[pb]])
                return pb

            def emit_rest(ki, kb, pb, hg=hg):
                s = ki % NS
                lt3 = ltmp[s].rearrange("p (h t) -> p h t", h=4)
                P.op("dve", lambda e, kb=kb, pb=pb, lt3=lt3: e.tensor_tensor(out=lt3, in0=c.ps[pb][:].rearrange("p (h t) -> p h t", h=4), in1=maskT[:, kb:kb + 1, :].to_broadcast([128, 4, 128]), op=ALU.add),
                     [c.ps_b[pb], maskT_b], [ltmp_b[s]])
                if kb in near:
                    nb = near[kb]
                    P.op("dve", lambda e, lt3=lt3, nb=nb, hg=hg: e.tensor_tensor(out=lt3, in0=lt3, in1=biasT[:, 4 * hg:4 * hg + 4, nb, :], op=ALU.add), [ltmp_b[s], biasT_b], [ltmp_b[s]])
                P.op("act", lambda e, s=s: e.activation(out=R_(E[s]), in_=ltmp[s], func=AF.Exp, scale=1.0 / 16.0), [ltmp_b[s]], [E_b[s]])
                first, last = (ki == 0), (ki == len(kbs) - 1)
                for a in range(2):
                    P.op("pe", lambda e, a=a, kb=kb, s=s, first=first, last=last: e.matmul(c.ps[a][:], R_(kvtm[:, kb, a * 128:(a + 1) * 128]), R_(E[s]), start=first, stop=last),
                         [kvtm_b, E_b[s]], [c.ps_b[a]])
                P.op("pe", lambda e, s=s, first=first, last=last: e.matmul(c.ps[2][:], R_(onesR), R_(E[s]), start=first, stop=last), [id_b, E_b[s]], [c.ps_b[2]])

            pend = [emit_logits(0, kbs[0])]
            if len(kbs) > 1:
                pend.append(emit_logits(1, kbs[1]))
            for ki, kb in enumerate(kbs):
                if ki + 2 < len(kbs):
                    pend.append(emit_logits(ki + 2, kbs[ki + 2]))
                emit_rest(ki, kb, pend[ki])
            P.op("dve", lambda e: e.reciprocal(out=rden, in_=c.ps[2][:]), [c.ps_b[2], rtmp_b[0]], [rtmp_b[0]])
            for a in range(2):
                P.op("dve", lambda e, a=a: e.tensor_tensor(out=R_(onT[:, a, :]), in0=c.ps[a][:], in1=rden, op=ALU.mult), [c.ps_b[a], rtmp_b[0]], [onT_b])
            for hh in range(4):
                h = 4 * hg + hh
                pb = c.nextps(3, 5)
                for a in range(2):
                    P.op("pe", lambda e, a=a, h=h, hh=hh, pb=pb: e.matmul(c.ps[pb][:, 0:128], R_(wuv[:, h, a, :]), R_(onT[:, a, hh * 128:(hh + 1) * 128]), start=(a == 0), stop=(a == 1)),
                         [wuv_b, onT_b], [c.ps_b[pb]])
                s2 = hh % 2
                P.op("act", lambda e, pb=pb, s2=s2: e.copy(out=ltmp[s2][:, 0:128], in_=c.ps[pb][:, 0:128]), [c.ps_b[pb]], [ltmp_b[s2]])
                P.dma("sp", at_s[h, :, qs_], ltmp[s2][:, 0:128], [ltmp_b[s2]], [at_sb], ltmp_b[s2])
        P.barrier()
    P.barrier()
    for k in range(KC):
        P.dma("sp", X[:, k, :], xpark[:, k, :], [xpark_b], [X_b[2 * k], X_b[2 * k + 1]], None)
    mixedT = H[:, 0:2048].rearrange("p (a t) -> p a t", a=2); mixed_b = P.buf("dmixedT")
    wo = [R_(WR[:, 9216 + s * 2048:9216 + (s + 1) * 2048]) for s in range(2)]; wo_b = P.bufs(2, "wo")
    for hp in range(4):
        for a in range(2):
            P.dma("pool", R_(mixedT[:, a, :]), at_s[2 * hp + a], [at_sb], [mixed_b], mixed_b)
            if dbg is not None:
                P.dma("sp", dbg[8 + 2 * hp + a], mixedT[:, a, :], [mixed_b], [], mixed_b)
        outproj_partial(c, X, X_b, R_(mixedT), mixed_b, wout_d, [8 + 2 * hp, 8 + 2 * hp + 1], wo, wo_b)


def rel_bucket_np(rel):
    rel = np.asarray(rel, np.int32)
    ret = np.where(rel > 0, 16, 0)
    n = np.abs(rel)
    nf = np.maximum(n, 1).astype(np.float32)
    large = 8 + (np.log(nf / np.float32(8)) / np.float32(math.log(256 / 8)) * np.float32(8)).astype(np.int32)
    large = np.minimum(large, 15)
    return ret + np.where(n < 8, n, large)


def dsa_host_inputs(inp, k3res, b, q):
    kvT_all = np.zeros((2, 128, 4096), np.float32)
    kvtm_all = np.zeros((32, 128, 256), np.float32)
    kix_all = np.zeros((128, 4096), np.float32)
    slotadd = np.zeros((128, 3), np.float32)
    slotok = np.ones((128, 3), np.float32)
    srcs = [q] + [q - 1 - i for i in range(3)]
    for sl, sq_ in enumerate(srcs):
        if sq_ < 0:
            slotadd[:, sl - 1] = -BIG
            slotok[:, sl - 1] = 0.0
            continue
        r = k3res[b * 4 + sq_]
        kvT_all[:, :, sl * 1024:(sl + 1) * 1024] = r["kvT"]
        kvtm_all[sl * 8:(sl + 1) * 8] = r["kvtm"]
        kix_all[0:64, sl * 1024:(sl + 1) * 1024] = r["kixT"][0:64]
        kix_all[64:128, sl * 1024:(sl + 1) * 1024] = r["kixT"][0:64]
    t = np.arange(128)
    ok = (t[None, :] // 64) <= (t[:, None] // 64)
    admok = ok.astype(np.float32)
    admadd = np.where(ok, 0.0, -BIG).astype(np.float32)
    s_ = np.arange(128)[:, None, None]; nb = np.arange(3)[None, :, None]; tt = np.arange(128)[None, None, :]
    rel = (nb - 2) * 128 + s_ - tt
    bk = rel_bucket_np(rel)
    rb = inp["rel_bias"]
    bias_g = np.ascontiguousarray(rb[bk].transpose(0, 3, 1, 2))
    wuv = inp["dsa_w_uv"][0]
    return {"kvT_all": kvT_all, "kvtm_all": kvtm_all, "kixT_all": kix_all, "slotadd": slotadd, "slotok": slotok, "admadd": admadd, "admok": admok,
            "wuq": tile_fm(inp["dsa_w_uq"][0]), "wqi": tile_fm(inp["dsa_w_qidx"][0]), "cqg": vec_fm(inp["dsa_cq_g"][0]), "qng": vec_fm(inp["dsa_qnorm_g"][0]),
            "wuv": np.ascontiguousarray(wuv.reshape(8, 2, 128, 128).transpose(2, 0, 1, 3)), "bias_g": bias_g.astype(np.float32), "crow": bcast_rows(rb[15]),
            "ident2": np.eye(128, dtype=np.float32)}


_PROGS = {}


def _prog(name, fn):
    if name not in _PROGS:
        _PROGS[name] = fn()
    return _PROGS[name]


def _run(nc, ims):
    return run_bass_kernel_spmd(nc, ims, core_ids=list(range(NCORES))).results


def kernel_unfused(x, ln_mix_g, ln_ffn_g, w_ffn_gate, w_ffn_up, w_ffn_down, rel_bias,
           ev_w_in, ev_w_out, sgu_ln_g, sgu_ln_b, sgu_w_s, sgu_b_s,
           od_w_in, od_w_out, hgrn_lb, hgrn_norm_g, dsa_cq_g, dsa_ckv_g,
           dsa_w_uq, dsa_qnorm_g, dsa_w_qidx, dsa_w_uv):
    f = lambda a: np.ascontiguousarray(np.asarray(a, dtype=np.float32))
    x = f(x)
    inp = {"rel_bias": f(rel_bias), "dsa_w_uq": f(dsa_w_uq), "dsa_w_qidx": f(dsa_w_qidx), "dsa_cq_g": f(dsa_cq_g),
           "dsa_qnorm_g": f(dsa_qnorm_g), "dsa_w_uv": f(dsa_w_uv)}
    cores = [(c // 4, c % 4) for c in range(NCORES)]
    xs = [x_to_fm(x[b, q * T:(q + 1) * T]) for b, q in cores]
    ropes = [rope_tables(q) for q in range(4)]
    w_in = f(ev_w_in)[0]
    gm0 = vec_fm(f(ln_mix_g)[0])
    wtm1 = tile_tm(np.ascontiguousarray(w_in[:, 1024:3072]))
    kz1 = k1_tables()
    ims = [{"xT": xs[c], "g": gm0, "wtm": wtm1, "cos_tm": ropes[q][1][0], "sin_tm": ropes[q][1][1], "kz1": kz1} for c, (b, q) in enumerate(cores)]
    r1 = _run(_prog("k1", build_k1), ims)
    del wtm1
    wfm = tile_fm(np.ascontiguousarray(np.concatenate([w_in[:, 0:2048], w_in[:, 3072:5120]], 1)))
    wtm = tile_tm(np.ascontiguousarray(np.concatenate([w_in[:, 1024:3072], w_in[:, 5120:6144]], 1)))
    common = {"g_mix": gm0, "g_ffn": vec_fm(f(ln_ffn_g)[0]), "wfm": wfm, "wtm": wtm,
              "lng_b": bcast_rows(f(sgu_ln_g)[0]), "lnb_b": bcast_rows(f(sgu_ln_b)[0]),
              "wsT": np.ascontiguousarray(f(sgu_w_s)[0].transpose(2, 0, 1)), "bsb": bcast_rows(f(sgu_b_s)[0]),
              "wout": f(ev_w_out)[0], "wg": tile_fm(f(w_ffn_gate)[0]), "wu": tile_fm(f(w_ffn_up)[0]), "wd": f(w_ffn_down)[0]}
    ims = []
    for c, (b, q) in enumerate(cores):
        dmask, xi, kz, sw = k2_tables(q)
        im = dict(common)
        im.update({"xT": xs[c], "cosT": ropes[q][0][0], "sinT": ropes[q][0][1], "cos_tm": ropes[q][1][0], "sin_tm": ropes[q][1][1],
                   "dmask": dmask, "xi": xi, "kz": kz, "sw": sw, "sprev": np.stack([r1[b * 4 + i]["sloc"] for i in range(3)])})
        ims.append(im)
    r2 = _run(_prog("k2", build_k2), ims)
    del common, ims, wfm, wtm
    x1s = [r["out"] for r in r2]
    wfm, wtm = l1_weights(f(od_w_in)[0])
    ltri, mtri, ident = l1_tables()
    lb = f(hgrn_lb)
    common = {"g_mix": vec_fm(f(ln_mix_g)[1]), "wfm": wfm, "wtm": wtm, "lb0f": vec_fm(lb[0]), "lb1f": vec_fm(lb[1]),
              "lb0t": bcast_rows(lb[0]), "lb1t": bcast_rows(lb[1]), "ltri": ltri, "mtri": mtri, "ident": ident}
    ckvg = f(dsa_ckv_g)[0]
    ims = []
    for c in range(NCORES):
        im = dict(common)
        im.update({"xT": x1s[c], "ckvg_f": vec_fm(ckvg), "ckvg_t": bcast_rows(ckvg)})
        ims.append(im)
    r3 = _run(_prog("k3", build_k3), ims)
    common.update({"g_ffn": vec_fm(f(ln_ffn_g)[1]), "ngain": vec_fm(f(hgrn_norm_g)[0]), "wout": f(od_w_out)[0],
                   "wg": tile_fm(f(w_ffn_gate)[1]), "wu": tile_fm(f(w_ffn_up)[1]), "wd": f(w_ffn_down)[1]})
    ims = []
    for c, (b, q) in enumerate(cores):
        sprev = np.zeros((3, 8, 128, 128), np.float32)
        aprev = np.ones((128, 3, 8), np.float32)
        for i in range(3):
            if i < q:
                sprev[i] = r3[b * 4 + i]["sloc"]
                aprev[:, i, :] = r3[b * 4 + i]["aout"]
        im = dict(common)
        im.update({"xT": x1s[c], "sprev": sprev, "aprev": np.ascontiguousarray(aprev.reshape(128, 24))})
        im.update(dsa_host_inputs(inp, r3, b, q))
        ims.append(im)
    r4 = _run(_prog("k4", lambda: build_k4(dsa=True, dbg=False)), ims)
    out = np.zeros((2, 4096, D), np.float32)
    for c, (b, q) in enumerate(cores):
        out[b, q * T:(q + 1) * T] = fm_to_x(r4[c]["out"])
    return out


def dsa_prep(c, X, zfm, zfm_b, ztm, ztm_b, sq, sq_b, rstd, rstd_b, ckvg_f, ckvg_t, kvT_all, kvtm_all, kix_all, slot):
    P = c.P
    keys_b = P.buf("dsakeys")
    cs = slice(slot * 1024, (slot + 1) * 1024)
    Xf = X[:, :, :].rearrange("p k t -> p (k t)")
    gf, gf_b = load_small(c, "ckvgf_sb", ckvg_f, [128, 2])
    gt, gt_b = load_small(c, "ckvgt_sb", ckvg_t, [128, 256])
    ckT = Xf[:, 0:2048].rearrange("p (a t) -> p a t", a=2); ckT_b = P.buf("ckT")
    for a in range(2):
        P.dma("sp", ckT[:, a, :], zfm[27 + a], [zfm_b], [ckT_b], None)
    rstd_fm(c, ckT, ckT_b, 2, 256, sq, sq_b, rstd, rstd_b)
    for a in range(2):
        P.op("dve", lambda e, a=a: e.scalar_tensor_tensor(out=ckT[:, a, :], in0=ckT[:, a, :], scalar=gf[:, a:a + 1], in1=rstd[:, :], op0=ALU.mult, op1=ALU.mult),
             [ckT_b, gf_b, rstd_b[0], rstd_b[1]], [ckT_b])
        P.dma("sp", kvT_all[a][:, cs], ckT[:, a, :], [ckT_b], [keys_b], None)
    kx = Xf[:, 2048:3072]; kx_b = P.buf("kx")
    P.dma("sp", kx, zfm[29], [zfm_b], [kx_b], None)
    P.dma("sp", kix_all[0:64, cs], kx[0:64, :], [kx_b], [keys_b], None)
    P.dma("sp", kix_all[64:128, cs], kx[0:64, :], [kx_b], [keys_b], None)
    ck = Xf[:, 4096:6144].rearrange("p (j r) -> p j r", j=8); ck_b = P.buf("ck")
    junk = Xf[:, 6144:6400]; junk_b = P.buf("ckjunk")
    st = c.sb("ck_st", [128, 8, 2]); st_b = P.buf("ckst")
    P.op("dve", lambda e: e.memset(st[:], 0.0), [], [st_b])
    P.dma("sp", ck, ztm[:, :, 2048:2304].rearrange("j p r -> p j r"), [ztm_b], [ck_b], None)
    for j in range(8):
        P.op("act", lambda e, j=j: e.activation(out=junk, in_=ck[:, j, :], func=AF.Square, accum_out=st[:, j, 0:1]), [ck_b, st_b, junk_b], [junk_b, st_b])
        P.op("act", lambda e, j=j: e.activation(out=st[:, j, 1:2], in_=st[:, j, 0:1], func=AF.Sqrt, bias=c.eps[:, 0:1], scale=1.0 / 256), [st_b, c.eps_b], [st_b])
        P.op("dve", lambda e, j=j: e.reciprocal(out=st[:, j, 1:2], in_=st[:, j, 1:2]), [st_b], [st_b])
        P.op("dve", lambda e, j=j: e.scalar_tensor_tensor(out=ck[:, j, :], in0=ck[:, j, :], scalar=st[:, j, 1:2], in1=gt[:], op0=ALU.mult, op1=ALU.mult),
             [ck_b, st_b, gt_b], [ck_b])
    P.dma("sp", kvtm_all[slot * 8:(slot + 1) * 8].rearrange("k p r -> p k r"), ck, [ck_b], [keys_b], None)


def build_fused():
    c = Ctx("fused")
    P = c.P
    x_slots = c.din("x_slots", [4, 128, KC, T])
    out = c.dout("out", [128, KC, T])
    g_mix0 = c.din("g_mix0", [128, KC]); g_ffn0 = c.din("g_ffn0", [128, KC])
    g_mix1 = c.din("g_mix1", [128, KC]); g_ffn1 = c.din("g_ffn1", [128, KC])
    wfm0 = c.din("wfm0", [32, 128, KC * 128]); wtm0 = c.din("wtm0", [12, 128, KC * 256])
    wfm1 = c.din("wfm1", [L1_FM, 128, KC * 128]); wtm1 = c.din("wtm1", [L1_TM, 128, KC * 256])
    cosT = c.din("cosT", [4, 128, 1024]); sinT = c.din("sinT", [4, 128, 1024])
    cos_tm = c.din("cos_tm", [4, 128, 8, 128]); sin_tm = c.din("sin_tm", [4, 128, 8, 128])
    d0 = {"dmask": c.din("dmask", [128, 4, 128]), "xi": c.din("xi", [128, 4, 128]), "kz": c.din("kz", [128, 4]),
          "lng_b": c.din("lng_b", [128, 1024]), "lnb_b": c.din("lnb_b", [128, 1024]),
          "wsT": c.din("wsT", [128, 4, 128]), "bsb": c.din("bsb", [128, 4, 128])}
    wout0 = c.din("wout0", [D, D]); wout1 = c.din("wout1", [D, D])
    wg0 = c.din("wg0", [FC, 128, 2048]); wu0 = c.din("wu0", [FC, 128, 2048]); wd0 = c.din("wd0", [DFF, D])
    wg1 = c.din("wg1", [FC, 128, 2048]); wu1 = c.din("wu1", [FC, 128, 2048]); wd1 = c.din("wd1", [DFF, D])
    d1 = l1_common_inputs(c)
    d1["ngain"] = c.din("ngain", [128, 8])
    ckvg_f = c.din("ckvg_f", [128, 2]); ckvg_t = c.din("ckvg_t", [128, 256])
    zfm0 = c.dscr("zfm0", [32, 128, 1024]); ztm0 = c.dscr("ztm0", [8, 128, 3072])
    zfm1 = c.dscr("zfm1", [L1_FM, 128, 1024]); ztm1 = c.dscr("ztm1", [8, 128, L1_TM * 256])
    sstate = c.dscr("sstate", [4, 2, 128, 256]); hstate = c.dscr("hstate", [8, 128, 128])
    keys = {"kvT_all": c.dscr("kvT_all_s", [2, 128, 4096]), "kvtm_all": c.dscr("kvtm_all_s", [32, 128, 256]), "kixT_all": c.dscr("kix_all_s", [128, 4096])}
    X = c.sb("X", [128, KC, T]); X_b = P.bufs(2 * KC, "X")
    H = c.sb("H", [128, KC * T]); H_b = P.buf("H")
    WR = c.sb("WR", [128, 14336])
    FS = c.sb("FS", [128, 4096])
    Hr = H[:, :].bitcast(F32R).rearrange("p (k t) -> p k t", k=KC)
    gm0, gm0_b = load_small(c, "gm0_sb", g_mix0, [128, KC]); gf0, gf0_b = load_small(c, "gf0_sb", g_ffn0, [128, KC])
    gm1, gm1_b = load_small(c, "gm1_sb", g_mix1, [128, KC]); gf1, gf1_b = load_small(c, "gf1_sb", g_ffn1, [128, KC])
    sq = FS[:, 0:1024]; sq_b = P.bufs(2, "sq")
    rstd = FS[:, 1024:2048]; rstd_b = P.bufs(2, "rstd")
    ftmp = FS[:, 2048:3072]
    ftmp2 = [FS[:, 3072:3584], FS[:, 3584:4096]]
    pre_fm = list(range(8, 16)) + [27, 28, 29]
    pre_tm = list(range(0, 9))
    for p in range(4):
        P.barrier()
        load_X(c, X, X_b, x_slots[p])
        rmsnorm_fm(c, X, X_b, Hr, H_b, gm0, gm0_b, sq, sq_b, rstd, rstd_b)
        zfm_b, ztm_b = proj_phase(c, Hr, H_b, KC, wfm0, 32, zfm0, wtm0, 12, ztm0, WR, "p0")
        P.barrier()
        dd = dict(d0); dd.update({"cosT": cosT[p], "sinT": sinT[p], "cos_tm": cos_tm[p], "sin_tm": sin_tm[p]})
        retention_phase(c, X, X_b, H, WR, FS, zfm0, zfm_b, ztm0, ztm_b, dd, wout0, sq, sq_b, rstd, rstd_b, sstate=sstate, first=(p == 0))
        P.barrier()
        sgu_phase(c, X, X_b, H, WR, FS, zfm0, zfm_b, ztm0, ztm_b, dd, wout0, 24, 2048)
        P.barrier()
        rmsnorm_fm(c, X, X_b, Hr, H_b, gf0, gf0_b, sq, sq_b, rstd, rstd_b)
        ffn_phase(c, X, X_b, Hr, H_b, wg0, wu0, wd0, WR, ftmp, ftmp2)
        P.barrier()
        rmsnorm_fm(c, X, X_b, Hr, H_b, gm1, gm1_b, sq, sq_b, rstd, rstd_b)
        if p < 3:
            zfm_b, ztm_b = proj_phase(c, Hr, H_b, KC, wfm1, L1_FM, zfm1, wtm1, L1_TM, ztm1, WR, "p1", fm_list=pre_fm, tm_list=pre_tm)
            P.barrier()
            hgrn_phase(c, X, X_b, H, WR, FS, zfm1, zfm_b, ztm1, ztm_b, d1, None, sq, sq_b, rstd, rstd_b, outputs=False, hstate=hstate, first=(p == 0))
            P.barrier()
            dsa_prep(c, X, zfm1, zfm_b, ztm1, ztm_b, sq, sq_b, rstd, rstd_b, ckvg_f, ckvg_t, keys["kvT_all"], keys["kvtm_all"], keys["kixT_all"], 3 - p)
        else:
            zfm_b, ztm_b = proj_phase(c, Hr, H_b, KC, wfm1, L1_FM, zfm1, wtm1, L1_TM, ztm1, WR, "p1")
            P.barrier()
            hgrn_phase(c, X, X_b, H, WR, FS, zfm1, zfm_b, ztm1, ztm_b, d1, wout1, sq, sq_b, rstd, rstd_b, outputs=True, hstate=hstate, first=False)
            P.barrier()
            prep = lambda: dsa_prep(c, X, zfm1, zfm_b, ztm1, ztm_b, sq, sq_b, rstd, rstd_b, ckvg_f, ckvg_t, keys["kvT_all"], keys["kvtm_all"], keys["kixT_all"], 0)
            dsa_phase(c, X, X_b, H, WR, FS, zfm1, zfm_b, ztm1, ztm_b, wout1, sq, sq_b, rstd, rstd_b, None, keys=keys, prep=prep)
            P.barrier()
            rmsnorm_fm(c, X, X_b, Hr, H_b, gf1, gf1_b, sq, sq_b, rstd, rstd_b)
            ffn_phase(c, X, X_b, Hr, H_b, wg1, wu1, wd1, WR, ftmp, ftmp2)
            store_X(c, X, X_b, out)
    print("fused program:", {e: len(v) for e, v in P.q.items()}, "lanes", len(P.alllanes))
    return c.finish(X_b)


def kernel(x, ln_mix_g, ln_ffn_g, w_ffn_gate, w_ffn_up, w_ffn_down, rel_bias,
           ev_w_in, ev_w_out, sgu_ln_g, sgu_ln_b, sgu_w_s, sgu_b_s,
           od_w_in, od_w_out, hgrn_lb, hgrn_norm_g, dsa_cq_g, dsa_ckv_g,
           dsa_w_uq, dsa_qnorm_g, dsa_w_qidx, dsa_w_uv):
    f = lambda a: np.ascontiguousarray(np.asarray(a, dtype=np.float32))
    x = f(x)
    w_in0 = f(ev_w_in)[0]
    wfm1, wtm1 = l1_weights(f(od_w_in)[0])
    ltri, mtri, ident = l1_tables()
    lb = f(hgrn_lb)
    ckvg = f(dsa_ckv_g)[0]
    rb = f(rel_bias)
    dmask, xi, kz, _ = k2_tables(0)
    common = {
        "g_mix0": vec_fm(f(ln_mix_g)[0]), "g_ffn0": vec_fm(f(ln_ffn_g)[0]), "g_mix1": vec_fm(f(ln_mix_g)[1]), "g_ffn1": vec_fm(f(ln_ffn_g)[1]),
        "wfm0": tile_fm(np.ascontiguousarray(np.concatenate([w_in0[:, 0:2048], w_in0[:, 3072:5120]], 1))),
        "wtm0": tile_tm(np.ascontiguousarray(np.concatenate([w_in0[:, 1024:3072], w_in0[:, 5120:6144]], 1))),
        "wfm1": wfm1, "wtm1": wtm1, "dmask": dmask, "xi": xi, "kz": kz,
        "lng_b": bcast_rows(f(sgu_ln_g)[0]), "lnb_b": bcast_rows(f(sgu_ln_b)[0]),
        "wsT": np.ascontiguousarray(f(sgu_w_s)[0].transpose(2, 0, 1)), "bsb": bcast_rows(f(sgu_b_s)[0]),
        "wout0": f(ev_w_out)[0], "wout1": f(od_w_out)[0],
        "wg0": tile_fm(f(w_ffn_gate)[0]), "wu0": tile_fm(f(w_ffn_up)[0]), "wd0": f(w_ffn_down)[0],
        "wg1": tile_fm(f(w_ffn_gate)[1]), "wu1": tile_fm(f(w_ffn_up)[1]), "wd1": f(w_ffn_down)[1],
        "lb0f": vec_fm(lb[0]), "lb1f": vec_fm(lb[1]), "lb0t": bcast_rows(lb[0]), "lb1t": bcast_rows(lb[1]),
        "ltri": ltri, "mtri": mtri, "ident": ident, "ngain": vec_fm(f(hgrn_norm_g)[0]),
        "ckvg_f": vec_fm(ckvg), "ckvg_t": bcast_rows(ckvg),
        "wuq": tile_fm(f(dsa_w_uq)[0]), "wqi": tile_fm(f(dsa_w_qidx)[0]), "cqg": vec_fm(f(dsa_cq_g)[0]), "qng": vec_fm(f(dsa_qnorm_g)[0]),
        "wuv": np.ascontiguousarray(f(dsa_w_uv)[0].reshape(8, 2, 128, 128).transpose(2, 0, 1, 3)),
        "crow": bcast_rows(rb[15]), "ident2": np.eye(128, dtype=np.float32),
    }
    t = np.arange(128)
    ok = (t[None, :] // 64) <= (t[:, None] // 64)
    common["admok"] = ok.astype(np.float32)
    common["admadd"] = np.where(ok, 0.0, -BIG).astype(np.float32)
    s_ = np.arange(128)[:, None, None]; nb = np.arange(3)[None, :, None]; tt = np.arange(128)[None, None, :]
    bk = rel_bucket_np((nb - 2) * 128 + s_ - tt)
    common["bias_g"] = np.ascontiguousarray(rb[bk].transpose(0, 3, 1, 2)).astype(np.float32)
    ropes = [rope_tables(q) for q in range(4)]
    ims = []
    for c in range(NCORES):
        b, q = c // 4, c % 4
        xs = np.zeros((4, 128, KC, T), np.float32)
        slotadd = np.zeros((128, 3), np.float32); slotok = np.ones((128, 3), np.float32)
        qs = [max(q - 3 + p, 0) for p in range(4)]
        for p in range(4):
            qq = q - 3 + p
            if qq >= 0:
                xs[p] = x_to_fm(x[b, qq * T:(qq + 1) * T])
            else:
                slotadd[:, 3 - p - 1] = -BIG
                slotok[:, 3 - p - 1] = 0.0
        im = dict(common)
        im.update({"x_slots": xs, "slotadd": slotadd, "slotok": slotok,
                   "cosT": np.stack([ropes[k][0][0] for k in qs]), "sinT": np.stack([ropes[k][0][1] for k in qs]),
                   "cos_tm": np.stack([ropes[k][1][0] for k in qs]), "sin_tm": np.stack([ropes[k][1][1] for k in qs])})
        ims.append(im)
    res = _run(_prog("fused", build_fused), ims)
    out = np.zeros((2, 4096, D), np.float32)
    for c in range(NCORES):
        b, q = c // 4, c % 4
        out[b, q * T:(q + 1) * T] = fm_to_x(res[c]["out"])
    return out
```

```python
import math
from contextlib import ExitStack

import numpy as np
import concourse.bass as bass
import concourse.mybir as mybir
from concourse.bass_utils import run_bass_kernel_spmd

F32 = mybir.dt.float32
F32R = mybir.dt.float32r
ALU = mybir.AluOpType
AF = mybir.ActivationFunctionType
AX = mybir.AxisListType

NCORES = 8
T = 1024
D = 2048
KC = D // 128
DFF = 5632
FC = DFF // 128
EPS = 1e-6
SELF_SYNC = True


class Lane:
    __slots__ = ("name", "sem", "cnt", "step")

    def __init__(self, name, sem, step):
        self.name, self.sem, self.cnt, self.step = name, sem, 0, step


class Buf:
    __slots__ = ("name", "lw", "rd", "dlane")

    def __init__(self, name):
        self.name = name
        self.lw = None
        self.rd = {}
        self.dlane = None


class Prog:
    ENGS = ("pe", "act", "dve", "pool", "sp")

    def __init__(self, nc, stack, self_sync=True):
        self.nc = nc
        self.stack = stack
        self.q = {e: [] for e in self.ENGS}
        self.lanes = {}
        for e in ("pe", "act", "dve", "pool"):
            self.lanes[e] = Lane(e, stack.enter_context(nc.semaphore("s_" + e)), 1)
        self.alllanes = list(self.lanes.values())
        self.waited = {e: {} for e in self.ENGS}
        self.self_sync = self_sync
        self.nbuf = 0
        self.ndl = 0
        self.bufmap = {}
        self.shared = {}

    def buf(self, name=None):
        self.nbuf += 1
        name = name or f"b{self.nbuf}"
        if name not in self.bufmap:
            self.bufmap[name] = Buf(name)
        return self.bufmap[name]

    def bufs(self, n, name="b"):
        return [self.buf(f"{name}{i}") for i in range(n)]

    def dlane(self, b):
        if b.dlane is None:
            self.ndl += 1
            b.dlane = Lane("d_" + b.name, self.stack.enter_context(self.nc.semaphore(f"sd{self.ndl}")), 16)
            self.alllanes.append(b.dlane)
        return b.dlane

    def _deps(self, eng, reads, writes):
        deps = {}
        for b in reads:
            if b.lw is not None and deps.get(b.lw[0], 0) < b.lw[1]:
                deps[b.lw[0]] = b.lw[1]
        for b in writes:
            if b.lw is not None and deps.get(b.lw[0], 0) < b.lw[1]:
                deps[b.lw[0]] = b.lw[1]
            for ln, v in b.rd.items():
                if deps.get(ln, 0) < v:
                    deps[ln] = v
        out = []
        w = self.waited[eng]
        for ln, v in deps.items():
            if ln.name == eng and (eng == "pe" or not self.self_sync):
                continue
            if w.get(ln, 0) >= v:
                continue
            w[ln] = v
            out.append((ln, v))
        return out

    def _mark(self, lane, val, reads, writes):
        for b in reads:
            if b.rd.get(lane, 0) < val:
                b.rd[lane] = val
        for b in writes:
            b.lw = (lane, val)
            b.rd = {}

    def op(self, eng, fn, reads=(), writes=()):
        waits = self._deps(eng, reads, writes)
        lane = self.lanes[eng]
        lane.cnt += 1
        self.q[eng].append((fn, waits, lane, 1))
        self._mark(lane, lane.cnt, reads, writes)

    def dma(self, q, out_ap, in_ap, reads, writes, lane_buf=None):
        waits = self._deps(q, reads, writes)
        if lane_buf is None:
            if q not in self.shared:
                self.shared[q] = self.buf("shared_" + q)
            lane = self.dlane(self.shared[q])
            if lane.cnt > 0 and self.waited[q].get(lane, 0) < lane.cnt:
                self.waited[q][lane] = lane.cnt
                waits.append((lane, lane.cnt))
        else:
            lane = self.dlane(lane_buf)
        lane.cnt += 16

        def fn(e, out_ap=out_ap, in_ap=in_ap):
            return e.dma_start(out=out_ap, in_=in_ap)
        self.q[q].append((fn, waits, lane, 16))
        self._mark(lane, lane.cnt, reads, writes)

    def barrier(self):
        for e in self.ENGS:
            waits = []
            w = self.waited[e]
            for ln in self.alllanes:
                if ln.cnt > 0 and w.get(ln, 0) < ln.cnt and not (ln.name == e and e == "pe"):
                    w[ln] = ln.cnt
                    waits.append((ln, ln.cnt))
            if waits:
                self.q[e].append((None, waits, None, 0))

    def emit(self):
        nc = self.nc
        engmap = {"pe": "tensor", "act": "scalar", "dve": "vector", "pool": "gpsimd", "sp": "sync"}
        with nc.Block() as block:
            for e in self.ENGS:
                items = self.q[e]

                def body(engine, items=items):
                    for fn, waits, lane, step in items:
                        for ln, v in waits:
                            engine.wait_ge(ln.sem, v)
                        if fn is not None:
                            fn(engine).then_inc(lane.sem, step)
                if items:
                    getattr(block, engmap[e])(body)


class Ctx:
    def __init__(self, name):
        self.nc = bass.Bass("TRN2", target_bir_lowering=False)
        self.st = ExitStack()
        self.P = Prog(self.nc, self.st, self_sync=SELF_SYNC)
        self.name = name
        self.nt = 0
        nc, P = self.nc, self.P
        self.ps = [self.st.enter_context(nc.psum_tensor(f"ps{i}", [128, 512], F32)) for i in range(8)]
        self.ps_b = P.bufs(8, "ps")
        self.ones = self.sb("ones", [128, 128])
        self.ones_b = P.buf("ones")
        P.op("dve", lambda e: e.memset(self.ones[:], 1.0), [], [self.ones_b])
        self.eps = self.sb("epsc", [128, 1])
        self.eps_b = P.buf("eps")
        P.op("dve", lambda e: e.memset(self.eps[:], EPS), [], [self.eps_b])
        self.psrot = 0

    def sb(self, name, shape, dt=F32):
        if not hasattr(self, "_sbmap"):
            self._sbmap = {}
        if name not in self._sbmap:
            self._sbmap[name] = self.st.enter_context(self.nc.sbuf_tensor(name, shape, dt))
        return self._sbmap[name]

    def din(self, name, shape, dt=F32):
        return self.nc.dram_tensor(name, list(shape), dt, kind="ExternalInput").ap()

    def dout(self, name, shape, dt=F32):
        return self.nc.dram_tensor(name, list(shape), dt, kind="ExternalOutput").ap()

    def dscr(self, name, shape, dt=F32):
        return self.nc.dram_tensor(name, list(shape), dt, kind="Internal").ap()

    def finish(self, final_bufs):
        P = self.P
        waits = P._deps("sp", final_bufs, final_bufs)
        P.q["sp"].append((None, waits, None, 0))
        P.emit()
        self.st.close()
        return self.nc

    def nextps(self, lo=4, n=4):
        i = lo + self.psrot % n
        self.psrot += 1
        return i


def load_X(c, X, X_b, xT_d, q="sp"):
    P = c.P
    for g in range(4):
        P.dma(q, X[:, 4 * g:4 * g + 4, :], xT_d[:, 4 * g:4 * g + 4, :], [], X_b[8 * g:8 * g + 8], X_b[8 * g])


def store_X(c, X, X_b, out_d, q="sp"):
    P = c.P
    for g in range(4):
        P.dma(q, out_d[:, 4 * g:4 * g + 4, :], X[:, 4 * g:4 * g + 4, :], X_b[8 * g:8 * g + 8], [], X_b[8 * g])


def rmsnorm_fm(c, X, X_b, Hr, H_b, g_sb, g_b, sq, sq_b, rstd, rstd_b):
    P = c.P
    for half in range(2):
        hs = slice(half * 512, (half + 1) * 512)
        pb = half
        for k in range(KC):
            sl = (half * KC + k) % 2
            ss = slice(sl * 512, (sl + 1) * 512)
            P.op("act", lambda e, k=k, hs=hs, ss=ss: e.activation(out=sq[:, ss], in_=X[:, k, hs], func=AF.Square),
                 [X_b[2 * k + half]], [sq_b[sl]])
            P.op("pe", lambda e, k=k, ss=ss, pb=pb: e.matmul(c.ps[pb][:], c.ones[:], sq[:, ss], start=(k == 0), stop=(k == KC - 1)),
                 [c.ones_b, sq_b[sl]], [c.ps_b[pb]])
        P.op("act", lambda e, hs=hs, pb=pb: e.activation(out=rstd[:, hs], in_=c.ps[pb][:], func=AF.Sqrt, bias=c.eps[:, 0:1], scale=1.0 / D),
             [c.ps_b[pb], c.eps_b], [rstd_b[half]])
        P.op("dve", lambda e, hs=hs: e.reciprocal(out=rstd[:, hs], in_=rstd[:, hs]), [rstd_b[half]], [rstd_b[half]])
        for k in range(KC):
            P.op("dve", lambda e, k=k, hs=hs: e.scalar_tensor_tensor(out=Hr[:, k, hs], in0=X[:, k, hs], scalar=g_sb[:, k:k + 1], in1=rstd[:, hs],
                                                                     op0=ALU.mult, op1=ALU.mult),
                 [X_b[2 * k + half], g_b, rstd_b[half]], [H_b])


def ffn_phase(c, X, X_b, Hr, H_b, wg_d, wu_d, wd_d, WR, tmp=None, tmp2=None):
    P = c.P
    wgu = [(WR[:, s * 4096:s * 4096 + 2048].bitcast(F32R), WR[:, s * 4096 + 2048:(s + 1) * 4096].bitcast(F32R)) for s in range(2)]
    wgu_b = [(P.buf(f"wg{s}"), P.buf(f"wu{s}")) for s in range(2)]
    wd = [WR[:, 8192 + s * 2048:8192 + (s + 1) * 2048].bitcast(F32R) for s in range(2)]
    wd_b = P.bufs(2, "wd")
    act = [WR[:, 12288 + s * 1024:12288 + (s + 1) * 1024] for s in range(2)]
    act_b = P.bufs(2, "actt")
    if tmp is None:
        tmp = c.sb("ffn_tmp", [128, 1024])
    tmp_b = P.bufs(2, "ffntmp")
    tmp2_b = P.bufs(2, "ffntmp2")

    def issue_loads(f):
        s = f % 2
        P.dma("pool", wgu[s][0], wg_d[f], [], [wgu_b[s][0]], wgu_b[s][0])
        P.dma("pool", wgu[s][1], wu_d[f], [], [wgu_b[s][1]], wgu_b[s][1])
        P.dma("pool", wd[s], wd_d[f * 128:(f + 1) * 128, :], [], [wd_b[s]], wd_b[s])

    issue_loads(0)
    for f in range(FC):
        s = f % 2
        if f + 1 < FC:
            issue_loads(f + 1)
        wg3 = wgu[s][0].rearrange("p (k j) -> p k j", k=KC)
        wu3 = wgu[s][1].rearrange("p (k j) -> p k j", k=KC)
        for half in range(2):
            hs = slice(half * 512, (half + 1) * 512)
            for (w3, wb, pb) in ((wg3, wgu_b[s][0], half), (wu3, wgu_b[s][1], 2 + half)):
                for k in range(KC):
                    P.op("pe", lambda e, w3=w3, k=k, hs=hs, pb=pb: e.matmul(c.ps[pb][:], w3[:, k, :], Hr[:, k, hs], start=(k == 0), stop=(k == KC - 1)),
                         [wb, H_b], [c.ps_b[pb]])
        for half in range(2):
            hs = slice(half * 512, (half + 1) * 512)
            P.op("act", lambda e, hs=hs, half=half: e.activation(out=tmp[:, hs], in_=c.ps[half][:], func=AF.Silu),
                 [c.ps_b[half]], [tmp_b[half]])
            P.op("dve", lambda e, hs=hs, half=half, s=s: e.tensor_tensor(out=act[s][:, hs].bitcast(F32R), in0=tmp[:, hs], in1=c.ps[2 + half][:], op=ALU.mult),
                 [tmp_b[half], c.ps_b[2 + half]], [act_b[s]])
        for n in range(KC):
            for half in range(2):
                hs = slice(half * 512, (half + 1) * 512)
                pb = c.nextps()
                P.op("pe", lambda e, n=n, hs=hs, pb=pb, s=s: e.matmul(c.ps[pb][:], wd[s][:, n * 128:(n + 1) * 128], act[s][:, hs].bitcast(F32R), start=True, stop=True),
                     [wd_b[s], act_b[s]], [c.ps_b[pb]])
                idx = 2 * n + half
                if tmp2 is not None and idx % 3 == 2:
                    s2 = (idx // 3) % 2
                    P.op("act", lambda e, pb=pb, s2=s2: e.copy(out=tmp2[s2], in_=c.ps[pb][:]), [c.ps_b[pb]], [tmp2_b[s2]])
                    P.op("pool", lambda e, n=n, hs=hs, s2=s2: e.tensor_tensor(out=X[:, n, hs], in0=X[:, n, hs], in1=tmp2[s2], op=ALU.add),
                         [X_b[2 * n + half], tmp2_b[s2]], [X_b[2 * n + half]])
                else:
                    P.op("dve", lambda e, n=n, hs=hs, pb=pb: e.tensor_tensor(out=X[:, n, hs], in0=X[:, n, hs], in1=c.ps[pb][:], op=ALU.add),
                         [X_b[2 * n + half], c.ps_b[pb]], [X_b[2 * n + half]])


def load_small(c, name, dram_ap, shape, q="sp", dt=F32):
    t = c.sb(name, shape, dt)
    b = c.P.buf(name)
    c.P.dma(q, t[:], dram_ap, [], [b], None)
    return t, b


def build_ffn_test():
    c = Ctx("ffn")
    P = c.P
    xT = c.din("xT", [128, KC, T])
    g = c.din("g", [128, KC])
    wg = c.din("wg", [FC, 128, 2048])
    wu = c.din("wu", [FC, 128, 2048])
    wd = c.din("wd", [DFF, D])
    out = c.dout("out", [128, KC, T])
    X = c.sb("X", [128, KC, T]); X_b = P.bufs(2 * KC, "X")
    H = c.sb("H", [128, KC * T]); H_b = P.buf("H")
    WR = c.sb("WR", [128, 14336])
    Hr = H[:, :].bitcast(F32R).rearrange("p (k t) -> p k t", k=KC)
    g_sb, g_b = load_small(c, "g_sb", g, [128, KC])
    sq = c.sb("sq", [128, T]); sq_b = P.bufs(2, "sq")
    rstd = c.sb("rstd", [128, T]); rstd_b = P.bufs(2, "rstd")
    load_X(c, X, X_b, xT)
    rmsnorm_fm(c, X, X_b, Hr, H_b, g_sb, g_b, sq, sq_b, rstd, rstd_b)
    t2 = c.sb("ffn_tmp2", [128, 1024])
    ffn_phase(c, X, X_b, Hr, H_b, wg, wu, wd, WR, None, [t2[:, 0:512], t2[:, 512:1024]])
    store_X(c, X, X_b, out)
    return c.finish(X_b)


def tile_fm(W):
    K, N = W.shape
    return np.ascontiguousarray(W.reshape(K // 128, 128, N // 128, 128).transpose(2, 1, 0, 3)).reshape(N // 128, 128, (K // 128) * 128)


def tile_tm(W, cw=256):
    K, N = W.shape
    return np.ascontiguousarray(W.reshape(K // 128, 128, N // cw, cw).transpose(2, 1, 0, 3)).reshape(N // cw, 128, (K // 128) * cw)


def x_to_fm(xs):
    return np.ascontiguousarray(xs.T.reshape(KC, 128, xs.shape[0]).transpose(1, 0, 2))


def fm_to_x(o):
    return np.ascontiguousarray(o.transpose(1, 0, 2).reshape(D, o.shape[2]).T)


def vec_fm(g):
    return np.ascontiguousarray(g.reshape(-1, 128).T)


def proj_phase(c, Hr, H_b, kc, wfm_d, n_fm, zfm_d, wtm_d, n_tm, ztm_d, WR, tag, fm_list=None, tm_list=None):
    P = c.P
    fm_list = list(range(n_fm)) if fm_list is None else fm_list
    tm_list = list(range(n_tm)) if tm_list is None else tm_list
    zfm_b = P.buf(tag + "zfm")
    ztm_b = P.buf(tag + "ztm")
    wf = [WR[:, s * 2048:s * 2048 + kc * 128].bitcast(F32R).rearrange("p (k j) -> p k j", k=kc) for s in range(2)]
    wf_b = P.bufs(2, "pjwf")
    wt = [WR[:, 4096 + s * 4096:4096 + s * 4096 + kc * 256].bitcast(F32R).rearrange("p (k j) -> p k j", k=kc) for s in range(2)]
    wt_b = P.bufs(2, "pjwt")
    stf = [WR[:, 12288 + s * 1024:12288 + (s + 1) * 1024] for s in range(2)]
    stf_b = P.bufs(2, "pjstf")
    stt_t = c.sb("pjstt", [128, 2, 256])
    stt_b = P.bufs(2, "pjstt")
    if fm_list:
        P.dma("pool", wf[0], wfm_d[fm_list[0]].rearrange("p (k j) -> p k j", k=kc), [], [wf_b[0]], wf_b[0])
    for ni, n in enumerate(fm_list):
        s = ni % 2
        if ni + 1 < len(fm_list):
            P.dma("pool", wf[1 - s], wfm_d[fm_list[ni + 1]].rearrange("p (k j) -> p k j", k=kc), [], [wf_b[1 - s]], wf_b[1 - s])
        for half in range(2):
            hs = slice(half * 512, (half + 1) * 512)
            pb = c.nextps(0, 8)
            for k in range(kc):
                P.op("pe", lambda e, s=s, k=k, hs=hs, pb=pb: e.matmul(c.ps[pb][:], wf[s][:, k, :], Hr[:, k, hs], start=(k == 0), stop=(k == kc - 1)),
                     [wf_b[s], H_b], [c.ps_b[pb]])
            P.op("act", lambda e, s=s, hs=hs, pb=pb: e.copy(out=stf[s][:, hs].bitcast(F32R), in_=c.ps[pb][:]), [c.ps_b[pb]], [stf_b[s]])
        P.dma("sp", zfm_d[n], stf[s], [stf_b[s]], [zfm_b], stf_b[s])
    if tm_list:
        P.dma("pool", wt[0], wtm_d[tm_list[0]].rearrange("p (k j) -> p k j", k=kc), [], [wt_b[0]], wt_b[0])
    cnt = 0
    for gi, g in enumerate(tm_list):
        s = gi % 2
        if gi + 1 < len(tm_list):
            P.dma("pool", wt[1 - s], wtm_d[tm_list[gi + 1]].rearrange("p (k j) -> p k j", k=kc), [], [wt_b[1 - s]], wt_b[1 - s])
        for j in range(8):
            pb = c.nextps(0, 8)
            for k in range(kc):
                P.op("pe", lambda e, s=s, k=k, j=j, pb=pb: e.matmul(c.ps[pb][:, 0:256], Hr[:, k, j * 128:(j + 1) * 128], wt[s][:, k, :], start=(k == 0), stop=(k == kc - 1)),
                     [wt_b[s], H_b], [c.ps_b[pb]])
            ss = cnt % 2
            cnt += 1
            P.op("dve", lambda e, ss=ss, pb=pb: e.tensor_copy(out=stt_t[:, ss, :], in_=c.ps[pb][:, 0:256]), [c.ps_b[pb]], [stt_b[ss]])
            P.dma("sp", ztm_d[j, :, g * 256:(g + 1) * 256], stt_t[:, ss, :], [stt_b[ss]], [ztm_b], stt_b[ss])
    return zfm_b, ztm_b


def rotary_tm(c, raw, raw_b, outk, outk_b, cos_tm, sin_tm, tab_bs, t3, t2, tmp_b):
    P = c.P
    A, B = raw[:, :, 0:128], raw[:, :, 128:256]
    P.op("dve", lambda e: e.tensor_tensor(out=t3, in0=A, in1=cos_tm, op=ALU.mult), [raw_b] + tab_bs, [tmp_b[0]])
    P.op("dve", lambda e: e.tensor_tensor(out=t2, in0=B, in1=sin_tm, op=ALU.mult), [raw_b] + tab_bs, [tmp_b[1]])
    P.op("dve", lambda e: e.tensor_tensor(out=outk[:, :, 0:128], in0=t3, in1=t2, op=ALU.subtract), [tmp_b[0], tmp_b[1]], [outk_b])
    P.op("dve", lambda e: e.tensor_tensor(out=t3, in0=A, in1=sin_tm, op=ALU.mult), [raw_b] + tab_bs, [tmp_b[0]])
    P.op("dve", lambda e: e.tensor_tensor(out=t2, in0=B, in1=cos_tm, op=ALU.mult), [raw_b] + tab_bs, [tmp_b[1]])
    P.op("dve", lambda e: e.tensor_tensor(out=outk[:, :, 128:256], in0=t3, in1=t2, op=ALU.add), [tmp_b[0], tmp_b[1]], [outk_b])


RET_GAMMA = [1.0 - 2.0 ** (-5.0 - h) for h in range(4)]


def build_k1():
    c = Ctx("k1")
    P = c.P
    xT = c.din("xT", [128, KC, T])
    g = c.din("g", [128, KC])
    wtm = c.din("wtm", [8, 128, KC * 256])
    cos_d = c.din("cos_tm", [128, 8, 128])
    sin_d = c.din("sin_tm", [128, 8, 128])
    kz1_d = c.din("kz1", [128, 32])
    sloc = c.dout("sloc", [4, 2, 128, 256])
    ztm = c.dscr("ztm1", [8, 128, 2048])
    X = c.sb("X", [128, KC, T]); X_b = P.bufs(2 * KC, "X")
    H = c.sb("H", [128, KC * T]); H_b = P.buf("H")
    WR = c.sb("WR", [128, 14336])
    Hr = H[:, :].bitcast(F32R).rearrange("p (k t) -> p k t", k=KC)
    g_sb, g_b = load_small(c, "g_sb", g, [128, KC])
    cos_tm, cb = load_small(c, "cos_sb", cos_d, [128, 8, 128])
    sin_tm, sbb = load_small(c, "sin_sb", sin_d, [128, 8, 128])
    kz1, kz1_b = load_small(c, "kz1_sb", kz1_d, [128, 32])
    tab_b = P.buf("tabs")
    sq = c.sb("sq", [128, T]); sq_b = P.bufs(2, "sq")
    rstd = c.sb("rstd", [128, T]); rstd_b = P.bufs(2, "rstd")
    load_X(c, X, X_b, xT)
    rmsnorm_fm(c, X, X_b, Hr, H_b, g_sb, g_b, sq, sq_b, rstd, rstd_b)
    _, ztm_b = proj_phase(c, Hr, H_b, KC, None, 0, None, wtm, 8, ztm, WR, "k1")
    P.barrier()
    Xf = X[:, :, :].rearrange("p k t -> p (k t)")
    raw = Xf[:, 0:2048].rearrange("p (j c) -> p j c", j=8); raw_b = P.buf("raw")
    t3 = Xf[:, 6144:7168].rearrange("p (j c) -> p j c", j=8)
    t2 = Xf[:, 7168:8192].rearrange("p (j c) -> p j c", j=8)
    tmp_b = P.bufs(2, "rt")
    so = Xf[:, 8192:8704].rearrange("p (s c) -> p s c", s=2); so_b = P.bufs(2, "so")
    ktr = H[:, 0:2048].bitcast(F32R).rearrange("p (j c) -> p j c", j=8); kt_b = P.buf("kt")
    vt = H[:, 2048:4096].bitcast(F32R).rearrange("p (j c) -> p j c", j=8); vt_b = P.buf("vt")
    for h in range(4):
        P.dma("sp", raw, ztm[:, :, h * 256:(h + 1) * 256].rearrange("j p c -> p j c"), [ztm_b], [raw_b], raw_b)
        P.dma("pool", vt, ztm[:, :, 1024 + h * 256:1024 + (h + 1) * 256].rearrange("j p c -> p j c"), [ztm_b], [vt_b], vt_b)
        for j in range(8):
            P.op("dve", lambda e, j=j, h=h: e.tensor_scalar(out=raw[:, j, :], in0=raw[:, j, :], scalar1=kz1[:, j * 4 + h:j * 4 + h + 1], scalar2=None, op0=ALU.mult),
                 [raw_b, kz1_b], [raw_b])
        rotary_tm(c, raw, raw_b, ktr, kt_b, cos_tm[:], sin_tm[:], [cb, sbb], t3, t2, tmp_b)
        for dc in range(2):
            pb = c.nextps(0, 8)
            for j in range(8):
                P.op("pe", lambda e, j=j, dc=dc, pb=pb: e.matmul(c.ps[pb][:, 0:256], ktr[:, j, dc * 128:(dc + 1) * 128], vt[:, j, :], start=(j == 0), stop=(j == 7)),
                     [kt_b, vt_b], [c.ps_b[pb]])
            P.op("act", lambda e, dc=dc, pb=pb: e.copy(out=so[:, dc, :], in_=c.ps[pb][:, 0:256]), [c.ps_b[pb]], [so_b[dc]])
            P.dma("sp", sloc[h, dc], so[:, dc, :], [so_b[dc]], [], so_b[dc])
    return c.finish(so_b)


def rope_tables(qtr):
    pos = (np.arange(T, dtype=np.float32) + np.float32(qtr * T))
    inv = (np.float32(10000.0) ** (-(np.arange(0, 256, 2, dtype=np.float32) / np.float32(256)))).astype(np.float32)
    ang = (pos[:, None] * inv[None, :]).astype(np.float32)
    cos, sin = np.cos(ang).astype(np.float32), np.sin(ang).astype(np.float32)
    fm = (np.ascontiguousarray(cos.T), np.ascontiguousarray(sin.T))
    tm = (np.ascontiguousarray(cos.reshape(8, 128, 128).transpose(1, 0, 2)), np.ascontiguousarray(sin.reshape(8, 128, 128).transpose(1, 0, 2)))
    return fm, tm


def k1_tables():
    kz1 = np.zeros((128, 8, 4), np.float64)
    tb = np.arange(128)
    for j in range(8):
        for h in range(4):
            kz1[:, j, h] = RET_GAMMA[h] ** (1023 - (128 * j + tb)) / 16.0
    return kz1.reshape(128, 32).astype(np.float32)


def R_(ap):
    return ap.bitcast(F32R)


def outproj_partial(c, X, X_b, mixedT, mixed_b, wout_d, rows, wo, wo_b):
    P = c.P
    for mi, r in enumerate(rows):
        P.dma("pool", wo[mi], wout_d[r * 128:(r + 1) * 128, :], [], [wo_b[mi]], wo_b[mi])
    for n in range(KC):
        for half in range(2):
            hs = slice(half * 512, (half + 1) * 512)
            pb = c.nextps(0, 8)
            for mi in range(len(rows)):
                P.op("pe", lambda e, mi=mi, n=n, hs=hs, pb=pb: e.matmul(c.ps[pb][:], wo[mi][:, n * 128:(n + 1) * 128], mixedT[:, mi, hs],
                                                                        start=(mi == 0), stop=(mi == len(rows) - 1)),
                     [wo_b[mi], mixed_b], [c.ps_b[pb]])
            P.op("dve", lambda e, n=n, hs=hs, pb=pb: e.tensor_tensor(out=X[:, n, hs], in0=X[:, n, hs], in1=c.ps[pb][:], op=ALU.add),
                 [X_b[2 * n + half], c.ps_b[pb]], [X_b[2 * n + half]])


def groupnorm_gate(c, src, src_b, nchunk, width, gate, gate_b, gain, gain_b, outT, out_b, sq, sq_b, rstd, rstd_b):
    P = c.P
    for half in range(2):
        hs = slice(half * 512, (half + 1) * 512)
        pb = c.nextps(0, 8)
        for i in range(nchunk):
            sl = i % 2
            ss = slice(sl * 512, (sl + 1) * 512)
            P.op("act", lambda e, i=i, hs=hs, ss=ss: e.activation(out=sq[:, ss], in_=src[:, i, hs], func=AF.Square), [src_b], [sq_b[sl]])
            P.op("pe", lambda e, i=i, ss=ss, pb=pb: e.matmul(c.ps[pb][:], c.ones[:], sq[:, ss], start=(i == 0), stop=(i == nchunk - 1)),
                 [c.ones_b, sq_b[sl]], [c.ps_b[pb]])
        P.op("act", lambda e, hs=hs, pb=pb: e.activation(out=rstd[:, hs], in_=c.ps[pb][:], func=AF.Sqrt, bias=c.eps[:, 0:1], scale=1.0 / width),
             [c.ps_b[pb], c.eps_b], [rstd_b[half]])
        P.op("dve", lambda e, hs=hs: e.reciprocal(out=rstd[:, hs], in_=rstd[:, hs]), [rstd_b[half]], [rstd_b[half]])
    for i in range(nchunk):
        P.op("act", lambda e, i=i: e.activation(out=R_(gate[:, i, :]), in_=gate[:, i, :], func=AF.Silu), [gate_b], [gate_b])
    for i in range(nchunk):
        if gain is None:
            P.op("dve", lambda e, i=i: e.tensor_tensor(out=R_(outT[:, i, :]), in0=src[:, i, :], in1=rstd[:, :], op=ALU.mult),
                 [src_b, rstd_b[0], rstd_b[1]], [out_b])
        else:
            P.op("dve", lambda e, i=i: e.scalar_tensor_tensor(out=R_(outT[:, i, :]), in0=src[:, i, :], scalar=gain[:, i:i + 1], in1=rstd[:, :], op0=ALU.mult, op1=ALU.mult),
                 [src_b, rstd_b[0], rstd_b[1], gain_b], [out_b])
        P.op("dve", lambda e, i=i: e.tensor_tensor(out=R_(outT[:, i, :]), in0=outT[:, i, :], in1=gate[:, i, :], op=ALU.mult),
             [out_b, gate_b], [out_b])


def retention_phase(c, X, X_b, H, WR, FS, zfm, zfm_b, ztm, ztm_b, d, wout_d, sq, sq_b, rstd, rstd_b, sstate=None, first=False):
    P = c.P
    qT = H[:, 0:2048].rearrange("p (a t) -> p a t", a=2); qT_b = P.buf("qT")
    kT = H[:, 2048:4096].rearrange("p (a t) -> p a t", a=2); kT_b = P.buf("kT")
    qs = H[:, 4096:6144].rearrange("p (a t) -> p a t", a=2); qs_b = P.buf("qs")
    ktm = H[:, 6144:8192].rearrange("p (j c) -> p j c", j=8); ktm_b = P.buf("ktm")
    vtm = H[:, 8192:10240].rearrange("p (j c) -> p j c", j=8); vtm_b = P.buf("vtm")
    S = [H[:, 10240 + s * 512:10240 + (s + 1) * 512].rearrange("p (a e) -> p a e", a=2) for s in range(2)] + [H[:, 15616:16128].rearrange("p (a e) -> p a e", a=2)]
    S_b = P.bufs(3, "S")
    PT = [H[:, 11264 + s * 128:11264 + (s + 1) * 128] for s in range(2)]; PT_b = P.bufs(2, "PT")
    mixedT = H[:, 11520:13568].rearrange("p (a t) -> p a t", a=2); mixed_b = P.buf("mixedT")
    retT = H[:, 13568:15616].rearrange("p (a t) -> p a t", a=2); retT_b = P.buf("retT")
    tS = [H[:, 15616 + s * 256:15616 + (s + 1) * 256] for s in range(2)]; tS_b = P.bufs(2, "tS")
    cosT = WR[:, 0:1024]; sinT = WR[:, 1024:2048]
    cos_tm = WR[:, 2048:3072].rearrange("p (j i) -> p j i", j=8); sin_tm = WR[:, 3072:4096].rearrange("p (j i) -> p j i", j=8)
    dmask = WR[:, 4096:4608].rearrange("p (h i) -> p h i", h=4); xi = WR[:, 4608:5120].rearrange("p (h i) -> p h i", h=4)
    tab_b = P.bufs(6, "rtab")
    for i, (dst, key) in enumerate(((cosT, "cosT"), (sinT, "sinT"), (cos_tm, "cos_tm"), (sin_tm, "sin_tm"), (dmask, "dmask"), (xi, "xi"))):
        P.dma("pool", R_(dst), d[key], [], [tab_b[i]], None)
    sst_b = P.buf("sstate")
    rawA = WR[:, 5120:6144]; rawB = WR[:, 6144:7168]; raw_b = P.buf("rraw")
    rawtm = WR[:, 5120:7168].rearrange("p (j c) -> p j c", j=8)
    t3 = WR[:, 7168:8192]; t2 = WR[:, 8192:9216]; tmp_b = P.bufs(2, "rtmp")
    wo = [R_(WR[:, 9216 + s * 2048:9216 + (s + 1) * 2048]) for s in range(2)]; wo_b = P.bufs(2, "wo")
    gT = FS[:, 2048:4096].rearrange("p (a t) -> p a t", a=2); gT_b = P.buf("gT")
    kz, kz_b = load_small(c, "kz_sb", d["kz"], [128, 4])
    if sstate is None:
        sw, sw_b = load_small(c, "sw_sb", d["sw"], [128, 12])

    def rot_fm(chunk0, out3, out_b):
        P.dma("pool", R_(rawA), zfm[chunk0], [zfm_b], [raw_b], raw_b)
        P.dma("pool", R_(rawB), zfm[chunk0 + 1], [zfm_b], [raw_b], raw_b)
        tb = [tab_b[0], tab_b[1]]
        P.op("dve", lambda e: e.tensor_tensor(out=R_(t3), in0=rawA, in1=cosT, op=ALU.mult), [raw_b] + tb, [tmp_b[0]])
        P.op("dve", lambda e: e.tensor_tensor(out=R_(t2), in0=rawB, in1=sinT, op=ALU.mult), [raw_b] + tb, [tmp_b[1]])
        P.op("dve", lambda e: e.tensor_tensor(out=R_(out3[:, 0, :]), in0=t3, in1=t2, op=ALU.subtract), [tmp_b[0], tmp_b[1]], [out_b])
        P.op("dve", lambda e: e.tensor_tensor(out=R_(t3), in0=rawA, in1=sinT, op=ALU.mult), [raw_b] + tb, [tmp_b[0]])
        P.op("dve", lambda e: e.tensor_tensor(out=R_(t2), in0=rawB, in1=cosT, op=ALU.mult), [raw_b] + tb, [tmp_b[1]])
        P.op("dve", lambda e: e.tensor_tensor(out=R_(out3[:, 1, :]), in0=t3, in1=t2, op=ALU.add), [tmp_b[0], tmp_b[1]], [out_b])

    for h in range(4):
        g128 = float(RET_GAMMA[h] ** 128)
        rot_fm(2 * h, qT, qT_b)
        for a in range(2):
            P.op("dve", lambda e, a=a, h=h: e.tensor_tensor(out=R_(qs[:, a, :]).rearrange("p (j i) -> p j i", j=8), in0=qT[:, a, :].rearrange("p (j i) -> p j i", j=8),
                                                          in1=xi[:, h:h + 1, :].to_broadcast([128, 8, 128]), op=ALU.mult), [qT_b, tab_b[5]], [qs_b])
        rot_fm(8 + 2 * h, kT, kT_b)
        P.dma("pool", R_(rawtm), ztm[:, :, h * 256:(h + 1) * 256].rearrange("j p c -> p j c"), [ztm_b], [raw_b], raw_b)
        P.dma("pool", R_(vtm), ztm[:, :, 1024 + h * 256:1024 + (h + 1) * 256].rearrange("j p c -> p j c"), [ztm_b], [vtm_b], vtm_b)
        for a in range(2):
            P.dma("pool", R_(gT[:, a, :]), zfm[16 + 2 * h + a], [zfm_b], [gT_b], gT_b)
        P.op("dve", lambda e, h=h: e.tensor_scalar(out=R_(rawtm), in0=rawtm, scalar1=kz[:, h:h + 1], scalar2=None, op0=ALU.mult), [raw_b, kz_b], [raw_b])
        t3j = t3.rearrange("p (j i) -> p j i", j=8); t2j = t2.rearrange("p (j i) -> p j i", j=8)
        tb = [tab_b[2], tab_b[3]]
        A, B = rawtm[:, :, 0:128], rawtm[:, :, 128:256]
        P.op("dve", lambda e: e.tensor_tensor(out=R_(t3j), in0=A, in1=cos_tm, op=ALU.mult), [raw_b] + tb, [tmp_b[0]])
        P.op("dve", lambda e: e.tensor_tensor(out=R_(t2j), in0=B, in1=sin_tm, op=ALU.mult), [raw_b] + tb, [tmp_b[1]])
        P.op("dve", lambda e: e.tensor_tensor(out=R_(ktm[:, :, 0:128]), in0=t3j, in1=t2j, op=ALU.subtract), [tmp_b[0], tmp_b[1]], [ktm_b])
        P.op("dve", lambda e: e.tensor_tensor(out=R_(t3j), in0=A, in1=sin_tm, op=ALU.mult), [raw_b] + tb, [tmp_b[0]])
        P.op("dve", lambda e: e.tensor_tensor(out=R_(t2j), in0=B, in1=cos_tm, op=ALU.mult), [raw_b] + tb, [tmp_b[1]])
        P.op("dve", lambda e: e.tensor_tensor(out=R_(ktm[:, :, 128:256]), in0=t3j, in1=t2j, op=ALU.add), [tmp_b[0], tmp_b[1]], [ktm_b])
        if sstate is not None:
            for dc in range(2):
                if first:
                    P.op("dve", lambda e, dc=dc: e.tensor_scalar(out=R_(S[0][:, dc, :]), in0=dmask[:, 0:2, :].rearrange("p a i -> p (a i)"), scalar1=0.0, scalar2=None, op0=ALU.mult),
                         [tab_b[4]], [S_b[0]])
                else:
                    P.dma("pool", R_(S[0][:, dc, :]), sstate[h, dc], [sst_b], [S_b[0]], S_b[0])
        else:
            for dc in range(2):
                for i in range(3):
                    sl = (dc * 3 + i) % 2
                    P.dma("pool", R_(tS[sl]), d["sprev"][i, h, dc], [], [tS_b[sl]], tS_b[sl])
                    col = sw[:, i * 4 + h:i * 4 + h + 1]
                    if i == 0:
                        P.op("dve", lambda e, sl=sl, dc=dc, col=col: e.tensor_scalar(out=R_(S[0][:, dc, :]), in0=tS[sl], scalar1=col, scalar2=None, op0=ALU.mult),
                             [tS_b[sl], sw_b], [S_b[0]])
                    else:
                        P.op("dve", lambda e, sl=sl, dc=dc, col=col: e.scalar_tensor_tensor(out=R_(S[0][:, dc, :]), in0=tS[sl], scalar=col, in1=S[0][:, dc, :], op0=ALU.mult, op1=ALU.add),
                             [tS_b[sl], sw_b, S_b[0]], [S_b[0]])
        def stage1(j, h=h, g128=g128):
            js = slice(j * 128, (j + 1) * 128)
            pb = c.nextps(0, 8)
            for a in range(2):
                P.op("pe", lambda e, a=a, js=js, pb=pb: e.matmul(c.ps[pb][:, 0:128], R_(kT[:, a, js]), R_(qT[:, a, js]), start=(a == 0), stop=(a == 1)),
                     [kT_b, qT_b], [c.ps_b[pb]])
            P.op("dve", lambda e, pb=pb, j=j, h=h: e.tensor_tensor(out=R_(PT[j % 2]), in0=c.ps[pb][:, 0:128], in1=dmask[:, h, :], op=ALU.mult),
                 [c.ps_b[pb], tab_b[4]], [PT_b[j % 2]])
            if j < 7 or sstate is not None:
                cur, nxt = j % 3, (j + 1) % 3
                for a in range(2):
                    pb3 = c.nextps(0, 8)
                    P.op("pe", lambda e, a=a, j=j, pb3=pb3: e.matmul(c.ps[pb3][:, 0:256], R_(ktm[:, j, a * 128:(a + 1) * 128]), R_(vtm[:, j, :]), start=True, stop=True),
                         [ktm_b, vtm_b], [c.ps_b[pb3]])
                    P.op("dve", lambda e, a=a, pb3=pb3, cur=cur, nxt=nxt, g128=g128: e.scalar_tensor_tensor(out=R_(S[nxt][:, a, :]), in0=S[cur][:, a, :], scalar=g128, in1=c.ps[pb3][:, 0:256],
                                                                                                      op0=ALU.mult, op1=ALU.add),
                         [S_b[cur], c.ps_b[pb3]], [S_b[nxt]])

        def stage2(j):
            js = slice(j * 128, (j + 1) * 128)
            cur = j % 3
            for ec in range(2):
                es = slice(ec * 128, (ec + 1) * 128)
                pb2 = c.nextps(0, 8)
                P.op("pe", lambda e, j=j, es=es, pb2=pb2: e.matmul(c.ps[pb2][:, 0:128], R_(vtm[:, j, es]), R_(PT[j % 2]), start=True, stop=False),
                     [vtm_b, PT_b[j % 2]], [c.ps_b[pb2]])
                for a in range(2):
                    P.op("pe", lambda e, a=a, es=es, js=js, pb2=pb2, cur=cur: e.matmul(c.ps[pb2][:, 0:128], R_(S[cur][:, a, es]), R_(qs[:, a, js]), start=False, stop=(a == 1)),
                         [S_b[cur], qs_b], [c.ps_b[pb2]])
                P.op("act", lambda e, ec=ec, js=js, pb2=pb2: e.copy(out=R_(retT[:, ec, js]), in_=c.ps[pb2][:, 0:128]), [c.ps_b[pb2]], [retT_b])

        stage1(0)
        for j in range(8):
            if j + 1 < 8:
                stage1(j + 1)
            stage2(j)
        if sstate is not None:
            for dc in range(2):
                P.dma("sp", sstate[h, dc], S[2][:, dc, :], [S_b[2]], [sst_b], S_b[2])
        groupnorm_gate(c, retT, retT_b, 2, 256, gT, gT_b, None, None, mixedT, mixed_b, sq, sq_b, rstd, rstd_b)
        outproj_partial(c, X, X_b, R_(mixedT), mixed_b, wout_d, [2 * h, 2 * h + 1], wo, wo_b)


def sgu_phase(c, X, X_b, H, WR, FS, zfm, zfm_b, ztm, ztm_b, d, wout_d, u_chunk0, vs_col0):
    P = c.P
    vs = H[:, 0:8192].rearrange("p (j c) -> p j c", j=8); vs_b = P.bufs(8, "vs")
    uT = H[:, 8192:10240].rearrange("p (a t) -> p a t", a=2); uT_b = P.buf("uT")
    mixedT = H[:, 11520:13568].rearrange("p (a t) -> p a t", a=2); mixed_b = P.buf("smixedT")
    lng = WR[:, 0:1024]; lnb = WR[:, 1024:2048]
    wsT = WR[:, 2048:2560].rearrange("p (g i) -> p g i", g=4); bsb = WR[:, 2560:3072].rearrange("p (g i) -> p g i", g=4)
    junk = WR[:, 3072:4096]; junk_b = P.buf("sjunk")
    tmp = [WR[:, 4096 + s * 512:4096 + (s + 1) * 512] for s in range(2)]; tmp_b = P.bufs(2, "stmp")
    wo = [R_(WR[:, 9216 + s * 2048:9216 + (s + 1) * 2048]) for s in range(2)]; wo_b = P.bufs(2, "wo")
    tb = P.bufs(4, "stab")
    for i, (dst, key) in enumerate(((lng, "lng_b"), (lnb, "lnb_b"), (wsT, "wsT"), (bsb, "bsb"))):
        P.dma("pool", R_(dst), d[key], [], [tb[i]], None)
    P.op("dve", lambda e: e.tensor_scalar(out=R_(wsT[64:128, :, 0:64]), in0=wsT[64:128, :, 0:64], scalar1=0.0, scalar2=None, op0=ALU.mult), [tb[2]], [tb[2]])
    st = c.sb("sgu_st", [128, 8, 4]); st_b = P.bufs(8, "sgst")
    P.op("dve", lambda e: e.memset(st[:], 0.0), [], st_b)
    for j in range(8):
        P.dma("pool", R_(vs[:, j, :]), ztm[j, :, vs_col0:vs_col0 + 1024], [ztm_b], [vs_b[j]], None)
    for j in range(8):
        v = vs[:, j, :]
        P.op("act", lambda e, v=v, j=j: e.activation(out=R_(v), in_=v, func=AF.Gelu, accum_out=st[:, j, 0:1]), [vs_b[j], st_b[j]], [vs_b[j], st_b[j]])
        P.op("dve", lambda e, j=j: e.tensor_scalar(out=st[:, j, 1:2], in0=st[:, j, 0:1], scalar1=-1.0 / 1024, scalar2=None, op0=ALU.mult), [st_b[j]], [st_b[j]])
        P.op("act", lambda e, v=v, j=j: e.activation(out=R_(junk), in_=v, func=AF.Square, bias=st[:, j, 1:2], accum_out=st[:, j, 2:3]), [vs_b[j], st_b[j], junk_b], [junk_b, st_b[j]])
        P.op("act", lambda e, j=j: e.activation(out=st[:, j, 3:4], in_=st[:, j, 2:3], func=AF.Sqrt, bias=c.eps[:, 0:1], scale=1.0 / 1024), [st_b[j], c.eps_b], [st_b[j]])
        P.op("dve", lambda e, j=j: e.reciprocal(out=st[:, j, 3:4], in_=st[:, j, 3:4]), [st_b[j]], [st_b[j]])
        P.op("dve", lambda e, v=v, j=j: e.tensor_scalar(out=R_(v), in0=v, scalar1=st[:, j, 1:2], scalar2=st[:, j, 3:4], op0=ALU.add, op1=ALU.mult), [vs_b[j], st_b[j]], [vs_b[j]])
        P.op("dve", lambda e, v=v: e.tensor_tensor(out=R_(v), in0=v, in1=lng, op=ALU.mult), [vs_b[j], tb[0]], [vs_b[j]])
        P.op("dve", lambda e, v=v: e.tensor_tensor(out=R_(v), in0=v, in1=lnb, op=ALU.add), [vs_b[j], tb[1]], [vs_b[j]])
    for g in range(4):
        for a in range(2):
            P.dma("pool", R_(uT[:, a, :]), zfm[u_chunk0 + 2 * g + a], [zfm_b], [uT_b], uT_b)
        for a in range(2):
            P.op("act", lambda e, a=a: e.activation(out=R_(uT[:, a, :]), in_=uT[:, a, :], func=AF.Gelu), [uT_b], [uT_b])
        for a in range(2):
            cc = 2 * g + a
            for wq in range(2):
                pb = c.nextps(0, 8)
                for w in range(4):
                    j = 4 * wq + w
                    P.op("pe", lambda e, j=j, w=w, cc=cc, g=g, pb=pb: e.matmul(c.ps[pb][:, w * 128:(w + 1) * 128], R_(vs[:, j, cc * 128:(cc + 1) * 128]), R_(wsT[:, g, :]), start=True, stop=True),
                         [vs_b[j], tb[2]], [c.ps_b[pb]])
                sl = (a * 2 + wq) % 2
                P.op("dve", lambda e, pb=pb, sl=sl, g=g: e.tensor_tensor(out=R_(tmp[sl]).rearrange("p (w i) -> p w i", w=4), in0=c.ps[pb][:].rearrange("p (w i) -> p w i", w=4),
                                                                       in1=bsb[:, g:g + 1, :].to_broadcast([128, 4, 128]), op=ALU.add), [c.ps_b[pb], tb[3]], [tmp_b[sl]])
                P.op("dve", lambda e, a=a, wq=wq, sl=sl: e.tensor_tensor(out=R_(mixedT[:, a, wq * 512:(wq + 1) * 512]), in0=tmp[sl], in1=uT[:, a, wq * 512:(wq + 1) * 512], op=ALU.mult),
                     [tmp_b[sl], uT_b], [mixed_b])
        outproj_partial(c, X, X_b, R_(mixedT), mixed_b, wout_d, [8 + 2 * g, 8 + 2 * g + 1], wo, wo_b)


def build_k2():
    c = Ctx("k2")
    P = c.P
    xT = c.din("xT", [128, KC, T])
    g_mix = c.din("g_mix", [128, KC]); g_ffn = c.din("g_ffn", [128, KC])
    wfm = c.din("wfm", [32, 128, KC * 128]); wtm = c.din("wtm", [12, 128, KC * 256])
    d = {"cosT": c.din("cosT", [128, 1024]), "sinT": c.din("sinT", [128, 1024]),
         "cos_tm": c.din("cos_tm", [128, 8, 128]), "sin_tm": c.din("sin_tm", [128, 8, 128]),
         "dmask": c.din("dmask", [128, 4, 128]), "xi": c.din("xi", [128, 4, 128]), "kz": c.din("kz", [128, 4]),
         "sw": c.din("sw", [128, 12]), "sprev": c.din("sprev", [3, 4, 2, 128, 256]),
         "lng_b": c.din("lng_b", [128, 1024]), "lnb_b": c.din("lnb_b", [128, 1024]),
         "wsT": c.din("wsT", [128, 4, 128]), "bsb": c.din("bsb", [128, 4, 128])}
    wout = c.din("wout", [D, D])
    wg = c.din("wg", [FC, 128, 2048]); wu = c.din("wu", [FC, 128, 2048]); wd = c.din("wd", [DFF, D])
    out = c.dout("out", [128, KC, T])
    zfm = c.dscr("zfm", [32, 128, 1024])
    ztm = c.dscr("ztm", [8, 128, 3072])
    X = c.sb("X", [128, KC, T]); X_b = P.bufs(2 * KC, "X")
    H = c.sb("H", [128, KC * T]); H_b = P.buf("H")
    WR = c.sb("WR", [128, 14336])
    FS = c.sb("FS", [128, 4096])
    Hr = H[:, :].bitcast(F32R).rearrange("p (k t) -> p k t", k=KC)
    gm_sb, gm_b = load_small(c, "gm_sb", g_mix, [128, KC])
    gf_sb, gf_b = load_small(c, "gf_sb", g_ffn, [128, KC])
    sq = FS[:, 0:1024]; sq_b = P.bufs(2, "sq")
    rstd = FS[:, 1024:2048]; rstd_b = P.bufs(2, "rstd")
    load_X(c, X, X_b, xT)
    rmsnorm_fm(c, X, X_b, Hr, H_b, gm_sb, gm_b, sq, sq_b, rstd, rstd_b)
    zfm_b, ztm_b = proj_phase(c, Hr, H_b, KC, wfm, 32, zfm, wtm, 12, ztm, WR, "k2")
    P.barrier()
    retention_phase(c, X, X_b, H, WR, FS, zfm, zfm_b, ztm, ztm_b, d, wout, sq, sq_b, rstd, rstd_b)
    P.barrier()
    sgu_phase(c, X, X_b, H, WR, FS, zfm, zfm_b, ztm, ztm_b, d, wout, 24, 2048)
    P.barrier()
    rmsnorm_fm(c, X, X_b, Hr, H_b, gf_sb, gf_b, sq, sq_b, rstd, rstd_b)
    ffn_phase(c, X, X_b, Hr, H_b, wg, wu, wd, WR, FS[:, 2048:3072])
    store_X(c, X, X_b, out)
    return c.finish(X_b)


def k2_tables(qtr):
    tb = np.arange(128)
    dmask = np.zeros((128, 4, 128), np.float64)
    xi = np.zeros((128, 4, 128), np.float64)
    kz = np.zeros((128, 4), np.float64)
    sw = np.zeros((128, 12), np.float64)
    ch = tb // 64
    for h in range(4):
        gm = RET_GAMMA[h]
        dmask[:, h, :] = (gm ** np.abs(tb[None, :] - tb[:, None])) * (ch[:, None] <= ch[None, :]) / 16.0
        xi[:, h, :] = (gm ** (tb + 1.0))[None, :]
        kz[:, h] = gm ** (127.0 - tb) / 16.0
        for i in range(3):
            if i < qtr:
                sw[:, i * 4 + h] = gm ** (1024.0 * (qtr - i - 1))
    return dmask.astype(np.float32), xi.astype(np.float32), kz.astype(np.float32), sw.astype(np.float32)


def hgrn_phase(c, X, X_b, H, WR, FS, zfm, zfm_b, ztm, ztm_b, d, wout_d, sq, sq_b, rstd, rstd_b, outputs, dbg=None, hstate=None, first=False):
    P = c.P
    qhat = H[:, 0:1024]; qhat_b = P.buf("qhat")
    khat = H[:, 1024:2048]; khat_b = P.buf("khat")
    eb = H[:, 2048:3072]; eb_b = P.buf("eb")
    enb = H[:, 3072:4096]; enb_b = P.buf("enb")
    vtm = H[:, 4096:5120].rearrange("p (j e) -> p j e", j=8); vtm_b = P.buf("hvtm")
    rawq = H[:, 5120:6144]; rawq_b = P.buf("rawq")
    rawf = H[:, 6144:7168]; rawf_b = P.buf("rawf")
    kk = H[:, 7168:7296]; kk_b = P.buf("kk")
    ktl = H[:, 7296:7424]; ktl_b = P.buf("ktl")
    am = [H[:, 7424 + s * 128:7424 + (s + 1) * 128] for s in range(2)]; am_b = P.bufs(2, "am")
    St = [H[:, 7680 + s * 128:7680 + (s + 1) * 128] for s in range(2)]; St_b = P.bufs(2, "St")
    outT = H[:, 7936:8960].rearrange("p (a t) -> p a t", a=1); outT_b = P.buf("houtT")
    mixedT = H[:, 8960:9984].rearrange("p (a t) -> p a t", a=1); mixed_b = P.buf("hmixedT")
    ident = H[:, 9984:10112]; ident_b = P.buf("ident")
    tS = H[:, 10112:10240]; tS_b = P.buf("htS")
    lbt = WR[:, 0:1024]; omlt = WR[:, 1024:2048]; lbt_b = P.buf("lbt")
    rawft = WR[:, 2048:3072].rearrange("p (j e) -> p j e", j=8); rawft_b = P.buf("rawft")
    l1t = WR[:, 3072:4096]
    wo = [R_(WR[:, 9216 + s * 2048:9216 + (s + 1) * 2048]) for s in range(2)]; wo_b = P.bufs(2, "wo")
    logf = FS[:, 2048:3072].rearrange("p (j e) -> p j e", j=8); logf_b = P.buf("logf")
    gate = FS[:, 3072:4096].rearrange("p (a t) -> p a t", a=1); gate_b = P.buf("hgate")
    ltri, ltri_b = load_small(c, "ltri_sb", d["ltri"], [128, 128])
    mtri, mtri_b = load_small(c, "mtri_sb", d["mtri"], [128, 128])
    P.dma("pool", R_(ident), d["ident"], [], [ident_b], None)
    hst_b = P.buf("hstate")
    lbf = c.sb("lbf", [128, 4, 8]); lbf_b = P.buf("lbf")
    P.dma("sp", lbf[:, 3, :], d["lb0f"], [], [lbf_b], None)
    P.dma("sp", lbf[:, 0, :], d["lb1f"], [], [lbf_b], None)
    P.op("dve", lambda e: e.tensor_tensor(out=lbf[:, 0, :], in0=lbf[:, 0, :], in1=lbf[:, 3, :], op=ALU.subtract), [lbf_b], [lbf_b])
    P.op("act", lambda e: e.activation(out=lbf[:, 0, :], in_=lbf[:, 0, :], func=AF.Sigmoid), [lbf_b], [lbf_b])
    P.op("dve", lambda e: e.tensor_scalar(out=lbf[:, 1, :], in0=lbf[:, 0, :], scalar1=-1.0, scalar2=1.0, op0=ALU.mult, op1=ALU.add), [lbf_b], [lbf_b])
    P.op("dve", lambda e: e.tensor_scalar(out=lbf[:, 2, :], in0=lbf[:, 1, :], scalar1=-1.0, scalar2=None, op0=ALU.mult), [lbf_b], [lbf_b])
    P.dma("pool", R_(lbt), d["lb1t"], [], [lbt_b], None)
    P.dma("pool", R_(l1t), d["lb0t"], [], [lbt_b], None)
    P.op("dve", lambda e: e.tensor_tensor(out=R_(lbt), in0=lbt, in1=l1t, op=ALU.subtract), [lbt_b], [lbt_b])
    P.op("act", lambda e: e.activation(out=R_(lbt), in_=lbt, func=AF.Sigmoid), [lbt_b], [lbt_b])
    P.op("dve", lambda e: e.tensor_scalar(out=R_(omlt), in0=lbt, scalar1=-1.0, scalar2=1.0, op0=ALU.mult, op1=ALU.add), [lbt_b], [lbt_b])
    fused = hstate is not None
    if outputs:
        ngain, ngain_b = load_small(c, "ngain_sb", d["ngain"], [128, 8])
        if not fused:
            aprev, aprev_b = load_small(c, "aprev_sb", d["aprev"], [128, 24])
    if not outputs and not fused:
        aprod = c.sb("aprod", [128, 8]); aprod_b = P.buf("aprod")
        P.op("dve", lambda e: e.memset(aprod[:], 1.0), [], [aprod_b])
    if not outputs:
        sout = c.sb("hsout", [128, 128]); sout_b = P.buf("hsout")

    for h in range(8):
        hs_ = slice(h * 128, (h + 1) * 128)
        if outputs:
            P.dma("pool", R_(rawq), zfm[h], [zfm_b], [rawq_b], rawq_b)
        P.dma("pool", R_(rawf), zfm[8 + h], [zfm_b], [rawf_b], rawf_b)
        P.dma("pool", R_(rawft), ztm[:, :, h * 128:(h + 1) * 128].rearrange("j p e -> p j e"), [ztm_b], [rawft_b], rawft_b)
        P.dma("pool", R_(vtm), ztm[:, :, 1024 + h * 128:1024 + (h + 1) * 128].rearrange("j p e -> p j e"), [ztm_b], [vtm_b], vtm_b)
        if outputs:
            P.dma("pool", R_(gate[:, 0, :]), zfm[16 + h], [zfm_b], [gate_b], gate_b)
        P.op("act", lambda e: e.activation(out=R_(rawft), in_=rawft, func=AF.Sigmoid), [rawft_b], [rawft_b])
        P.op("dve", lambda e, hs_=hs_: e.tensor_tensor(out=R_(rawft), in0=rawft, in1=omlt[:, hs_].rearrange("p (o e) -> p o e", o=1).to_broadcast([128, 8, 128]), op=ALU.mult),
             [rawft_b, lbt_b], [rawft_b])
        P.op("dve", lambda e, hs_=hs_: e.tensor_tensor(out=R_(rawft), in0=rawft, in1=lbt[:, hs_].rearrange("p (o e) -> p o e", o=1).to_broadcast([128, 8, 128]), op=ALU.add),
             [rawft_b, lbt_b], [rawft_b])
        P.op("act", lambda e: e.activation(out=logf, in_=rawft, func=AF.Ln), [rawft_b], [logf_b])
        for half in range(2):
            pb = c.nextps(0, 8)
            for jj in range(4):
                j = half * 4 + jj
                P.op("pe", lambda e, j=j, jj=jj, pb=pb: e.matmul(c.ps[pb][:, jj * 128:(jj + 1) * 128], logf[:, j, :], ltri[:], start=True, stop=True),
                     [logf_b, ltri_b], [c.ps_b[pb]])
            hsl = slice(half * 512, (half + 1) * 512)
            P.op("act", lambda e, hsl=hsl, pb=pb: e.activation(out=R_(eb[:, hsl]), in_=c.ps[pb][:], func=AF.Exp), [c.ps_b[pb]], [eb_b])
            P.op("act", lambda e, hsl=hsl, pb=pb: e.activation(out=R_(enb[:, hsl]), in_=c.ps[pb][:], func=AF.Exp, scale=-1.0), [c.ps_b[pb]], [enb_b])
        if outputs:
            P.op("act", lambda e: e.activation(out=R_(rawq), in_=rawq, func=AF.Silu), [rawq_b], [rawq_b])
            P.op("dve", lambda e: e.tensor_tensor(out=R_(qhat), in0=rawq, in1=eb, op=ALU.mult), [rawq_b, eb_b], [qhat_b])
        P.op("act", lambda e: e.activation(out=R_(rawf), in_=rawf, func=AF.Sigmoid), [rawf_b], [rawf_b])
        P.op("dve", lambda e, h=h: e.tensor_scalar(out=R_(rawf), in0=rawf, scalar1=lbf[:, 2, h:h + 1], scalar2=lbf[:, 1, h:h + 1], op0=ALU.mult, op1=ALU.add), [rawf_b, lbf_b], [rawf_b])
        P.op("dve", lambda e: e.tensor_tensor(out=R_(khat), in0=rawf, in1=enb, op=ALU.mult), [rawf_b, enb_b], [khat_b])
        if fused:
            if first:
                P.op("dve", lambda e: e.tensor_scalar(out=R_(St[0]), in0=ident, scalar1=0.0, scalar2=None, op0=ALU.mult), [ident_b], [St_b[0]])
            else:
                P.dma("pool", R_(St[0]), hstate[h], [hst_b], [St_b[0]], St_b[0])
        elif outputs:
            for i in range(3):
                P.dma("pool", R_(tS), d["sprev"][i, h], [], [tS_b], tS_b)
                if i == 0:
                    P.op("dve", lambda e: e.tensor_copy(out=R_(St[0]), in_=tS), [tS_b], [St_b[0]])
                else:
                    P.op("dve", lambda e, i=i, h=h: e.scalar_tensor_tensor(out=R_(St[0]), in0=St[0], scalar=aprev[:, i * 8 + h:i * 8 + h + 1], in1=tS, op0=ALU.mult, op1=ALU.add),
                         [St_b[0], aprev_b, tS_b], [St_b[0]])
        else:
            P.op("dve", lambda e: e.tensor_scalar(out=R_(St[0]), in0=ident, scalar1=0.0, scalar2=None, op0=ALU.mult), [ident_b], [St_b[0]])
        for j in range(8):
            js = slice(j * 128, (j + 1) * 128)
            ja = slice(j * 128, j * 128 + 64); jb = slice(j * 128 + 64, (j + 1) * 128)
            ca = j * 128 + 63; cb_ = j * 128 + 127
            cur = j % 2
            P.op("dve", lambda e, ja=ja, ca=ca: e.tensor_scalar(out=R_(kk[:, 0:64]), in0=khat[:, ja], scalar1=eb[:, ca:ca + 1], scalar2=None, op0=ALU.mult), [khat_b, eb_b], [kk_b])
            P.op("dve", lambda e, jb=jb, cb_=cb_: e.tensor_scalar(out=R_(kk[:, 64:128]), in0=khat[:, jb], scalar1=eb[:, cb_:cb_ + 1], scalar2=None, op0=ALU.mult), [khat_b, eb_b], [kk_b])
            pt = c.nextps(0, 8)
            P.op("pe", lambda e, pt=pt: e.matmul(c.ps[pt][:, 0:128], R_(kk), R_(ident), start=True, stop=True), [kk_b, ident_b], [c.ps_b[pt]])
            P.op("act", lambda e, pt=pt: e.copy(out=R_(ktl), in_=c.ps[pt][:, 0:128]), [c.ps_b[pt]], [ktl_b])
            pa = c.nextps(0, 8)
            P.op("pe", lambda e, j=j, pa=pa: e.matmul(c.ps[pa][:, 0:128], R_(ktl[0:64, :]), R_(vtm[0:64, j, :]), start=True, stop=True), [ktl_b, vtm_b], [c.ps_b[pa]])
            P.op("dve", lambda e, pa=pa, ca=ca: e.scalar_tensor_tensor(out=R_(St[1]), in0=St[0], scalar=eb[:, ca:ca + 1], in1=c.ps[pa][:, 0:128], op0=ALU.mult, op1=ALU.add),
                 [St_b[0], eb_b, c.ps_b[pa]], [St_b[1]])
            if outputs:
                pb = c.nextps(0, 8)
                P.op("pe", lambda e, js=js, pb=pb: e.matmul(c.ps[pb][:, 0:128], R_(khat[:, js]), R_(qhat[:, js]), start=True, stop=True), [khat_b, qhat_b], [c.ps_b[pb]])
                P.op("dve", lambda e, pb=pb, cur=cur: e.tensor_tensor(out=R_(am[cur]), in0=c.ps[pb][:, 0:128], in1=mtri[:], op=ALU.mult), [c.ps_b[pb], mtri_b], [am_b[cur]])
                po = c.nextps(0, 8)
                for (lo, qsl, stt) in ((0, ja, 0), (64, jb, 1)):
                    P.op("pe", lambda e, j=j, lo=lo, po=po, cur=cur: e.matmul(c.ps[po][:, lo:lo + 64], R_(vtm[:, j, :]), R_(am[cur][:, lo:lo + 64]), start=True, stop=False),
                         [vtm_b, am_b[cur]], [c.ps_b[po]])
                    P.op("pe", lambda e, lo=lo, po=po, qsl=qsl, stt=stt: e.matmul(c.ps[po][:, lo:lo + 64], R_(St[stt]), R_(qhat[:, qsl]), start=False, stop=True),
                         [St_b[stt], qhat_b], [c.ps_b[po]])
                P.op("act", lambda e, js=js, po=po: e.copy(out=R_(outT[:, 0, js]), in_=c.ps[po][:, 0:128]), [c.ps_b[po]], [outT_b])
            elif not fused:
                P.op("dve", lambda e, ca=ca, h=h: e.tensor_tensor(out=aprod[:, h:h + 1], in0=aprod[:, h:h + 1], in1=eb[:, ca:ca + 1], op=ALU.mult), [aprod_b, eb_b], [aprod_b])
                P.op("dve", lambda e, cb_=cb_, h=h: e.tensor_tensor(out=aprod[:, h:h + 1], in0=aprod[:, h:h + 1], in1=eb[:, cb_:cb_ + 1], op=ALU.mult), [aprod_b, eb_b], [aprod_b])
            if j < 7 or not outputs:
                pc = c.nextps(0, 8)
                P.op("pe", lambda e, j=j, pc=pc: e.matmul(c.ps[pc][:, 0:128], R_(ktl[64:128, :]), R_(vtm[64:128, j, :]), start=True, stop=True), [ktl_b, vtm_b], [c.ps_b[pc]])
                P.op("dve", lambda e, pc=pc, cb_=cb_: e.scalar_tensor_tensor(out=R_(St[0]), in0=St[1], scalar=eb[:, cb_:cb_ + 1], in1=c.ps[pc][:, 0:128], op0=ALU.mult, op1=ALU.add),
                     [St_b[1], eb_b, c.ps_b[pc]], [St_b[0]])
        if outputs:
            groupnorm_gate(c, outT, outT_b, 1, 128, gate, gate_b, ngain[:, h:h + 1], ngain_b, mixedT, mixed_b, sq, sq_b, rstd, rstd_b)
            if dbg is not None:
                P.dma("sp", dbg[h], mixedT[:, 0, :], [mixed_b], [], mixed_b)
            outproj_partial(c, X, X_b, R_(mixedT), mixed_b, wout_d, [h], wo, wo_b)
        else:
            P.op("dve", lambda e: e.tensor_copy(out=sout[:], in_=St[0]), [St_b[0]], [sout_b])
            if fused:
                P.dma("sp", hstate[h], sout[:], [sout_b], [hst_b], sout_b)
            else:
                P.dma("sp", d["sloc"][h], sout[:], [sout_b], [], sout_b)
    if not outputs and not fused:
        P.dma("sp", d["aout"], aprod[:], [aprod_b], [], aprod_b)
        return [sout_b, aprod_b]
    return []


def rstd_fm(c, src, src_b, nchunk, width, sq, sq_b, rstd, rstd_b):
    P = c.P
    for half in range(2):
        hs = slice(half * 512, (half + 1) * 512)
        pb = c.nextps(0, 8)
        for i in range(nchunk):
            sl = i % 2
            ss = slice(sl * 512, (sl + 1) * 512)
            P.op("act", lambda e, i=i, hs=hs, ss=ss: e.activation(out=sq[:, ss], in_=src[:, i, hs], func=AF.Square), [src_b], [sq_b[sl]])
            P.op("pe", lambda e, i=i, ss=ss, pb=pb: e.matmul(c.ps[pb][:], c.ones[:], sq[:, ss], start=(i == 0), stop=(i == nchunk - 1)),
                 [c.ones_b, sq_b[sl]], [c.ps_b[pb]])
        P.op("act", lambda e, hs=hs, pb=pb: e.activation(out=rstd[:, hs], in_=c.ps[pb][:], func=AF.Sqrt, bias=c.eps[:, 0:1], scale=1.0 / width),
             [c.ps_b[pb], c.eps_b], [rstd_b[half]])
        P.op("dve", lambda e, hs=hs: e.reciprocal(out=rstd[:, hs], in_=rstd[:, hs]), [rstd_b[half]], [rstd_b[half]])


L1_FM = 30
L1_TM = 10


def l1_common_inputs(c):
    d = {"lb0f": c.din("lb0f", [128, 8]), "lb1f": c.din("lb1f", [128, 8]), "lb0t": c.din("lb0t", [128, 1024]), "lb1t": c.din("lb1t", [128, 1024]),
         "ltri": c.din("ltri", [128, 128]), "mtri": c.din("mtri", [128, 128]), "ident": c.din("ident", [128, 128])}
    return d


def build_k3():
    c = Ctx("k3")
    P = c.P
    xT = c.din("xT", [128, KC, T])
    g_mix = c.din("g_mix", [128, KC])
    wfm = c.din("wfm", [L1_FM, 128, KC * 128]); wtm = c.din("wtm", [L1_TM, 128, KC * 256])
    d = l1_common_inputs(c)
    ckvg_f = c.din("ckvg_f", [128, 2]); ckvg_t = c.din("ckvg_t", [128, 256])
    d["sloc"] = c.dout("sloc", [8, 128, 128]); d["aout"] = c.dout("aout", [128, 8])
    kvtm_o = c.dout("kvtm", [8, 128, 256]); kvT_o = c.dout("kvT", [2, 128, 1024]); kixT_o = c.dout("kixT", [128, 1024])
    zfm = c.dscr("zfm1", [L1_FM, 128, 1024]); ztm = c.dscr("ztm1", [8, 128, L1_TM * 256])
    X = c.sb("X", [128, KC, T]); X_b = P.bufs(2 * KC, "X")
    H = c.sb("H", [128, KC * T]); H_b = P.buf("H")
    WR = c.sb("WR", [128, 14336])
    FS = c.sb("FS", [128, 4096])
    Hr = H[:, :].bitcast(F32R).rearrange("p (k t) -> p k t", k=KC)
    gm_sb, gm_b = load_small(c, "gm_sb", g_mix, [128, KC])
    sq = FS[:, 0:1024]; sq_b = P.bufs(2, "sq")
    rstd = FS[:, 1024:2048]; rstd_b = P.bufs(2, "rstd")
    load_X(c, X, X_b, xT)
    rmsnorm_fm(c, X, X_b, Hr, H_b, gm_sb, gm_b, sq, sq_b, rstd, rstd_b)
    zfm_b, ztm_b = proj_phase(c, Hr, H_b, KC, wfm, L1_FM, zfm, wtm, L1_TM, ztm, WR, "k3")
    P.barrier()
    fin = hgrn_phase(c, X, X_b, H, WR, FS, zfm, zfm_b, ztm, ztm_b, d, None, sq, sq_b, rstd, rstd_b, outputs=False)
    P.barrier()
    Xf = X[:, :, :].rearrange("p k t -> p (k t)")
    gf, gf_b = load_small(c, "ckvgf_sb", ckvg_f, [128, 2])
    gt, gt_b = load_small(c, "ckvgt_sb", ckvg_t, [128, 256])
    ckT = Xf[:, 0:2048].rearrange("p (a t) -> p a t", a=2); ckT_b = P.buf("ckT")
    for a in range(2):
        P.dma("sp", ckT[:, a, :], zfm[27 + a], [zfm_b], [ckT_b], ckT_b)
    rstd_fm(c, ckT, ckT_b, 2, 256, sq, sq_b, rstd, rstd_b)
    for a in range(2):
        P.op("dve", lambda e, a=a: e.scalar_tensor_tensor(out=ckT[:, a, :], in0=ckT[:, a, :], scalar=gf[:, a:a + 1], in1=rstd[:, :], op0=ALU.mult, op1=ALU.mult),
             [ckT_b, gf_b, rstd_b[0], rstd_b[1]], [ckT_b])
        P.dma("sp", kvT_o[a], ckT[:, a, :], [ckT_b], [], ckT_b)
    kx = Xf[:, 2048:3072]; kx_b = P.buf("kx")
    P.dma("sp", kx, zfm[29], [zfm_b], [kx_b], kx_b)
    P.dma("sp", kixT_o, kx, [kx_b], [], kx_b)
    ck = Xf[:, 4096:6144].rearrange("p (j r) -> p j r", j=8); ck_b = P.bufs(8, "ck")
    junk = Xf[:, 6144:6400]; junk_b = P.buf("ckjunk")
    st = c.sb("ck_st", [128, 8, 2]); st_b = P.bufs(8, "ckst")
    P.op("dve", lambda e: e.memset(st[:], 0.0), [], st_b)
    for j in range(8):
        P.dma("sp", ck[:, j, :], ztm[j, :, 2048:2304], [ztm_b], [ck_b[j]], ck_b[j])
        P.op("act", lambda e, j=j: e.activation(out=junk, in_=ck[:, j, :], func=AF.Square, accum_out=st[:, j, 0:1]), [ck_b[j], st_b[j], junk_b], [junk_b, st_b[j]])
        P.op("act", lambda e, j=j: e.activation(out=st[:, j, 1:2], in_=st[:, j, 0:1], func=AF.Sqrt, bias=c.eps[:, 0:1], scale=1.0 / 256), [st_b[j], c.eps_b], [st_b[j]])
        P.op("dve", lambda e, j=j: e.reciprocal(out=st[:, j, 1:2], in_=st[:, j, 1:2]), [st_b[j]], [st_b[j]])
        P.op("dve", lambda e, j=j: e.scalar_tensor_tensor(out=ck[:, j, :], in0=ck[:, j, :], scalar=st[:, j, 1:2], in1=gt[:], op0=ALU.mult, op1=ALU.mult),
             [ck_b[j], st_b[j], gt_b], [ck_b[j]])
        P.dma("sp", kvtm_o[j], ck[:, j, :], [ck_b[j]], [], ck_b[j])
    return c.finish(fin + [ckT_b, kx_b] + ck_b)


def l1_weights(w_in):
    pad_fm = np.zeros((D, 128), np.float32); pad_fm[:, 0:80] = w_in[:, 4736:4816]
    fm = np.concatenate([w_in[:, 0:2048], w_in[:, 3072:4096], w_in[:, 4096:4480], w_in[:, 4480:4736], pad_fm], 1)
    pad_tm = np.zeros((D, 256), np.float32); pad_tm[:, 0:80] = w_in[:, 4736:4816]
    tm = np.concatenate([w_in[:, 1024:3072], w_in[:, 4480:4736], pad_tm], 1)
    return tile_fm(np.ascontiguousarray(fm)), tile_tm(np.ascontiguousarray(tm))


def l1_tables():
    s = np.arange(128)
    same = (s[:, None] // 64) == (s[None, :] // 64)
    tri = (same & (s[:, None] <= s[None, :])).astype(np.float32)
    return tri.copy(), tri.copy(), np.eye(128, dtype=np.float32)


def bcast_rows(v, n=128):
    return np.ascontiguousarray(np.broadcast_to(np.asarray(v)[None], (n,) + np.asarray(v).shape))


def build_k4(dsa=True, dbg=False):
    c = Ctx("k4")
    P = c.P
    xT = c.din("xT", [128, KC, T])
    g_mix = c.din("g_mix", [128, KC]); g_ffn = c.din("g_ffn", [128, KC])
    wfm = c.din("wfm", [L1_FM, 128, KC * 128]); wtm = c.din("wtm", [L1_TM, 128, KC * 256])
    d = l1_common_inputs(c)
    d["ngain"] = c.din("ngain", [128, 8]); d["sprev"] = c.din("sprev", [3, 8, 128, 128]); d["aprev"] = c.din("aprev", [128, 24])
    wout = c.din("wout", [D, D])
    wg = c.din("wg", [FC, 128, 2048]); wu = c.din("wu", [FC, 128, 2048]); wd = c.din("wd", [DFF, D])
    out = c.dout("out", [128, KC, T])
    dbg_o = c.dout("dbg", [16, 128, 1024]) if dbg else None
    zfm = c.dscr("zfm1", [L1_FM, 128, 1024]); ztm = c.dscr("ztm1", [8, 128, L1_TM * 256])
    X = c.sb("X", [128, KC, T]); X_b = P.bufs(2 * KC, "X")
    H = c.sb("H", [128, KC * T]); H_b = P.buf("H")
    WR = c.sb("WR", [128, 14336])
    FS = c.sb("FS", [128, 4096])
    Hr = H[:, :].bitcast(F32R).rearrange("p (k t) -> p k t", k=KC)
    gm_sb, gm_b = load_small(c, "gm_sb", g_mix, [128, KC])
    gf_sb, gf_b = load_small(c, "gf_sb", g_ffn, [128, KC])
    sq = FS[:, 0:1024]; sq_b = P.bufs(2, "sq")
    rstd = FS[:, 1024:2048]; rstd_b = P.bufs(2, "rstd")
    load_X(c, X, X_b, xT)
    rmsnorm_fm(c, X, X_b, Hr, H_b, gm_sb, gm_b, sq, sq_b, rstd, rstd_b)
    zfm_b, ztm_b = proj_phase(c, Hr, H_b, KC, wfm, L1_FM, zfm, wtm, L1_TM, ztm, WR, "k4")
    P.barrier()
    hgrn_phase(c, X, X_b, H, WR, FS, zfm, zfm_b, ztm, ztm_b, d, wout, sq, sq_b, rstd, rstd_b, outputs=True, dbg=dbg_o)
    P.barrier()
    if dsa:
        dsa_phase(c, X, X_b, H, WR, FS, zfm, zfm_b, ztm, ztm_b, wout, sq, sq_b, rstd, rstd_b, dbg_o)
        P.barrier()
    rmsnorm_fm(c, X, X_b, Hr, H_b, gf_sb, gf_b, sq, sq_b, rstd, rstd_b)
    ffn_phase(c, X, X_b, Hr, H_b, wg, wu, wd, WR, FS[:, 2048:3072])
    store_X(c, X, X_b, out)
    return c.finish(X_b)


BIG = 1.0e30
NBIS = 24
REPL = -3.0e38


def dsa_phase(c, X, X_b, H, WR, FS, zfm, zfm_b, ztm, ztm_b, wout_d, sq, sq_b, rstd, rstd_b, dbg=None, keys=None, prep=None):
    P = c.P
    d = {} if keys is not None else {"kvT_all": c.din("kvT_all", [2, 128, 4096]), "kvtm_all": c.din("kvtm_all", [32, 128, 256]), "kixT_all": c.din("kixT_all", [128, 4096])}
    keys_b = P.buf("dsakeys")
    if keys is not None:
        d.update(keys)
    d.update({"slotadd": c.din("slotadd", [128, 3]), "slotok": c.din("slotok", [128, 3]),
         "admadd": c.din("admadd", [128, 128]), "admok": c.din("admok", [128, 128]),
         "wuq": c.din("wuq", [16, 128, 384]), "wqi": c.din("wqi", [8, 128, 384]),
         "cqg": c.din("cqg", [128, 3]), "qng": c.din("qng", [128, 2]), "wuv": c.din("wuv", [128, 8, 2, 128]),
         "bias_g": c.din("bias_g", [128, 8, 3, 128]), "crow": c.din("crow", [128, 8]), "ident2": c.din("ident2", [128, 128])})
    xpark = c.dscr("xpark", [128, KC, T])
    qT_s = c.dscr("qT_s", [16, 128, 1024]); qiT_s = c.dscr("qiT_s", [8, 128, 1024]); at_s = c.dscr("at_s", [8, 128, 1024])
    qT_sb = P.buf("qT_s"); qiT_sb = P.buf("qiT_s"); at_sb = P.buf("at_s"); xpark_b = P.buf("xpark")
    for k in range(KC):
        P.dma("sp", xpark[:, k, :], X[:, k, :], [X_b[2 * k], X_b[2 * k + 1]], [xpark_b], X_b[8 * (k // 4)])
    P.barrier()
    if prep is not None:
        prep()
        P.barrier()
    Xf = X[:, :, :].rearrange("p k t -> p (k t)")
    sc = Xf[:, 0:4096]; sc_b = P.buf("sc")
    maskT = Xf[:, 4096:8192].rearrange("p (k t) -> p k t", k=32); maskT_b = P.buf("maskT")
    biasT = Xf[:, 8192:11264].rearrange("p (h n t) -> p h n t", h=8, n=3); biasT_b = P.buf("biasT")
    wi = Xf[:, 11264:11392].rearrange("p (j h) -> p j h", j=8)
    aw = Xf[:, 11392:11520].rearrange("p (j h) -> p j h", j=8)
    sg = Xf[:, 11520:11648].rearrange("p (j h) -> p j h", j=8); wi_b = P.buf("wi")
    admadd = Xf[:, 11648:11776]; admok = Xf[:, 11776:11904]; adm_b = P.buf("adm")
    m8 = Xf[:, 11904:11912]; m8_b = P.buf("m8")
    bs = Xf[:, 11912:11920]; bs_b = P.buf("bisect")
    junk = Xf[:, 12288:16384]; junk_b = P.buf("bjunk")
    qraw = Xf[:, 12288:14336].rearrange("p (a t) -> p a t", a=2); qraw_b = P.buf("qraw")
    qout = Xf[:, 14336:16384].rearrange("p (a t) -> p a t", a=2); qout_b = P.buf("qout")
    kvT = H[:, 0:8192].rearrange("p (a s) -> p a s", a=2); kvT_b = P.buf("kvTall")
    kvtm = H[:, 8192:16384].rearrange("p (k r) -> p k r", k=32); kvtm_b = P.buf("kvtmall")
    kix = WR[:, 0:4096]; kix_b = P.buf("kixall")
    sel = WR[:, 4096:8192]; sel_b = P.buf("sel")
    E = [WR[:, 4096 + s * 512:4096 + (s + 1) * 512] for s in range(3)]; E_b = P.bufs(3, "E")
    onT = WR[:, 6144:7168].rearrange("p (a t) -> p a t", a=2); onT_b = P.buf("onT")
    qblk = WR[:, 8192:10240].rearrange("p (n t) -> p n t", n=16); qblk_b = P.buf("qblk")
    qiblk = WR[:, 10240:11264].rearrange("p (n t) -> p n t", n=8); qiblk_b = P.buf("qiblk")
    wuv = WR[:, 11264:13312].rearrange("p (h a v) -> p h a v", h=8, a=2); wuv_b = P.buf("wuv")
    ident = WR[:, 13312:13440]; onesR = WR[:, 13440:13568]; id_b = P.buf("ident2")
    cqT = WR[:, 8192:11264].rearrange("p (a t) -> p a t", a=3); cqT_b = P.buf("cqT")
    wq = [WR[:, 11264 + s * 384:11264 + (s + 1) * 384].rearrange("p (k j) -> p k j", k=3) for s in range(2)]; wq_b = P.bufs(2, "wq")
    rtmp = [FS[:, 2048 + s * 512:2048 + (s + 1) * 512] for s in range(2)]; rtmp_b = P.bufs(2, "rtmp")
    ltmp = [FS[:, 2560 + s * 512:2560 + (s + 1) * 512] for s in range(3)]; ltmp_b = P.bufs(3, "ltmp")
    rden = FS[:, 2048:2560]
    cqg, cqg_b = load_small(c, "cqg_sb", d["cqg"], [128, 3])
    qng, qng_b = load_small(c, "qng_sb", d["qng"], [128, 2])
    slotadd, sla_b = load_small(c, "slotadd_sb", d["slotadd"], [128, 3])
    slotok, slo_b = load_small(c, "slotok_sb", d["slotok"], [128, 3])
    crow, crow_b = load_small(c, "crow_sb", d["crow"], [128, 8])
    P.dma("pool", R_(ident), d["ident2"], [], [id_b], None)
    P.op("dve", lambda e: e.tensor_scalar(out=R_(onesR), in0=ident, scalar1=0.0, scalar2=1.0, op0=ALU.mult, op1=ALU.add), [id_b], [id_b])
    P.dma("sp", admadd, d["admadd"], [], [adm_b], None)
    P.dma("sp", admok, d["admok"], [], [adm_b], None)
    for a in range(3):
        P.dma("pool", R_(cqT[:, a, :]), zfm[24 + a], [zfm_b], [cqT_b], cqT_b)
    rstd_fm(c, cqT, cqT_b, 3, 384, sq, sq_b, rstd, rstd_b)
    for a in range(3):
        P.op("dve", lambda e, a=a: e.scalar_tensor_tensor(out=R_(cqT[:, a, :]), in0=cqT[:, a, :], scalar=cqg[:, a:a + 1], in1=rstd[:, :], op0=ALU.mult, op1=ALU.mult),
             [cqT_b, cqg_b, rstd_b[0], rstd_b[1]], [cqT_b])
    nw = 0

    def small_proj(w_d, n, dst, dst_b, act_eng):
        nonlocal nw
        s = nw % 2
        nw += 1
        P.dma("pool", R_(wq[s]), w_d[n].rearrange("p (k j) -> p k j", k=3), [], [wq_b[s]], wq_b[s])
        for half in range(2):
            hs = slice(half * 512, (half + 1) * 512)
            pb = c.nextps(0, 8)
            for k in range(3):
                P.op("pe", lambda e, s=s, k=k, hs=hs, pb=pb: e.matmul(c.ps[pb][:], R_(wq[s][:, k, :]), R_(cqT[:, k, hs]), start=(k == 0), stop=(k == 2)),
                     [wq_b[s], cqT_b], [c.ps_b[pb]])
            P.op(act_eng, (lambda e, hs=hs, pb=pb: e.copy(out=dst[:, hs], in_=c.ps[pb][:])) if act_eng == "act" else
                 (lambda e, hs=hs, pb=pb: e.tensor_copy(out=dst[:, hs], in_=c.ps[pb][:])), [c.ps_b[pb]], [dst_b])

    for h in range(8):
        for a in range(2):
            small_proj(d["wuq"], 2 * h + a, qraw[:, a, :], qraw_b, "act")
        rstd_fm(c, qraw, qraw_b, 2, 256, sq, sq_b, rstd, rstd_b)
        for a in range(2):
            P.op("dve", lambda e, a=a: e.scalar_tensor_tensor(out=qout[:, a, :], in0=qraw[:, a, :], scalar=qng[:, a:a + 1], in1=rstd[:, :], op0=ALU.mult, op1=ALU.mult),
                 [qraw_b, qng_b, rstd_b[0], rstd_b[1]], [qout_b])
            P.dma("sp", qT_s[2 * h + a], qout[:, a, :], [qout_b], [qT_sb], qout_b)
    for n in range(8):
        a = n % 2
        small_proj(d["wqi"], n, qout[:, a, :], qout_b, "dve")
        P.dma("sp", qiT_s[n], qout[:, a, :], [qout_b], [qiT_sb], qout_b)
    P.barrier()
    for a in range(2):
        P.dma("pool", R_(kvT[:, a, :]), d["kvT_all"][a], [keys_b], [kvT_b], None)
    for k4 in range(4):
        P.dma("pool", R_(kvtm[:, k4 * 8:(k4 + 1) * 8, :]), d["kvtm_all"][k4 * 8:(k4 + 1) * 8].rearrange("k p r -> p k r"), [keys_b], [kvtm_b], None)
    P.dma("pool", R_(kix), d["kixT_all"], [keys_b], [kix_b], None)
    P.dma("pool", R_(wuv), d["wuv"], [], [wuv_b], None)
    P.dma("sp", wi, ztm[:, :, 2304 + 64:2304 + 80].rearrange("j p h -> p j h"), [ztm_b], [wi_b], None)
    P.op("act", lambda e: e.activation(out=sg, in_=wi, func=AF.Sign), [wi_b], [wi_b])
    P.op("dve", lambda e: e.tensor_tensor(out=aw, in0=wi, in1=sg, op=ALU.mult), [wi_b], [wi_b])
    P.op("dve", lambda e: e.tensor_scalar(out=aw, in0=aw, scalar1=1.0 / 32.0, scalar2=None, op0=ALU.mult), [wi_b], [wi_b])
    P.dma("sp", biasT, d["bias_g"], [], [biasT_b], None)
    for h in range(8):
        P.op("dve", lambda e, h=h: e.tensor_scalar(out=biasT[:, h], in0=biasT[:, h], scalar1=crow[:, h:h + 1], scalar2=16.0, op0=ALU.subtract, op1=ALU.mult),
             [biasT_b, crow_b], [biasT_b])
    for qb in range(8):
        qs_ = slice(qb * 128, (qb + 1) * 128)
        nown = (qb + 1) * 128
        P.dma("pool", R_(qblk), qT_s[:, :, qs_].rearrange("n p t -> p n t"), [qT_sb], [qblk_b], qblk_b)
        P.dma("pool", R_(qiblk), qiT_s[:, :, qs_].rearrange("n p t -> p n t"), [qiT_sb], [qiblk_b], qiblk_b)
        tiles = [(lo, min(lo + 512, nown)) for lo in range(0, nown, 512)] + [(lo, lo + 512) for lo in range(1024, 4096, 512)]
        cnt = 0
        for (lo, hi_) in tiles:
            n = hi_ - lo
            for hd in range(16):
                pr = slice((hd % 2) * 64, (hd % 2) * 64 + 64)
                pb = c.nextps(3, 5)
                P.op("pe", lambda e, hd=hd, pr=pr, lo=lo, hi_=hi_, n=n, pb=pb: e.matmul(c.ps[pb][:, 0:n], R_(qiblk[pr, hd // 2, :]), R_(kix[pr, lo:hi_]), start=True, stop=True),
                     [qiblk_b, kix_b], [c.ps_b[pb]])
                s = cnt % 2
                cnt += 1
                P.op("act", lambda e, hd=hd, n=n, pb=pb, s=s, qb=qb: e.activation(out=rtmp[s][:, 0:n], in_=c.ps[pb][:, 0:n], func=AF.Relu, scale=aw[:, qb, hd:hd + 1]),
                     [c.ps_b[pb], wi_b], [rtmp_b[s]])
                if hd == 0:
                    P.op("dve", lambda e, lo=lo, hi_=hi_, n=n, s=s, qb=qb, hd=hd: e.tensor_scalar(out=sc[:, lo:hi_], in0=rtmp[s][:, 0:n], scalar1=sg[:, qb, hd:hd + 1], scalar2=None, op0=ALU.mult),
                         [rtmp_b[s], wi_b], [sc_b])
                else:
                    P.op("dve", lambda e, lo=lo, hi_=hi_, n=n, s=s, qb=qb, hd=hd: e.scalar_tensor_tensor(out=sc[:, lo:hi_], in0=rtmp[s][:, 0:n], scalar=sg[:, qb, hd:hd + 1], in1=sc[:, lo:hi_],
                                                                                                  op0=ALU.mult, op1=ALU.add), [rtmp_b[s], wi_b, sc_b], [sc_b])
        dsl = slice(qb * 128, (qb + 1) * 128)
        if nown < 1024:
            P.op("dve", lambda e, nown=nown: e.memset(sc[:, nown:1024], 0.0), [sc_b], [sc_b])
        P.op("dve", lambda e: e.tensor_reduce(out=bs[:, 5:6], in_=sc, axis=AX.X, op=ALU.max, apply_absolute_value=True), [sc_b, bs_b], [bs_b])
        P.op("dve", lambda e: e.tensor_scalar(out=bs[:, 0:1], in0=bs[:, 5:6], scalar1=-1.0, scalar2=-1.0, op0=ALU.mult, op1=ALU.add), [bs_b], [bs_b])
        P.op("dve", lambda e: e.tensor_scalar(out=bs[:, 1:2], in0=bs[:, 5:6], scalar1=2.0, scalar2=2.0, op0=ALU.mult, op1=ALU.add), [bs_b], [bs_b])
        P.op("dve", lambda e, dsl=dsl: e.tensor_tensor(out=sc[:, dsl], in0=sc[:, dsl], in1=admadd, op=ALU.add), [sc_b, adm_b], [sc_b])
        if nown < 1024:
            P.op("dve", lambda e, nown=nown: e.memset(sc[:, nown:1024], -BIG), [sc_b], [sc_b])
        for i in range(3):
            P.op("dve", lambda e, i=i: e.tensor_scalar(out=sc[:, 1024 * (i + 1):1024 * (i + 2)], in0=sc[:, 1024 * (i + 1):1024 * (i + 2)], scalar1=slotadd[:, i:i + 1], scalar2=None, op0=ALU.add),
                 [sc_b, sla_b], [sc_b])
        for it in range(NBIS):
            P.op("dve", lambda e: e.tensor_scalar(out=bs[:, 1:2], in0=bs[:, 1:2], scalar1=0.5, scalar2=None, op0=ALU.mult), [bs_b], [bs_b])
            P.op("dve", lambda e: e.tensor_tensor(out=bs[:, 2:3], in0=bs[:, 0:1], in1=bs[:, 1:2], op=ALU.add), [bs_b], [bs_b])
            P.op("dve", lambda e: e.tensor_scalar(out=junk, in0=sc, scalar1=bs[:, 2:3], scalar2=0.0, op0=ALU.is_ge, op1=ALU.add, accum_out=bs[:, 3:4]), [sc_b, bs_b, junk_b], [junk_b, bs_b])
            P.op("dve", lambda e: e.tensor_scalar(out=bs[:, 4:5], in0=bs[:, 3:4], scalar1=255.5, scalar2=None, op0=ALU.is_ge), [bs_b], [bs_b])
            P.op("dve", lambda e: e.scalar_tensor_tensor(out=bs[:, 0:1], in0=bs[:, 4:5], scalar=bs[:, 1:2], in1=bs[:, 0:1], op0=ALU.mult, op1=ALU.add), [bs_b], [bs_b])
        P.barrier()
        P.op("dve", lambda e: e.tensor_scalar(out=R_(sel), in0=sc, scalar1=bs[:, 0:1], scalar2=None, op0=ALU.is_ge), [sc_b, bs_b], [sel_b])
        P.op("dve", lambda e, dsl=dsl: e.tensor_tensor(out=R_(sel[:, dsl]), in0=sel[:, dsl], in1=admok, op=ALU.mult), [sel_b, adm_b], [sel_b])
        if nown < 1024:
            P.op("dve", lambda e, nown=nown: e.tensor_scalar(out=R_(sel[:, nown:1024]), in0=sel[:, nown:1024], scalar1=0.0, scalar2=None, op0=ALU.mult), [sel_b], [sel_b])
        for i in range(3):
            P.op("dve", lambda e, i=i: e.tensor_scalar(out=R_(sel[:, 1024 * (i + 1):1024 * (i + 2)]), in0=sel[:, 1024 * (i + 1):1024 * (i + 2)], scalar1=slotok[:, i:i + 1], scalar2=None, op0=ALU.mult),
                 [sel_b, slo_b], [sel_b])
        kbs = list(range(qb + 1)) + list(range(8, 32))
        for kb in kbs:
            pb = c.nextps(3, 5)
            P.op("pe", lambda e, kb=kb, pb=pb: e.matmul(c.ps[pb][:, 0:128], R_(sel[:, kb * 128:(kb + 1) * 128]), R_(ident), start=True, stop=True), [sel_b, id_b], [c.ps_b[pb]])
            P.op("dve", lambda e, kb=kb, pb=pb: e.tensor_scalar(out=maskT[:, kb, :], in0=c.ps[pb][:, 0:128], scalar1=-1.0, scalar2=BIG, op0=ALU.add, op1=ALU.mult), [c.ps_b[pb]], [maskT_b])
        P.barrier()
        near = {}
        for g in (0, -1, -2):
            kb = qb + g
            near[kb if kb >= 0 else 16 + kb] = 2 + g
        NS = 3
        for hg in range(2):
            def emit_logits(ki, kb, hg=hg):
                ks = slice(kb * 128, (kb + 1) * 128)
                pb = c.nextps(3, 5)
                for a in range(2):
                    P.op("pe", lambda e, a=a, ks=ks, pb=pb, hg=hg: e.matmul(c.ps[pb][:], R_(kvT[:, a, ks]), R_(qblk[:, 8 * hg + a:8 * hg + 8:2, :]), start=(a == 0), stop=(a == 1)),
                         [kvT_b, qblk_b], [c.ps_b[pb]])
                return pb

            def emit_rest(ki, kb, pb, hg=hg):
                s = ki % NS
                lt3 = ltmp[s].rearrange("p (h t) -> p h t", h=4)
                P.op("dve", lambda e, kb=kb, pb=pb, lt3=lt3: e.tensor_tensor(out=lt3, in0=c.ps[pb][:].rearrange("p (h t) -> p h t", h=4), in1=maskT[:, kb:kb + 1, :].to_broadcast([128, 4, 128]), op=ALU.add),
                     [c.ps_b[pb], maskT_b], [ltmp_b[s]])
                if kb in near:
                    nb = near[kb]
                    P.op("dve", lambda e, lt3=lt3, nb=nb, hg=hg: e.tensor_tensor(out=lt3, in0=lt3, in1=biasT[:, 4 * hg:4 * hg + 4, nb, :], op=ALU.add), [ltmp_b[s], biasT_b], [ltmp_b[s]])
                P.op("act", lambda e, s=s: e.activation(out=R_(E[s]), in_=ltmp[s], func=AF.Exp, scale=1.0 / 16.0), [ltmp_b[s]], [E_b[s]])
                first, last = (ki == 0), (ki == len(kbs) - 1)
                for a in range(2):
                    P.op("pe", lambda e, a=a, kb=kb, s=s, first=first, last=last: e.matmul(c.ps[a][:], R_(kvtm[:, kb, a * 128:(a + 1) * 128]), R_(E[s]), start=first, stop=last),
                         [kvtm_b, E_b[s]], [c.ps_b[a]])
                P.op("pe", lambda e, s=s, first=first, last=last: e.matmul(c.ps[2][:], R_(onesR), R_(E[s]), start=first, stop=last), [id_b, E_b[s]], [c.ps_b[2]])

            pend = [emit_logits(0, kbs[0])]
            if len(kbs) > 1:
                pend.append(emit_logits(1, kbs[1]))
            for ki, kb in enumerate(kbs):
                if ki + 2 < len(kbs):
                    pend.append(emit_logits(ki + 2, kbs[ki + 2]))
                emit_rest(ki, kb, pend[ki])
            P.op("dve", lambda e: e.reciprocal(out=rden, in_=c.ps[2][:]), [c.ps_b[2], rtmp_b[0]], [rtmp_b[0]])
            for a in range(2):
                P.op("dve", lambda e, a=a: e.tensor_tensor(out=R_(onT[:, a, :]), in0=c.ps[a][:], in1=rden, op=ALU.mult), [c.ps_b[a], rtmp_b[0]], [onT_b])
            for hh in range(4):
                h = 4 * hg + hh
                pb = c.nextps(3, 5)
                for a in range(2):
                    P.op("pe", lambda e, a=a, h=h, hh=hh, pb=pb: e.matmul(c.ps[pb][:, 0:128], R_(wuv[:, h, a, :]), R_(onT[:, a, hh * 128:(hh + 1) * 128]), start=(a == 0), stop=(a == 1)),
                         [wuv_b, onT_b], [c.ps_b[pb]])
                s2 = hh % 2
                P.op("act", lambda e, pb=pb, s2=s2: e.copy(out=ltmp[s2][:, 0:128], in_=c.ps[pb][:, 0:128]), [c.ps_b[pb]], [ltmp_b[s2]])
                P.dma("sp", at_s[h, :, qs_], ltmp[s2][:, 0:128], [ltmp_b[s2]], [at_sb], ltmp_b[s2])
        P.barrier()
    P.barrier()
    for k in range(KC):
        P.dma("sp", X[:, k, :], xpark[:, k, :], [xpark_b], [X_b[2 * k], X_b[2 * k + 1]], None)
    mixedT = H[:, 0:2048].rearrange("p (a t) -> p a t", a=2); mixed_b = P.buf("dmixedT")
    wo = [R_(WR[:, 9216 + s * 2048:9216 + (s + 1) * 2048]) for s in range(2)]; wo_b = P.bufs(2, "wo")
    for hp in range(4):
        for a in range(2):
            P.dma("pool", R_(mixedT[:, a, :]), at_s[2 * hp + a], [at_sb], [mixed_b], mixed_b)
            if dbg is not None:
                P.dma("sp", dbg[8 + 2 * hp + a], mixedT[:, a, :], [mixed_b], [], mixed_b)
        outproj_partial(c, X, X_b, R_(mixedT), mixed_b, wout_d, [8 + 2 * hp, 8 + 2 * hp + 1], wo, wo_b)


def rel_bucket_np(rel):
    rel = np.asarray(rel, np.int32)
    ret = np.where(rel > 0, 16, 0)
    n = np.abs(rel)
    nf = np.maximum(n, 1).astype(np.float32)
    large = 8 + (np.log(nf / np.float32(8)) / np.float32(math.log(256 / 8)) * np.float32(8)).astype(np.int32)
    large = np.minimum(large, 15)
    return ret + np.where(n < 8, n, large)


def dsa_host_inputs(inp, k3res, b, q):
    kvT_all = np.zeros((2, 128, 4096), np.float32)
    kvtm_all = np.zeros((32, 128, 256), np.float32)
    kix_all = np.zeros((128, 4096), np.float32)
    slotadd = np.zeros((128, 3), np.float32)
    slotok = np.ones((128, 3), np.float32)
    srcs = [q] + [q - 1 - i for i in range(3)]
    for sl, sq_ in enumerate(srcs):
        if sq_ < 0:
            slotadd[:, sl - 1] = -BIG
            slotok[:, sl - 1] = 0.0
            continue
        r = k3res[b * 4 + sq_]
        kvT_all[:, :, sl * 1024:(sl + 1) * 1024] = r["kvT"]
        kvtm_all[sl * 8:(sl + 1) * 8] = r["kvtm"]
        kix_all[0:64, sl * 1024:(sl + 1) * 1024] = r["kixT"][0:64]
        kix_all[64:128, sl * 1024:(sl + 1) * 1024] = r["kixT"][0:64]
    t = np.arange(128)
    ok = (t[None, :] // 64) <= (t[:, None] // 64)
    admok = ok.astype(np.float32)
    admadd = np.where(ok, 0.0, -BIG).astype(np.float32)
    s_ = np.arange(128)[:, None, None]; nb = np.arange(3)[None, :, None]; tt = np.arange(128)[None, None, :]
    rel = (nb - 2) * 128 + s_ - tt
    bk = rel_bucket_np(rel)
    rb = inp["rel_bias"]
    bias_g = np.ascontiguousarray(rb[bk].transpose(0, 3, 1, 2))
    wuv = inp["dsa_w_uv"][0]
    return {"kvT_all": kvT_all, "kvtm_all": kvtm_all, "kixT_all": kix_all, "slotadd": slotadd, "slotok": slotok, "admadd": admadd, "admok": admok,
            "wuq": tile_fm(inp["dsa_w_uq"][0]), "wqi": tile_fm(inp["dsa_w_qidx"][0]), "cqg": vec_fm(inp["dsa_cq_g"][0]), "qng": vec_fm(inp["dsa_qnorm_g"][0]),
            "wuv": np.ascontiguousarray(wuv.reshape(8, 2, 128, 128).transpose(2, 0, 1, 3)), "bias_g": bias_g.astype(np.float32), "crow": bcast_rows(rb[15]),
            "ident2": np.eye(128, dtype=np.float32)}


_PROGS = {}


def _prog(name, fn):
    if name not in _PROGS:
        _PROGS[name] = fn()
    return _PROGS[name]


def _run(nc, ims):
    return run_bass_kernel_spmd(nc, ims, core_ids=list(range(NCORES))).results


def kernel_unfused(x, ln_mix_g, ln_ffn_g, w_ffn_gate, w_ffn_up, w_ffn_down, rel_bias,
           ev_w_in, ev_w_out, sgu_ln_g, sgu_ln_b, sgu_w_s, sgu_b_s,
           od_w_in, od_w_out, hgrn_lb, hgrn_norm_g, dsa_cq_g, dsa_ckv_g,
           dsa_w_uq, dsa_qnorm_g, dsa_w_qidx, dsa_w_uv):
    f = lambda a: np.ascontiguousarray(np.asarray(a, dtype=np.float32))
    x = f(x)
    inp = {"rel_bias": f(rel_bias), "dsa_w_uq": f(dsa_w_uq), "dsa_w_qidx": f(dsa_w_qidx), "dsa_cq_g": f(dsa_cq_g),
           "dsa_qnorm_g": f(dsa_qnorm_g), "dsa_w_uv": f(dsa_w_uv)}
    cores = [(c // 4, c % 4) for c in range(NCORES)]
    xs = [x_to_fm(x[b, q * T:(q + 1) * T]) for b, q in cores]
    ropes = [rope_tables(q) for q in range(4)]
    w_in = f(ev_w_in)[0]
    gm0 = vec_fm(f(ln_mix_g)[0])
    wtm1 = tile_tm(np.ascontiguousarray(w_in[:, 1024:3072]))
    kz1 = k1_tables()
    ims = [{"xT": xs[c], "g": gm0, "wtm": wtm1, "cos_tm": ropes[q][1][0], "sin_tm": ropes[q][1][1], "kz1": kz1} for c, (b, q) in enumerate(cores)]
    r1 = _run(_prog("k1", build_k1), ims)
    del wtm1
    wfm = tile_fm(np.ascontiguousarray(np.concatenate([w_in[:, 0:2048], w_in[:, 3072:5120]], 1)))
    wtm = tile_tm(np.ascontiguousarray(np.concatenate([w_in[:, 1024:3072], w_in[:, 5120:6144]], 1)))
    common = {"g_mix": gm0, "g_ffn": vec_fm(f(ln_ffn_g)[0]), "wfm": wfm, "wtm": wtm,
              "lng_b": bcast_rows(f(sgu_ln_g)[0]), "lnb_b": bcast_rows(f(sgu_ln_b)[0]),
              "wsT": np.ascontiguousarray(f(sgu_w_s)[0].transpose(2, 0, 1)), "bsb": bcast_rows(f(sgu_b_s)[0]),
              "wout": f(ev_w_out)[0], "wg": tile_fm(f(w_ffn_gate)[0]), "wu": tile_fm(f(w_ffn_up)[0]), "wd": f(w_ffn_down)[0]}
    ims = []
    for c, (b, q) in enumerate(cores):
        dmask, xi, kz, sw = k2_tables(q)
        im = dict(common)
        im.update({"xT": xs[c], "cosT": ropes[q][0][0], "sinT": ropes[q][0][1], "cos_tm": ropes[q][1][0], "sin_tm": ropes[q][1][1],
                   "dmask": dmask, "xi": xi, "kz": kz, "sw": sw, "sprev": np.stack([r1[b * 4 + i]["sloc"] for i in range(3)])})
        ims.append(im)
    r2 = _run(_prog("k2", build_k2), ims)
    del common, ims, wfm, wtm
    x1s = [r["out"] for r in r2]
    wfm, wtm = l1_weights(f(od_w_in)[0])
    ltri, mtri, ident = l1_tables()
    lb = f(hgrn_lb)
    common = {"g_mix": vec_fm(f(ln_mix_g)[1]), "wfm": wfm, "wtm": wtm, "lb0f": vec_fm(lb[0]), "lb1f": vec_fm(lb[1]),
              "lb0t": bcast_rows(lb[0]), "lb1t": bcast_rows(lb[1]), "ltri": ltri, "mtri": mtri, "ident": ident}
    ckvg = f(dsa_ckv_g)[0]
    ims = []
    for c in range(NCORES):
        im = dict(common)
        im.update({"xT": x1s[c], "ckvg_f": vec_fm(ckvg), "ckvg_t": bcast_rows(ckvg)})
        ims.append(im)
    r3 = _run(_prog("k3", build_k3), ims)
    common.update({"g_ffn": vec_fm(f(ln_ffn_g)[1]), "ngain": vec_fm(f(hgrn_norm_g)[0]), "wout": f(od_w_out)[0],
                   "wg": tile_fm(f(w_ffn_gate)[1]), "wu": tile_fm(f(w_ffn_up)[1]), "wd": f(w_ffn_down)[1]})
    ims = []
    for c, (b, q) in enumerate(cores):
        sprev = np.zeros((3, 8, 128, 128), np.float32)
        aprev = np.ones((128, 3, 8), np.float32)
        for i in range(3):
            if i < q:
                sprev[i] = r3[b * 4 + i]["sloc"]
                aprev[:, i, :] = r3[b * 4 + i]["aout"]
        im = dict(common)
        im.update({"xT": x1s[c], "sprev": sprev, "aprev": np.ascontiguousarray(aprev.reshape(128, 24))})
        im.update(dsa_host_inputs(inp, r3, b, q))
        ims.append(im)
    r4 = _run(_prog("k4", lambda: build_k4(dsa=True, dbg=False)), ims)
    out = np.zeros((2, 4096, D), np.float32)
    for c, (b, q) in enumerate(cores):
        out[b, q * T:(q + 1) * T] = fm_to_x(r4[c]["out"])
    return out


def dsa_prep(c, X, zfm, zfm_b, ztm, ztm_b, sq, sq_b, rstd, rstd_b, ckvg_f, ckvg_t, kvT_all, kvtm_all, kix_all, slot):
    P = c.P
    keys_b = P.buf("dsakeys")
    cs = slice(slot * 1024, (slot + 1) * 1024)
    Xf = X[:, :, :].rearrange("p k t -> p (k t)")
    gf, gf_b = load_small(c, "ckvgf_sb", ckvg_f, [128, 2])
    gt, gt_b = load_small(c, "ckvgt_sb", ckvg_t, [128, 256])
    ckT = Xf[:, 0:2048].rearrange("p (a t) -> p a t", a=2); ckT_b = P.buf("ckT")
    for a in range(2):
        P.dma("sp", ckT[:, a, :], zfm[27 + a], [zfm_b], [ckT_b], None)
    rstd_fm(c, ckT, ckT_b, 2, 256, sq, sq_b, rstd, rstd_b)
    for a in range(2):
        P.op("dve", lambda e, a=a: e.scalar_tensor_tensor(out=ckT[:, a, :], in0=ckT[:, a, :], scalar=gf[:, a:a + 1], in1=rstd[:, :], op0=ALU.mult, op1=ALU.mult),
             [ckT_b, gf_b, rstd_b[0], rstd_b[1]], [ckT_b])
        P.dma("sp", kvT_all[a][:, cs], ckT[:, a, :], [ckT_b], [keys_b], None)
    kx = Xf[:, 2048:3072]; kx_b = P.buf("kx")
    P.dma("sp", kx, zfm[29], [zfm_b], [kx_b], None)
    P.dma("sp", kix_all[0:64, cs], kx[0:64, :], [kx_b], [keys_b], None)
    P.dma("sp", kix_all[64:128, cs], kx[0:64, :], [kx_b], [keys_b], None)
    ck = Xf[:, 4096:6144].rearrange("p (j r) -> p j r", j=8); ck_b = P.buf("ck")
    junk = Xf[:, 6144:6400]; junk_b = P.buf("ckjunk")
    st = c.sb("ck_st", [128, 8, 2]); st_b = P.buf("ckst")
    P.op("dve", lambda e: e.memset(st[:], 0.0), [], [st_b])
    P.dma("sp", ck, ztm[:, :, 2048:2304].rearrange("j p r -> p j r"), [ztm_b], [ck_b], None)
    for j in range(8):
        P.op("act", lambda e, j=j: e.activation(out=junk, in_=ck[:, j, :], func=AF.Square, accum_out=st[:, j, 0:1]), [ck_b, st_b, junk_b], [junk_b, st_b])
        P.op("act", lambda e, j=j: e.activation(out=st[:, j, 1:2], in_=st[:, j, 0:1], func=AF.Sqrt, bias=c.eps[:, 0:1], scale=1.0 / 256), [st_b, c.eps_b], [st_b])
        P.op("dve", lambda e, j=j: e.reciprocal(out=st[:, j, 1:2], in_=st[:, j, 1:2]), [st_b], [st_b])
        P.op("dve", lambda e, j=j: e.scalar_tensor_tensor(out=ck[:, j, :], in0=ck[:, j, :], scalar=st[:, j, 1:2], in1=gt[:], op0=ALU.mult, op1=ALU.mult),
             [ck_b, st_b, gt_b], [ck_b])
    P.dma("sp", kvtm_all[slot * 8:(slot + 1) * 8].rearrange("k p r -> p k r"), ck, [ck_b], [keys_b], None)


def build_fused():
    c = Ctx("fused")
    P = c.P
    x_slots = c.din("x_slots", [4, 128, KC, T])
    out = c.dout("out", [128, KC, T])
    g_mix0 = c.din("g_mix0", [128, KC]); g_ffn0 = c.din("g_ffn0", [128, KC])
    g_mix1 = c.din("g_mix1", [128, KC]); g_ffn1 = c.din("g_ffn1", [128, KC])
    wfm0 = c.din("wfm0", [32, 128, KC * 128]); wtm0 = c.din("wtm0", [12, 128, KC * 256])
    wfm1 = c.din("wfm1", [L1_FM, 128, KC * 128]); wtm1 = c.din("wtm1", [L1_TM, 128, KC * 256])
    cosT = c.din("cosT", [4, 128, 1024]); sinT = c.din("sinT", [4, 128, 1024])
    cos_tm = c.din("cos_tm", [4, 128, 8, 128]); sin_tm = c.din("sin_tm", [4, 128, 8, 128])
    d0 = {"dmask": c.din("dmask", [128, 4, 128]), "xi": c.din("xi", [128, 4, 128]), "kz": c.din("kz", [128, 4]),
          "lng_b": c.din("lng_b", [128, 1024]), "lnb_b": c.din("lnb_b", [128, 1024]),
          "wsT": c.din("wsT", [128, 4, 128]), "bsb": c.din("bsb", [128, 4, 128])}
    wout0 = c.din("wout0", [D, D]); wout1 = c.din("wout1", [D, D])
    wg0 = c.din("wg0", [FC, 128, 2048]); wu0 = c.din("wu0", [FC, 128, 2048]); wd0 = c.din("wd0", [DFF, D])
    wg1 = c.din("wg1", [FC, 128, 2048]); wu1 = c.din("wu1", [FC, 128, 2048]); wd1 = c.din("wd1", [DFF, D])
    d1 = l1_common_inputs(c)
    d1["ngain"] = c.din("ngain", [128, 8])
    ckvg_f = c.din("ckvg_f", [128, 2]); ckvg_t = c.din("ckvg_t", [128, 256])
    zfm0 = c.dscr("zfm0", [32, 128, 1024]); ztm0 = c.dscr("ztm0", [8, 128, 3072])
    zfm1 = c.dscr("zfm1", [L1_FM, 128, 1024]); ztm1 = c.dscr("ztm1", [8, 128, L1_TM * 256])
    sstate = c.dscr("sstate", [4, 2, 128, 256]); hstate = c.dscr("hstate", [8, 128, 128])
    keys = {"kvT_all": c.dscr("kvT_all_s", [2, 128, 4096]), "kvtm_all": c.dscr("kvtm_all_s", [32, 128, 256]), "kixT_all": c.dscr("kix_all_s", [128, 4096])}
    X = c.sb("X", [128, KC, T]); X_b = P.bufs(2 * KC, "X")
    H = c.sb("H", [128, KC * T]); H_b = P.buf("H")
    WR = c.sb("WR", [128, 14336])
    FS = c.sb("FS", [128, 4096])
    Hr = H[:, :].bitcast(F32R).rearrange("p (k t) -> p k t", k=KC)
    gm0, gm0_b = load_small(c, "gm0_sb", g_mix0, [128, KC]); gf0, gf0_b = load_small(c, "gf0_sb", g_ffn0, [128, KC])
    gm1, gm1_b = load_small(c, "gm1_sb", g_mix1, [128, KC]); gf1, gf1_b = load_small(c, "gf1_sb", g_ffn1, [128, KC])
    sq = FS[:, 0:1024]; sq_b = P.bufs(2, "sq")
    rstd = FS[:, 1024:2048]; rstd_b = P.bufs(2, "rstd")
    ftmp = FS[:, 2048:3072]
    ftmp2 = [FS[:, 3072:3584], FS[:, 3584:4096]]
    pre_fm = list(range(8, 16)) + [27, 28, 29]
    pre_tm = list(range(0, 9))
    for p in range(4):
        P.barrier()
        load_X(c, X, X_b, x_slots[p])
        rmsnorm_fm(c, X, X_b, Hr, H_b, gm0, gm0_b, sq, sq_b, rstd, rstd_b)
        zfm_b, ztm_b = proj_phase(c, Hr, H_b, KC, wfm0, 32, zfm0, wtm0, 12, ztm0, WR, "p0")
        P.barrier()
        dd = dict(d0); dd.update({"cosT": cosT[p], "sinT": sinT[p], "cos_tm": cos_tm[p], "sin_tm": sin_tm[p]})
        retention_phase(c, X, X_b, H, WR, FS, zfm0, zfm_b, ztm0, ztm_b, dd, wout0, sq, sq_b, rstd, rstd_b, sstate=sstate, first=(p == 0))
        P.barrier()
        sgu_phase(c, X, X_b, H, WR, FS, zfm0, zfm_b, ztm0, ztm_b, dd, wout0, 24, 2048)
        P.barrier()
        rmsnorm_fm(c, X, X_b, Hr, H_b, gf0, gf0_b, sq, sq_b, rstd, rstd_b)
        ffn_phase(c, X, X_b, Hr, H_b, wg0, wu0, wd0, WR, ftmp, ftmp2)
        P.barrier()
        rmsnorm_fm(c, X, X_b, Hr, H_b, gm1, gm1_b, sq, sq_b, rstd, rstd_b)
        if p < 3:
            zfm_b, ztm_b = proj_phase(c, Hr, H_b, KC, wfm1, L1_FM, zfm1, wtm1, L1_TM, ztm1, WR, "p1", fm_list=pre_fm, tm_list=pre_tm)
            P.barrier()
            hgrn_phase(c, X, X_b, H, WR, FS, zfm1, zfm_b, ztm1, ztm_b, d1, None, sq, sq_b, rstd, rstd_b, outputs=False, hstate=hstate, first=(p == 0))
            P.barrier()
            dsa_prep(c, X, zfm1, zfm_b, ztm1, ztm_b, sq, sq_b, rstd, rstd_b, ckvg_f, ckvg_t, keys["kvT_all"], keys["kvtm_all"], keys["kixT_all"], 3 - p)
        else:
            zfm_b, ztm_b = proj_phase(c, Hr, H_b, KC, wfm1, L1_FM, zfm1, wtm1, L1_TM, ztm1, WR, "p1")
            P.barrier()
            hgrn_phase(c, X, X_b, H, WR, FS, zfm1, zfm_b, ztm1, ztm_b, d1, wout1, sq, sq_b, rstd, rstd_b, outputs=True, hstate=hstate, first=False)
            P.barrier()
            prep = lambda: dsa_prep(c, X, zfm1, zfm_b, ztm1, ztm_b, sq, sq_b, rstd, rstd_b, ckvg_f, ckvg_t, keys["kvT_all"], keys["kvtm_all"], keys["kixT_all"], 0)
            dsa_phase(c, X, X_b, H, WR, FS, zfm1, zfm_b, ztm1, ztm_b, wout1, sq, sq_b, rstd, rstd_b, None, keys=keys, prep=prep)
            P.barrier()
            rmsnorm_fm(c, X, X_b, Hr, H_b, gf1, gf1_b, sq, sq_b, rstd, rstd_b)
            ffn_phase(c, X, X_b, Hr, H_b, wg1, wu1, wd1, WR, ftmp, ftmp2)
            store_X(c, X, X_b, out)
    print("fused program:", {e: len(v) for e, v in P.q.items()}, "lanes", len(P.alllanes))
    return c.finish(X_b)


def kernel(x, ln_mix_g, ln_ffn_g, w_ffn_gate, w_ffn_up, w_ffn_down, rel_bias,
           ev_w_in, ev_w_out, sgu_ln_g, sgu_ln_b, sgu_w_s, sgu_b_s,
           od_w_in, od_w_out, hgrn_lb, hgrn_norm_g, dsa_cq_g, dsa_ckv_g,
           dsa_w_uq, dsa_qnorm_g, dsa_w_qidx, dsa_w_uv):
    f = lambda a: np.ascontiguousarray(np.asarray(a, dtype=np.float32))
    x = f(x)
    w_in0 = f(ev_w_in)[0]
    wfm1, wtm1 = l1_weights(f(od_w_in)[0])
    ltri, mtri, ident = l1_tables()
    lb = f(hgrn_lb)
    ckvg = f(dsa_ckv_g)[0]
    rb = f(rel_bias)
    dmask, xi, kz, _ = k2_tables(0)
    common = {
        "g_mix0": vec_fm(f(ln_mix_g)[0]), "g_ffn0": vec_fm(f(ln_ffn_g)[0]), "g_mix1": vec_fm(f(ln_mix_g)[1]), "g_ffn1": vec_fm(f(ln_ffn_g)[1]),
        "wfm0": tile_fm(np.ascontiguousarray(np.concatenate([w_in0[:, 0:2048], w_in0[:, 3072:5120]], 1))),
        "wtm0": tile_tm(np.ascontiguousarray(np.concatenate([w_in0[:, 1024:3072], w_in0[:, 5120:6144]], 1))),
        "wfm1": wfm1, "wtm1": wtm1, "dmask": dmask, "xi": xi, "kz": kz,
        "lng_b": bcast_rows(f(sgu_ln_g)[0]), "lnb_b": bcast_rows(f(sgu_ln_b)[0]),
        "wsT": np.ascontiguousarray(f(sgu_w_s)[0].transpose(2, 0, 1)), "bsb": bcast_rows(f(sgu_b_s)[0]),
        "wout0": f(ev_w_out)[0], "wout1": f(od_w_out)[0],
        "wg0": tile_fm(f(w_ffn_gate)[0]), "wu0": tile_fm(f(w_ffn_up)[0]), "wd0": f(w_ffn_down)[0],
        "wg1": tile_fm(f(w_ffn_gate)[1]), "wu1": tile_fm(f(w_ffn_up)[1]), "wd1": f(w_ffn_down)[1],
        "lb0f": vec_fm(lb[0]), "lb1f": vec_fm(lb[1]), "lb0t": bcast_rows(lb[0]), "lb1t": bcast_rows(lb[1]),
        "ltri": ltri, "mtri": mtri, "ident": ident, "ngain": vec_fm(f(hgrn_norm_g)[0]),
        "ckvg_f": vec_fm(ckvg), "ckvg_t": bcast_rows(ckvg),
        "wuq": tile_fm(f(dsa_w_uq)[0]), "wqi": tile_fm(f(dsa_w_qidx)[0]), "cqg": vec_fm(f(dsa_cq_g)[0]), "qng": vec_fm(f(dsa_qnorm_g)[0]),
        "wuv": np.ascontiguousarray(f(dsa_w_uv)[0].reshape(8, 2, 128, 128).transpose(2, 0, 1, 3)),
        "crow": bcast_rows(rb[15]), "ident2": np.eye(128, dtype=np.float32),
    }
    t = np.arange(128)
    ok = (t[None, :] // 64) <= (t[:, None] // 64)
    common["admok"] = ok.astype(np.float32)
    common["admadd"] = np.where(ok, 0.0, -BIG).astype(np.float32)
    s_ = np.arange(128)[:, None, None]; nb = np.arange(3)[None, :, None]; tt = np.arange(128)[None, None, :]
    bk = rel_bucket_np((nb - 2) * 128 + s_ - tt)
    common["bias_g"] = np.ascontiguousarray(rb[bk].transpose(0, 3, 1, 2)).astype(np.float32)
    ropes = [rope_tables(q) for q in range(4)]
    ims = []
    for c in range(NCORES):
        b, q = c // 4, c % 4
        xs = np.zeros((4, 128, KC, T), np.float32)
        slotadd = np.zeros((128, 3), np.float32); slotok = np.ones((128, 3), np.float32)
        qs = [max(q - 3 + p, 0) for p in range(4)]
        for p in range(4):
            qq = q - 3 + p
            if qq >= 0:
                xs[p] = x_to_fm(x[b, qq * T:(qq + 1) * T])
            else:
                slotadd[:, 3 - p - 1] = -BIG
                slotok[:, 3 - p - 1] = 0.0
        im = dict(common)
        im.update({"x_slots": xs, "slotadd": slotadd, "slotok": slotok,
                   "cosT": np.stack([ropes[k][0][0] for k in qs]), "sinT": np.stack([ropes[k][0][1] for k in qs]),
                   "cos_tm": np.stack([ropes[k][1][0] for k in qs]), "sin_tm": np.stack([ropes[k][1][1] for k in qs])})
        ims.append(im)
    res = _run(_prog("fused", build_fused), ims)
    out = np.zeros((2, 4096, D), np.float32)
    for c in range(NCORES):
        b, q = c // 4, c % 4
        out[b, q * T:(q + 1) * T] = fm_to_x(res[c]["out"])
    return out
```

```python
import math
from contextlib import ExitStack

import numpy as np
import concourse.bass as bass
import concourse.mybir as mybir
from concourse.bass_utils import run_bass_kernel_spmd

F32 = mybir.dt.float32
F32R = mybir.dt.float32r
ALU = mybir.AluOpType
AF = mybir.ActivationFunctionType
AX = mybir.AxisListType

NCORES = 8
T = 1024
D = 2048
KC = D // 128
DFF = 5632
FC = DFF // 128
EPS = 1e-6
SELF_SYNC = True


class Lane:
    __slots__ = ("name", "sem", "cnt", "step")

    def __init__(self, name, sem, step):
        self.name, self.sem, self.cnt, self.step = name, sem, 0, step


class Buf:
    __slots__ = ("name", "lw", "rd", "dlane")

    def __init__(self, name):
        self.name = name
        self.lw = None
        self.rd = {}
        self.dlane = None


class Prog:
    ENGS = ("pe", "act", "dve", "pool", "sp")

    def __init__(self, nc, stack, self_sync=True):
        self.nc = nc
        self.stack = stack
        self.q = {e: [] for e in self.ENGS}
        self.lanes = {}
        for e in ("pe", "act", "dve", "pool"):
            self.lanes[e] = Lane(e, stack.enter_context(nc.semaphore("s_" + e)), 1)
        self.alllanes = list(self.lanes.values())
        self.waited = {e: {} for e in self.ENGS}
        self.self_sync = self_sync
        self.nbuf = 0
        self.ndl = 0
        self.bufmap = {}
        self.shared = {}

    def buf(self, name=None):
        self.nbuf += 1
        name = name or f"b{self.nbuf}"
        if name not in self.bufmap:
            self.bufmap[name] = Buf(name)
        return self.bufmap[name]

    def bufs(self, n, name="b"):
        return [self.buf(f"{name}{i}") for i in range(n)]

    def dlane(self, b):
        if b.dlane is None:
            self.ndl += 1
            b.dlane = Lane("d_" + b.name, self.stack.enter_context(self.nc.semaphore(f"sd{self.ndl}")), 16)
            self.alllanes.append(b.dlane)
        return b.dlane

    def _deps(self, eng, reads, writes):
        deps = {}
        for b in reads:
            if b.lw is not None and deps.get(b.lw[0], 0) < b.lw[1]:
                deps[b.lw[0]] = b.lw[1]
        for b in writes:
            if b.lw is not None and deps.get(b.lw[0], 0) < b.lw[1]:
                deps[b.lw[0]] = b.lw[1]
            for ln, v in b.rd.items():
                if deps.get(ln, 0) < v:
                    deps[ln] = v
        out = []
        w = self.waited[eng]
        for ln, v in deps.items():
            if ln.name == eng and (eng == "pe" or not self.self_sync):
                continue
            if w.get(ln, 0) >= v:
                continue
            w[ln] = v
            out.append((ln, v))
        return out

    def _mark(self, lane, val, reads, writes):
        for b in reads:
            if b.rd.get(lane, 0) < val:
                b.rd[lane] = val
        for b in writes:
            b.lw = (lane, val)
            b.rd = {}

    def op(self, eng, fn, reads=(), writes=()):
        waits = self._deps(eng, reads, writes)
        lane = self.lanes[eng]
        lane.cnt += 1
        self.q[eng].append((fn, waits, lane, 1))
        self._mark(lane, lane.cnt, reads, writes)

    def dma(self, q, out_ap, in_ap, reads, writes, lane_buf=None):
        waits = self._deps(q, reads, writes)
        if lane_buf is None:
            if q not in self.shared:
                self.shared[q] = self.buf("shared_" + q)
            lane = self.dlane(self.shared[q])
            if lane.cnt > 0 and self.waited[q].get(lane, 0) < lane.cnt:
                self.waited[q][lane] = lane.cnt
                waits.append((lane, lane.cnt))
        else:
            lane = self.dlane(lane_buf)
        lane.cnt += 16

        def fn(e, out_ap=out_ap, in_ap=in_ap):
            return e.dma_start(out=out_ap, in_=in_ap)
        self.q[q].append((fn, waits, lane, 16))
        self._mark(lane, lane.cnt, reads, writes)

    def barrier(self):
        for e in self.ENGS:
            waits = []
            w = self.waited[e]
            for ln in self.alllanes:
                if ln.cnt > 0 and w.get(ln, 0) < ln.cnt and not (ln.name == e and e == "pe"):
                    w[ln] = ln.cnt
                    waits.append((ln, ln.cnt))
            if waits:
                self.q[e].append((None, waits, None, 0))

    def emit(self):
        nc = self.nc
        engmap = {"pe": "tensor", "act": "scalar", "dve": "vector", "pool": "gpsimd", "sp": "sync"}
        with nc.Block() as block:
            for e in self.ENGS:
                items = self.q[e]

                def body(engine, items=items):
                    for fn, waits, lane, step in items:
                        for ln, v in waits:
                            engine.wait_ge(ln.sem, v)
                        if fn is not None:
                            fn(engine).then_inc(lane.sem, step)
                if items:
                    getattr(block, engmap[e])(body)


class Ctx:
    def __init__(self, name):
        self.nc = bass.Bass("TRN2", target_bir_lowering=False)
        self.st = ExitStack()
        self.P = Prog(self.nc, self.st, self_sync=SELF_SYNC)
        self.name = name
        self.nt = 0
        nc, P = self.nc, self.P
        self.ps = [self.st.enter_context(nc.psum_tensor(f"ps{i}", [128, 512], F32)) for i in range(8)]
        self.ps_b = P.bufs(8, "ps")
        self.ones = self.sb("ones", [128, 128])
        self.ones_b = P.buf("ones")
        P.op("dve", lambda e: e.memset(self.ones[:], 1.0), [], [self.ones_b])
        self.eps = self.sb("epsc", [128, 1])
        self.eps_b = P.buf("eps")
        P.op("dve", lambda e: e.memset(self.eps[:], EPS), [], [self.eps_b])
        self.psrot = 0

    def sb(self, name, shape, dt=F32):
        if not hasattr(self, "_sbmap"):
            self._sbmap = {}
        if name not in self._sbmap:
            self._sbmap[name] = self.st.enter_context(self.nc.sbuf_tensor(name, shape, dt))
        return self._sbmap[name]

    def din(self, name, shape, dt=F32):
        return self.nc.dram_tensor(name, list(shape), dt, kind="ExternalInput").ap()

    def dout(self, name, shape, dt=F32):
        return self.nc.dram_tensor(name, list(shape), dt, kind="ExternalOutput").ap()

    def dscr(self, name, shape, dt=F32):
        return self.nc.dram_tensor(name, list(shape), dt, kind="Internal").ap()

    def finish(self, final_bufs):
        P = self.P
        waits = P._deps("sp", final_bufs, final_bufs)
        P.q["sp"].append((None, waits, None, 0))
        P.emit()
        self.st.close()
        return self.nc

    def nextps(self, lo=4, n=4):
        i = lo + self.psrot % n
        self.psrot += 1
        return i


def load_X(c, X, X_b, xT_d, q="sp"):
    P = c.P
    for g in range(4):
        P.dma(q, X[:, 4 * g:4 * g + 4, :], xT_d[:, 4 * g:4 * g + 4, :], [], X_b[8 * g:8 * g + 8], X_b[8 * g])


def store_X(c, X, X_b, out_d, q="sp"):
    P = c.P
    for g in range(4):
        P.dma(q, out_d[:, 4 * g:4 * g + 4, :], X[:, 4 * g:4 * g + 4, :], X_b[8 * g:8 * g + 8], [], X_b[8 * g])


def rmsnorm_fm(c, X, X_b, Hr, H_b, g_sb, g_b, sq, sq_b, rstd, rstd_b):
    P = c.P
    for half in range(2):
        hs = slice(half * 512, (half + 1) * 512)
        pb = half
        for k in range(KC):
            sl = (half * KC + k) % 2
            ss = slice(sl * 512, (sl + 1) * 512)
            P.op("act", lambda e, k=k, hs=hs, ss=ss: e.activation(out=sq[:, ss], in_=X[:, k, hs], func=AF.Square),
                 [X_b[2 * k + half]], [sq_b[sl]])
            P.op("pe", lambda e, k=k, ss=ss, pb=pb: e.matmul(c.ps[pb][:], c.ones[:], sq[:, ss], start=(k == 0), stop=(k == KC - 1)),
                 [c.ones_b, sq_b[sl]], [c.ps_b[pb]])
        P.op("act", lambda e, hs=hs, pb=pb: e.activation(out=rstd[:, hs], in_=c.ps[pb][:], func=AF.Sqrt, bias=c.eps[:, 0:1], scale=1.0 / D),
             [c.ps_b[pb], c.eps_b], [rstd_b[half]])
        P.op("dve", lambda e, hs=hs: e.reciprocal(out=rstd[:, hs], in_=rstd[:, hs]), [rstd_b[half]], [rstd_b[half]])
        for k in range(KC):
            P.op("dve", lambda e, k=k, hs=hs: e.scalar_tensor_tensor(out=Hr[:, k, hs], in0=X[:, k, hs], scalar=g_sb[:, k:k + 1], in1=rstd[:, hs],
                                                                     op0=ALU.mult, op1=ALU.mult),
                 [X_b[2 * k + half], g_b, rstd_b[half]], [H_b])


def ffn_phase(c, X, X_b, Hr, H_b, wg_d, wu_d, wd_d, WR, tmp=None, tmp2=None):
    P = c.P
    wgu = [(WR[:, s * 4096:s * 4096 + 2048].bitcast(F32R), WR[:, s * 4096 + 2048:(s + 1) * 4096].bitcast(F32R)) for s in range(2)]
    wgu_b = [(P.buf(f"wg{s}"), P.buf(f"wu{s}")) for s in range(2)]
    wd = [WR[:, 8192 + s * 2048:8192 + (s + 1) * 2048].bitcast(F32R) for s in range(2)]
    wd_b = P.bufs(2, "wd")
    act = [WR[:, 12288 + s * 1024:12288 + (s + 1) * 1024] for s in range(2)]
    act_b = P.bufs(2, "actt")
    if tmp is None:
        tmp = c.sb("ffn_tmp", [128, 1024])
    tmp_b = P.bufs(2, "ffntmp")
    tmp2_b = P.bufs(2, "ffntmp2")

    def issue_loads(f):
        s = f % 2
        P.dma("pool", wgu[s][0], wg_d[f], [], [wgu_b[s][0]], wgu_b[s][0])
        P.dma("pool", wgu[s][1], wu_d[f], [], [wgu_b[s][1]], wgu_b[s][1])
        P.dma("pool", wd[s], wd_d[f * 128:(f + 1) * 128, :], [], [wd_b[s]], wd_b[s])

    issue_loads(0)
    for f in range(FC):
        s = f % 2
        if f + 1 < FC:
            issue_loads(f + 1)
        wg3 = wgu[s][0].rearrange("p (k j) -> p k j", k=KC)
        wu3 = wgu[s][1].rearrange("p (k j) -> p k j", k=KC)
        for half in range(2):
            hs = slice(half * 512, (half + 1) * 512)
            for (w3, wb, pb) in ((wg3, wgu_b[s][0], half), (wu3, wgu_b[s][1], 2 + half)):
                for k in range(KC):
                    P.op("pe", lambda e, w3=w3, k=k, hs=hs, pb=pb: e.matmul(c.ps[pb][:], w3[:, k, :], Hr[:, k, hs], start=(k == 0), stop=(k == KC - 1)),
                         [wb, H_b], [c.ps_b[pb]])
        for half in range(2):
            hs = slice(half * 512, (half + 1) * 512)
            P.op("act", lambda e, hs=hs, half=half: e.activation(out=tmp[:, hs], in_=c.ps[half][:], func=AF.Silu),
                 [c.ps_b[half]], [tmp_b[half]])
            P.op("dve", lambda e, hs=hs, half=half, s=s: e.tensor_tensor(out=act[s][:, hs].bitcast(F32R), in0=tmp[:, hs], in1=c.ps[2 + half][:], op=ALU.mult),
                 [tmp_b[half], c.ps_b[2 + half]], [act_b[s]])
        for n in range(KC):
            for half in range(2):
                hs = slice(half * 512, (half + 1) * 512)
                pb = c.nextps()
                P.op("pe", lambda e, n=n, hs=hs, pb=pb, s=s: e.matmul(c.ps[pb][:], wd[s][:, n * 128:(n + 1) * 128], act[s][:, hs].bitcast(F32R), start=True, stop=True),
                     [wd_b[s], act_b[s]], [c.ps_b[pb]])
                idx = 2 * n + half
                if tmp2 is not None and idx % 3 == 2:
                    s2 = (idx // 3) % 2
                    P.op("act", lambda e, pb=pb, s2=s2: e.copy(out=tmp2[s2], in_=c.ps[pb][:]), [c.ps_b[pb]], [tmp2_b[s2]])
                    P.op("pool", lambda e, n=n, hs=hs, s2=s2: e.tensor_tensor(out=X[:, n, hs], in0=X[:, n, hs], in1=tmp2[s2], op=ALU.add),
                         [X_b[2 * n + half], tmp2_b[s2]], [X_b[2 * n + half]])
                else:
                    P.op("dve", lambda e, n=n, hs=hs, pb=pb: e.tensor_tensor(out=X[:, n, hs], in0=X[:, n, hs], in1=c.ps[pb][:], op=ALU.add),
                         [X_b[2 * n + half], c.ps_b[pb]], [X_b[2 * n + half]])


def load_small(c, name, dram_ap, shape, q="sp", dt=F32):
    t = c.sb(name, shape, dt)
    b = c.P.buf(name)
    c.P.dma(q, t[:], dram_ap, [], [b], None)
    return t, b


def build_ffn_test():
    c = Ctx("ffn")
    P = c.P
    xT = c.din("xT", [128, KC, T])
    g = c.din("g", [128, KC])
    wg = c.din("wg", [FC, 128, 2048])
    wu = c.din("wu", [FC, 128, 2048])
    wd = c.din("wd", [DFF, D])
    out = c.dout("out", [128, KC, T])
    X = c.sb("X", [128, KC, T]); X_b = P.bufs(2 * KC, "X")
    H = c.sb("H", [128, KC * T]); H_b = P.buf("H")
    WR = c.sb("WR", [128, 14336])
    Hr = H[:, :].bitcast(F32R).rearrange("p (k t) -> p k t", k=KC)
    g_sb, g_b = load_small(c, "g_sb", g, [128, KC])
    sq = c.sb("sq", [128, T]); sq_b = P.bufs(2, "sq")
    rstd = c.sb("rstd", [128, T]); rstd_b = P.bufs(2, "rstd")
    load_X(c, X, X_b, xT)
    rmsnorm_fm(c, X, X_b, Hr, H_b, g_sb, g_b, sq, sq_b, rstd, rstd_b)
    t2 = c.sb("ffn_tmp2", [128, 1024])
    ffn_phase(c, X, X_b, Hr, H_b, wg, wu, wd, WR, None, [t2[:, 0:512], t2[:, 512:1024]])
    store_X(c, X, X_b, out)
    return c.finish(X_b)


def tile_fm(W):
    K, N = W.shape
    return np.ascontiguousarray(W.reshape(K // 128, 128, N // 128, 128).transpose(2, 1, 0, 3)).reshape(N // 128, 128, (K // 128) * 128)


def tile_tm(W, cw=256):
    K, N = W.shape
    return np.ascontiguousarray(W.reshape(K // 128, 128, N // cw, cw).transpose(2, 1, 0, 3)).reshape(N // cw, 128, (K // 128) * cw)


def x_to_fm(xs):
    return np.ascontiguousarray(xs.T.reshape(KC, 128, xs.shape[0]).transpose(1, 0, 2))


def fm_to_x(o):
    return np.ascontiguousarray(o.transpose(1, 0, 2).reshape(D, o.shape[2]).T)


def vec_fm(g):
    return np.ascontiguousarray(g.reshape(-1, 128).T)


def proj_phase(c, Hr, H_b, kc, wfm_d, n_fm, zfm_d, wtm_d, n_tm, ztm_d, WR, tag, fm_list=None, tm_list=None):
    P = c.P
    fm_list = list(range(n_fm)) if fm_list is None else fm_list
    tm_list = list(range(n_tm)) if tm_list is None else tm_list
    zfm_b = P.buf(tag + "zfm")
    ztm_b = P.buf(tag + "ztm")
    wf = [WR[:, s * 2048:s * 2048 + kc * 128].bitcast(F32R).rearrange("p (k j) -> p k j", k=kc) for s in range(2)]
    wf_b = P.bufs(2, "pjwf")
    wt = [WR[:, 4096 + s * 4096:4096 + s * 4096 + kc * 256].bitcast(F32R).rearrange("p (k j) -> p k j", k=kc) for s in range(2)]
    wt_b = P.bufs(2, "pjwt")
    stf = [WR[:, 12288 + s * 1024:12288 + (s + 1) * 1024] for s in range(2)]
    stf_b = P.bufs(2, "pjstf")
    stt_t = c.sb("pjstt", [128, 2, 256])
    stt_b = P.bufs(2, "pjstt")
    if fm_list:
        P.dma("pool", wf[0], wfm_d[fm_list[0]].rearrange("p (k j) -> p k j", k=kc), [], [wf_b[0]], wf_b[0])
    for ni, n in enumerate(fm_list):
        s = ni % 2
        if ni + 1 < len(fm_list):
            P.dma("pool", wf[1 - s], wfm_d[fm_list[ni + 1]].rearrange("p (k j) -> p k j", k=kc), [], [wf_b[1 - s]], wf_b[1 - s])
        for half in range(2):
            hs = slice(half * 512, (half + 1) * 512)
            pb = c.nextps(0, 8)
            for k in range(kc):
                P.op("pe", lambda e, s=s, k=k, hs=hs, pb=pb: e.matmul(c.ps[pb][:], wf[s][:, k, :], Hr[:, k, hs], start=(k == 0), stop=(k == kc - 1)),
                     [wf_b[s], H_b], [c.ps_b[pb]])
            P.op("act", lambda e, s=s, hs=hs, pb=pb: e.copy(out=stf[s][:, hs].bitcast(F32R), in_=c.ps[pb][:]), [c.ps_b[pb]], [stf_b[s]])
        P.dma("sp", zfm_d[n], stf[s], [stf_b[s]], [zfm_b], stf_b[s])
    if tm_list:
        P.dma("pool", wt[0], wtm_d[tm_list[0]].rearrange("p (k j) -> p k j", k=kc), [], [wt_b[0]], wt_b[0])
    cnt = 0
    for gi, g in enumerate(tm_list):
        s = gi % 2
        if gi + 1 < len(tm_list):
            P.dma("pool", wt[1 - s], wtm_d[tm_list[gi + 1]].rearrange("p (k j) -> p k j", k=kc), [], [wt_b[1 - s]], wt_b[1 - s])
        for j in range(8):
            pb = c.nextps(0, 8)
            for k in range(kc):
                P.op("pe", lambda e, s=s, k=k, j=j, pb=pb: e.matmul(c.ps[pb][:, 0:256], Hr[:, k, j * 128:(j + 1) * 128], wt[s][:, k, :], start=(k == 0), stop=(k == kc - 1)),
                     [wt_b[s], H_b], [c.ps_b[pb]])
            ss = cnt % 2
            cnt += 1
            P.op("dve", lambda e, ss=ss, pb=pb: e.tensor_copy(out=stt_t[:, ss, :], in_=c.ps[pb][:, 0:256]), [c.ps_b[pb]], [stt_b[ss]])
            P.dma("sp", ztm_d[j, :, g * 256:(g + 1) * 256], stt_t[:, ss, :], [stt_b[ss]], [ztm_b], stt_b[ss])
    return zfm_b, ztm_b


def rotary_tm(c, raw, raw_b, outk, outk_b, cos_tm, sin_tm, tab_bs, t3, t2, tmp_b):
    P = c.P
    A, B = raw[:, :, 0:128], raw[:, :, 128:256]
    P.op("dve", lambda e: e.tensor_tensor(out=t3, in0=A, in1=cos_tm, op=ALU.mult), [raw_b] + tab_bs, [tmp_b[0]])
    P.op("dve", lambda e: e.tensor_tensor(out=t2, in0=B, in1=sin_tm, op=ALU.mult), [raw_b] + tab_bs, [tmp_b[1]])
    P.op("dve", lambda e: e.tensor_tensor(out=outk[:, :, 0:128], in0=t3, in1=t2, op=ALU.subtract), [tmp_b[0], tmp_b[1]], [outk_b])
    P.op("dve", lambda e: e.tensor_tensor(out=t3, in0=A, in1=sin_tm, op=ALU.mult), [raw_b] + tab_bs, [tmp_b[0]])
    P.op("dve", lambda e: e.tensor_tensor(out=t2, in0=B, in1=cos_tm, op=ALU.mult), [raw_b] + tab_bs, [tmp_b[1]])
    P.op("dve", lambda e: e.tensor_tensor(out=outk[:, :, 128:256], in0=t3, in1=t2, op=ALU.add), [tmp_b[0], tmp_b[1]], [outk_b])


RET_GAMMA = [1.0 - 2.0 ** (-5.0 - h) for h in range(4)]


def build_k1():
    c = Ctx("k1")
    P = c.P
    xT = c.din("xT", [128, KC, T])
    g = c.din("g", [128, KC])
    wtm = c.din("wtm", [8, 128, KC * 256])
    cos_d = c.din("cos_tm", [128, 8, 128])
    sin_d = c.din("sin_tm", [128, 8, 128])
    kz1_d = c.din("kz1", [128, 32])
    sloc = c.dout("sloc", [4, 2, 128, 256])
    ztm = c.dscr("ztm1", [8, 128, 2048])
    X = c.sb("X", [128, KC, T]); X_b = P.bufs(2 * KC, "X")
    H = c.sb("H", [128, KC * T]); H_b = P.buf("H")
    WR = c.sb("WR", [128, 14336])
    Hr = H[:, :].bitcast(F32R).rearrange("p (k t) -> p k t", k=KC)
    g_sb, g_b = load_small(c, "g_sb", g, [128, KC])
    cos_tm, cb = load_small(c, "cos_sb", cos_d, [128, 8, 128])
    sin_tm, sbb = load_small(c, "sin_sb", sin_d, [128, 8, 128])
    kz1, kz1_b = load_small(c, "kz1_sb", kz1_d, [128, 32])
    tab_b = P.buf("tabs")
    sq = c.sb("sq", [128, T]); sq_b = P.bufs(2, "sq")
    rstd = c.sb("rstd", [128, T]); rstd_b = P.bufs(2, "rstd")
    load_X(c, X, X_b, xT)
    rmsnorm_fm(c, X, X_b, Hr, H_b, g_sb, g_b, sq, sq_b, rstd, rstd_b)
    _, ztm_b = proj_phase(c, Hr, H_b, KC, None, 0, None, wtm, 8, ztm, WR, "k1")
    P.barrier()
    Xf = X[:, :, :].rearrange("p k t -> p (k t)")
    raw = Xf[:, 0:2048].rearrange("p (j c) -> p j c", j=8); raw_b = P.buf("raw")
    t3 = Xf[:, 6144:7168].rearrange("p (j c) -> p j c", j=8)
    t2 = Xf[:, 7168:8192].rearrange("p (j c) -> p j c", j=8)
    tmp_b = P.bufs(2, "rt")
    so = Xf[:, 8192:8704].rearrange("p (s c) -> p s c", s=2); so_b = P.bufs(2, "so")
    ktr = H[:, 0:2048].bitcast(F32R).rearrange("p (j c) -> p j c", j=8); kt_b = P.buf("kt")
    vt = H[:, 2048:4096].bitcast(F32R).rearrange("p (j c) -> p j c", j=8); vt_b = P.buf("vt")
    for h in range(4):
        P.dma("sp", raw, ztm[:, :, h * 256:(h + 1) * 256].rearrange("j p c -> p j c"), [ztm_b], [raw_b], raw_b)
        P.dma("pool", vt, ztm[:, :, 1024 + h * 256:1024 + (h + 1) * 256].rearrange("j p c -> p j c"), [ztm_b], [vt_b], vt_b)
        for j in range(8):
            P.op("dve", lambda e, j=j, h=h: e.tensor_scalar(out=raw[:, j, :], in0=raw[:, j, :], scalar1=kz1[:, j * 4 + h:j * 4 + h + 1], scalar2=None, op0=ALU.mult),
                 [raw_b, kz1_b], [raw_b])
        rotary_tm(c, raw, raw_b, ktr, kt_b, cos_tm[:], sin_tm[:], [cb, sbb], t3, t2, tmp_b)
        for dc in range(2):
            pb = c.nextps(0, 8)
            for j in range(8):
                P.op("pe", lambda e, j=j, dc=dc, pb=pb: e.matmul(c.ps[pb][:, 0:256], ktr[:, j, dc * 128:(dc + 1) * 128], vt[:, j, :], start=(j == 0), stop=(j == 7)),
                     [kt_b, vt_b], [c.ps_b[pb]])
            P.op("act", lambda e, dc=dc, pb=pb: e.copy(out=so[:, dc, :], in_=c.ps[pb][:, 0:256]), [c.ps_b[pb]], [so_b[dc]])
            P.dma("sp", sloc[h, dc], so[:, dc, :], [so_b[dc]], [], so_b[dc])
    return c.finish(so_b)


def rope_tables(qtr):
    pos = (np.arange(T, dtype=np.float32) + np.float32(qtr * T))
    inv = (np.float32(10000.0) ** (-(np.arange(0, 256, 2, dtype=np.float32) / np.float32(256)))).astype(np.float32)
    ang = (pos[:, None] * inv[None, :]).astype(np.float32)
    cos, sin = np.cos(ang).astype(np.float32), np.sin(ang).astype(np.float32)
    fm = (np.ascontiguousarray(cos.T), np.ascontiguousarray(sin.T))
    tm = (np.ascontiguousarray(cos.reshape(8, 128, 128).transpose(1, 0, 2)), np.ascontiguousarray(sin.reshape(8, 128, 128).transpose(1, 0, 2)))
    return fm, tm


def k1_tables():
    kz1 = np.zeros((128, 8, 4), np.float64)
    tb = np.arange(128)
    for j in range(8):
        for h in range(4):
            kz1[:, j, h] = RET_GAMMA[h] ** (1023 - (128 * j + tb)) / 16.0
    return kz1.reshape(128, 32).astype(np.float32)


def R_(ap):
    return ap.bitcast(F32R)


def outproj_partial(c, X, X_b, mixedT, mixed_b, wout_d, rows, wo, wo_b):
    P = c.P
    for mi, r in enumerate(rows):
        P.dma("pool", wo[mi], wout_d[r * 128:(r + 1) * 128, :], [], [wo_b[mi]], wo_b[mi])
    for n in range(KC):
        for half in range(2):
            hs = slice(half * 512, (half + 1) * 512)
            pb = c.nextps(0, 8)
            for mi in range(len(rows)):
                P.op("pe", lambda e, mi=mi, n=n, hs=hs, pb=pb: e.matmul(c.ps[pb][:], wo[mi][:, n * 128:(n + 1) * 128], mixedT[:, mi, hs],
                                                                        start=(mi == 0), stop=(mi == len(rows) - 1)),
                     [wo_b[mi], mixed_b], [c.ps_b[pb]])
            P.op("dve", lambda e, n=n, hs=hs, pb=pb: e.tensor_tensor(out=X[:, n, hs], in0=X[:, n, hs], in1=c.ps[pb][:], op=ALU.add),
                 [X_b[2 * n + half], c.ps_b[pb]], [X_b[2 * n + half]])


def groupnorm_gate(c, src, src_b, nchunk, width, gate, gate_b, gain, gain_b, outT, out_b, sq, sq_b, rstd, rstd_b):
    P = c.P
    for half in range(2):
        hs = slice(half * 512, (half + 1) * 512)
        pb = c.nextps(0, 8)
        for i in range(nchunk):
            sl = i % 2
            ss = slice(sl * 512, (sl + 1) * 512)
            P.op("act", lambda e, i=i, hs=hs, ss=ss: e.activation(out=sq[:, ss], in_=src[:, i, hs], func=AF.Square), [src_b], [sq_b[sl]])
            P.op("pe", lambda e, i=i, ss=ss, pb=pb: e.matmul(c.ps[pb][:], c.ones[:], sq[:, ss], start=(i == 0), stop=(i == nchunk - 1)),
                 [c.ones_b, sq_b[sl]], [c.ps_b[pb]])
        P.op("act", lambda e, hs=hs, pb=pb: e.activation(out=rstd[:, hs], in_=c.ps[pb][:], func=AF.Sqrt, bias=c.eps[:, 0:1], scale=1.0 / width),
             [c.ps_b[pb], c.eps_b], [rstd_b[half]])
        P.op("dve", lambda e, hs=hs: e.reciprocal(out=rstd[:, hs], in_=rstd[:, hs]), [rstd_b[half]], [rstd_b[half]])
    for i in range(nchunk):
        P.op("act", lambda e, i=i: e.activation(out=R_(gate[:, i, :]), in_=gate[:, i, :], func=AF.Silu), [gate_b], [gate_b])
    for i in range(nchunk):
        if gain is None:
            P.op("dve", lambda e, i=i: e.tensor_tensor(out=R_(outT[:, i, :]), in0=src[:, i, :], in1=rstd[:, :], op=ALU.mult),
                 [src_b, rstd_b[0], rstd_b[1]], [out_b])
        else:
            P.op("dve", lambda e, i=i: e.scalar_tensor_tensor(out=R_(outT[:, i, :]), in0=src[:, i, :], scalar=gain[:, i:i + 1], in1=rstd[:, :], op0=ALU.mult, op1=ALU.mult),
                 [src_b, rstd_b[0], rstd_b[1], gain_b], [out_b])
        P.op("dve", lambda e, i=i: e.tensor_tensor(out=R_(outT[:, i, :]), in0=outT[:, i, :], in1=gate[:, i, :], op=ALU.mult),
             [out_b, gate_b], [out_b])


def retention_phase(c, X, X_b, H, WR, FS, zfm, zfm_b, ztm, ztm_b, d, wout_d, sq, sq_b, rstd, rstd_b, sstate=None, first=False):
    P = c.P
    qT = H[:, 0:2048].rearrange("p (a t) -> p a t", a=2); qT_b = P.buf("qT")
    kT = H[:, 2048:4096].rearrange("p (a t) -> p a t", a=2); kT_b = P.buf("kT")
    qs = H[:, 4096:6144].rearrange("p (a t) -> p a t", a=2); qs_b = P.buf("qs")
    ktm = H[:, 6144:8192].rearrange("p (j c) -> p j c", j=8); ktm_b = P.buf("ktm")
    vtm = H[:, 8192:10240].rearrange("p (j c) -> p j c", j=8); vtm_b = P.buf("vtm")
    S = [H[:, 10240 + s * 512:10240 + (s + 1) * 512].rearrange("p (a e) -> p a e", a=2) for s in range(2)] + [H[:, 15616:16128].rearrange("p (a e) -> p a e", a=2)]
    S_b = P.bufs(3, "S")
    PT = [H[:, 11264 + s * 128:11264 + (s + 1) * 128] for s in range(2)]; PT_b = P.bufs(2, "PT")
    mixedT = H[:, 11520:13568].rearrange("p (a t) -> p a t", a=2); mixed_b = P.buf("mixedT")
    retT = H[:, 13568:15616].rearrange("p (a t) -> p a t", a=2); retT_b = P.buf("retT")
    tS = [H[:, 15616 + s * 256:15616 + (s + 1) * 256] for s in range(2)]; tS_b = P.bufs(2, "tS")
    cosT = WR[:, 0:1024]; sinT = WR[:, 1024:2048]
    cos_tm = WR[:, 2048:3072].rearrange("p (j i) -> p j i", j=8); sin_tm = WR[:, 3072:4096].rearrange("p (j i) -> p j i", j=8)
    dmask = WR[:, 4096:4608].rearrange("p (h i) -> p h i", h=4); xi = WR[:, 4608:5120].rearrange("p (h i) -> p h i", h=4)
    tab_b = P.bufs(6, "rtab")
    for i, (dst, key) in enumerate(((cosT, "cosT"), (sinT, "sinT"), (cos_tm, "cos_tm"), (sin_tm, "sin_tm"), (dmask, "dmask"), (xi, "xi"))):
        P.dma("pool", R_(dst), d[key], [], [tab_b[i]], None)
    sst_b = P.buf("sstate")
    rawA = WR[:, 5120:6144]; rawB = WR[:, 6144:7168]; raw_b = P.buf("rraw")
    rawtm = WR[:, 5120:7168].rearrange("p (j c) -> p j c", j=8)
    t3 = WR[:, 7168:8192]; t2 = WR[:, 8192:9216]; tmp_b = P.bufs(2, "rtmp")
    wo = [R_(WR[:, 9216 + s * 2048:9216 + (s + 1) * 2048]) for s in range(2)]; wo_b = P.bufs(2, "wo")
    gT = FS[:, 2048:4096].rearrange("p (a t) -> p a t", a=2); gT_b = P.buf("gT")
    kz, kz_b = load_small(c, "kz_sb", d["kz"], [128, 4])
    if sstate is None:
        sw, sw_b = load_small(c, "sw_sb", d["sw"], [128, 12])

    def rot_fm(chunk0, out3, out_b):
        P.dma("pool", R_(rawA), zfm[chunk0], [zfm_b], [raw_b], raw_b)
        P.dma("pool", R_(rawB), zfm[chunk0 + 1], [zfm_b], [raw_b], raw_b)
        tb = [tab_b[0], tab_b[1]]
        P.op("dve", lambda e: e.tensor_tensor(out=R_(t3), in0=rawA, in1=cosT, op=ALU.mult), [raw_b] + tb, [tmp_b[0]])
        P.op("dve", lambda e: e.tensor_tensor(out=R_(t2), in0=rawB, in1=sinT, op=ALU.mult), [raw_b] + tb, [tmp_b[1]])
        P.op("dve", lambda e: e.tensor_tensor(out=R_(out3[:, 0, :]), in0=t3, in1=t2, op=ALU.subtract), [tmp_b[0], tmp_b[1]], [out_b])
        P.op("dve", lambda e: e.tensor_tensor(out=R_(t3), in0=rawA, in1=sinT, op=ALU.mult), [raw_b] + tb, [tmp_b[0]])
        P.op("dve", lambda e: e.tensor_tensor(out=R_(t2), in0=rawB, in1=cosT, op=ALU.mult), [raw_b] + tb, [tmp_b[1]])
        P.op("dve", lambda e: e.tensor_tensor(out=R_(out3[:, 1, :]), in0=t3, in1=t2, op=ALU.add), [tmp_b[0], tmp_b[1]], [out_b])

    for h in range(4):
        g128 = float(RET_GAMMA[h] ** 128)
        rot_fm(2 * h, qT, qT_b)
        for a in range(2):
            P.op("dve", lambda e, a=a, h=h: e.tensor_tensor(out=R_(qs[:, a, :]).rearrange("p (j i) -> p j i", j=8), in0=qT[:, a, :].rearrange("p (j i) -> p j i", j=8),
                                                          in1=xi[:, h:h + 1, :].to_broadcast([128, 8, 128]), op=ALU.mult), [qT_b, tab_b[5]], [qs_b])
        rot_fm(8 + 2 * h, kT, kT_b)
        P.dma("pool", R_(rawtm), ztm[:, :, h * 256:(h + 1) * 256].rearrange("j p c -> p j c"), [ztm_b], [raw_b], raw_b)
        P.dma("pool", R_(vtm), ztm[:, :, 1024 + h * 256:1024 + (h + 1) * 256].rearrange("j p c -> p j c"), [ztm_b], [vtm_b], vtm_b)
        for a in range(2):
            P.dma("pool", R_(gT[:, a, :]), zfm[16 + 2 * h + a], [zfm_b], [gT_b], gT_b)
        P.op("dve", lambda e, h=h: e.tensor_scalar(out=R_(rawtm), in0=rawtm, scalar1=kz[:, h:h + 1], scalar2=None, op0=ALU.mult), [raw_b, kz_b], [raw_b])
        t3j = t3.rearrange("p (j i) -> p j i", j=8); t2j = t2.rearrange("p (j i) -> p j i", j=8)
        tb = [tab_b[2], tab_b[3]]
        A, B = rawtm[:, :, 0:128], rawtm[:, :, 128:256]
        P.op("dve", lambda e: e.tensor_tensor(out=R_(t3j), in0=A, in1=cos_tm, op=ALU.mult), [raw_b] + tb, [tmp_b[0]])
        P.op("dve", lambda e: e.tensor_tensor(out=R_(t2j), in0=B, in1=sin_tm, op=ALU.mult), [raw_b] + tb, [tmp_b[1]])
        P.op("dve", lambda e: e.tensor_tensor(out=R_(ktm[:, :, 0:128]), in0=t3j, in1=t2j, op=ALU.subtract), [tmp_b[0], tmp_b[1]], [ktm_b])
        P.op("dve", lambda e: e.tensor_tensor(out=R_(t3j), in0=A, in1=sin_tm, op=ALU.mult), [raw_b] + tb, [tmp_b[0]])
        P.op("dve", lambda e: e.tensor_tensor(out=R_(t2j), in0=B, in1=cos_tm, op=ALU.mult), [raw_b] + tb, [tmp_b[1]])
        P.op("dve", lambda e: e.tensor_tensor(out=R_(ktm[:, :, 128:256]), in0=t3j, in1=t2j, op=ALU.add), [tmp_b[0], tmp_b[1]], [ktm_b])
        if sstate is not None:
            for dc in range(2):
                if first:
                    P.op("dve", lambda e, dc=dc: e.tensor_scalar(out=R_(S[0][:, dc, :]), in0=dmask[:, 0:2, :].rearrange("p a i -> p (a i)"), scalar1=0.0, scalar2=None, op0=ALU.mult),
                         [tab_b[4]], [S_b[0]])
                else:
                    P.dma("pool", R_(S[0][:, dc, :]), sstate[h, dc], [sst_b], [S_b[0]], S_b[0])
        else:
            for dc in range(2):
                for i in range(3):
                    sl = (dc * 3 + i) % 2
                    P.dma("pool", R_(tS[sl]), d["sprev"][i, h, dc], [], [tS_b[sl]], tS_b[sl])
                    col = sw[:, i * 4 + h:i * 4 + h + 1]
                    if i == 0:
                        P.op("dve", lambda e, sl=sl, dc=dc, col=col: e.tensor_scalar(out=R_(S[0][:, dc, :]), in0=tS[sl], scalar1=col, scalar2=None, op0=ALU.mult),
                             [tS_b[sl], sw_b], [S_b[0]])
                    else:
                        P.op("dve", lambda e, sl=sl, dc=dc, col=col: e.scalar_tensor_tensor(out=R_(S[0][:, dc, :]), in0=tS[sl], scalar=col, in1=S[0][:, dc, :], op0=ALU.mult, op1=ALU.add),
                             [tS_b[sl], sw_b, S_b[0]], [S_b[0]])
        def stage1(j, h=h, g128=g128):
            js = slice(j * 128, (j + 1) * 128)
            pb = c.nextps(0, 8)
            for a in range(2):
                P.op("pe", lambda e, a=a, js=js, pb=pb: e.matmul(c.ps[pb][:, 0:128], R_(kT[:, a, js]), R_(qT[:, a, js]), start=(a == 0), stop=(a == 1)),
                     [kT_b, qT_b], [c.ps_b[pb]])
            P.op("dve", lambda e, pb=pb, j=j, h=h: e.tensor_tensor(out=R_(PT[j % 2]), in0=c.ps[pb][:, 0:128], in1=dmask[:, h, :], op=ALU.mult),
                 [c.ps_b[pb], tab_b[4]], [PT_b[j % 2]])
            if j < 7 or sstate is not None:
                cur, nxt = j % 3, (j + 1) % 3
                for a in range(2):
                    pb3 = c.nextps(0, 8)
                    P.op("pe", lambda e, a=a, j=j, pb3=pb3: e.matmul(c.ps[pb3][:, 0:256], R_(ktm[:, j, a * 128:(a + 1) * 128]), R_(vtm[:, j, :]), start=True, stop=True),
                         [ktm_b, vtm_b], [c.ps_b[pb3]])
                    P.op("dve", lambda e, a=a, pb3=pb3, cur=cur, nxt=nxt, g128=g128: e.scalar_tensor_tensor(out=R_(S[nxt][:, a, :]), in0=S[cur][:, a, :], scalar=g128, in1=c.ps[pb3][:, 0:256],
                                                                                                      op0=ALU.mult, op1=ALU.add),
                         [S_b[cur], c.ps_b[pb3]], [S_b[nxt]])

        def stage2(j):
            js = slice(j * 128, (j + 1) * 128)
            cur = j % 3
            for ec in range(2):
                es = slice(ec * 128, (ec + 1) * 128)
                pb2 = c.nextps(0, 8)
                P.op("pe", lambda e, j=j, es=es, pb2=pb2: e.matmul(c.ps[pb2][:, 0:128], R_(vtm[:, j, es]), R_(PT[j % 2]), start=True, stop=False),
                     [vtm_b, PT_b[j % 2]], [c.ps_b[pb2]])
                for a in range(2):
                    P.op("pe", lambda e, a=a, es=es, js=js, pb2=pb2, cur=cur: e.matmul(c.ps[pb2][:, 0:128], R_(S[cur][:, a, es]), R_(qs[:, a, js]), start=False, stop=(a == 1)),
                         [S_b[cur], qs_b], [c.ps_b[pb2]])
                P.op("act", lambda e, ec=ec, js=js, pb2=pb2: e.copy(out=R_(retT[:, ec, js]), in_=c.ps[pb2][:, 0:128]), [c.ps_b[pb2]], [retT_b])

        stage1(0)
        for j in range(8):
            if j + 1 < 8:
                stage1(j + 1)
            stage2(j)
        if sstate is not None:
            for dc in range(2):
                P.dma("sp", sstate[h, dc], S[2][:, dc, :], [S_b[2]], [sst_b], S_b[2])
        groupnorm_gate(c, retT, retT_b, 2, 256, gT, gT_b, None, None, mixedT, mixed_b, sq, sq_b, rstd, rstd_b)
        outproj_partial(c, X, X_b, R_(mixedT), mixed_b, wout_d, [2 * h, 2 * h + 1], wo, wo_b)


def sgu_phase(c, X, X_b, H, WR, FS, zfm, zfm_b, ztm, ztm_b, d, wout_d, u_chunk0, vs_col0):
    P = c.P
    vs = H[:, 0:8192].rearrange("p (j c) -> p j c", j=8); vs_b = P.bufs(8, "vs")
    uT = H[:, 8192:10240].rearrange("p (a t) -> p a t", a=2); uT_b = P.buf("uT")
    mixedT = H[:, 11520:13568].rearrange("p (a t) -> p a t", a=2); mixed_b = P.buf("smixedT")
    lng = WR[:, 0:1024]; lnb = WR[:, 1024:2048]
    wsT = WR[:, 2048:2560].rearrange("p (g i) -> p g i", g=4); bsb = WR[:, 2560:3072].rearrange("p (g i) -> p g i", g=4)
    junk = WR[:, 3072:4096]; junk_b = P.buf("sjunk")
    tmp = [WR[:, 4096 + s * 512:4096 + (s + 1) * 512] for s in range(2)]; tmp_b = P.bufs(2, "stmp")
    wo = [R_(WR[:, 9216 + s * 2048:9216 + (s + 1) * 2048]) for s in range(2)]; wo_b = P.bufs(2, "wo")
    tb = P.bufs(4, "stab")
    for i, (dst, key) in enumerate(((lng, "lng_b"), (lnb, "lnb_b"), (wsT, "wsT"), (bsb, "bsb"))):
        P.dma("pool", R_(dst), d[key], [], [tb[i]], None)
    P.op("dve", lambda e: e.tensor_scalar(out=R_(wsT[64:128, :, 0:64]), in0=wsT[64:128, :, 0:64], scalar1=0.0, scalar2=None, op0=ALU.mult), [tb[2]], [tb[2]])
    st = c.sb("sgu_st", [128, 8, 4]); st_b = P.bufs(8, "sgst")
    P.op("dve", lambda e: e.memset(st[:], 0.0), [], st_b)
    for j in range(8):
        P.dma("pool", R_(vs[:, j, :]), ztm[j, :, vs_col0:vs_col0 + 1024], [ztm_b], [vs_b[j]], None)
    for j in range(8):
        v = vs[:, j, :]
        P.op("act", lambda e, v=v, j=j: e.activation(out=R_(v), in_=v, func=AF.Gelu, accum_out=st[:, j, 0:1]), [vs_b[j], st_b[j]], [vs_b[j], st_b[j]])
        P.op("dve", lambda e, j=j: e.tensor_scalar(out=st[:, j, 1:2], in0=st[:, j, 0:1], scalar1=-1.0 / 1024, scalar2=None, op0=ALU.mult), [st_b[j]], [st_b[j]])
        P.op("act", lambda e, v=v, j=j: e.activation(out=R_(junk), in_=v, func=AF.Square, bias=st[:, j, 1:2], accum_out=st[:, j, 2:3]), [vs_b[j], st_b[j], junk_b], [junk_b, st_b[j]])
        P.op("act", lambda e, j=j: e.activation(out=st[:, j, 3:4], in_=st[:, j, 2:3], func=AF.Sqrt, bias=c.eps[:, 0:1], scale=1.0 / 1024), [st_b[j], c.eps_b], [st_b[j]])
        P.op("dve", lambda e, j=j: e.reciprocal(out=st[:, j, 3:4], in_=st[:, j, 3:4]), [st_b[j]], [st_b[j]])
        P.op("dve", lambda e, v=v, j=j: e.tensor_scalar(out=R_(v), in0=v, scalar1=st[:, j, 1:2], scalar2=st[:, j, 3:4], op0=ALU.add, op1=ALU.mult), [vs_b[j], st_b[j]], [vs_b[j]])
        P.op("dve", lambda e, v=v: e.tensor_tensor(out=R_(v), in0=v, in1=lng, op=ALU.mult), [vs_b[j], tb[0]], [vs_b[j]])
        P.op("dve", lambda e, v=v: e.tensor_tensor(out=R_(v), in0=v, in1=lnb, op=ALU.add), [vs_b[j], tb[1]], [vs_b[j]])
    for g in range(4):
        for a in range(2):
            P.dma("pool", R_(uT[:, a, :]), zfm[u_chunk0 + 2 * g + a], [zfm_b], [uT_b], uT_b)
        for a in range(2):
            P.op("act", lambda e, a=a: e.activation(out=R_(uT[:, a, :]), in_=uT[:, a, :], func=AF.Gelu), [uT_b], [uT_b])
        for a in range(2):
            cc = 2 * g + a
            for wq in range(2):
                pb = c.nextps(0, 8)
                for w in range(4):
                    j = 4 * wq + w
                    P.op("pe", lambda e, j=j, w=w, cc=cc, g=g, pb=pb: e.matmul(c.ps[pb][:, w * 128:(w + 1) * 128], R_(vs[:, j, cc * 128:(cc + 1) * 128]), R_(wsT[:, g, :]), start=True, stop=True),
                         [vs_b[j], tb[2]], [c.ps_b[pb]])
                sl = (a * 2 + wq) % 2
                P.op("dve", lambda e, pb=pb, sl=sl, g=g: e.tensor_tensor(out=R_(tmp[sl]).rearrange("p (w i) -> p w i", w=4), in0=c.ps[pb][:].rearrange("p (w i) -> p w i", w=4),
                                                                       in1=bsb[:, g:g + 1, :].to_broadcast([128, 4, 128]), op=ALU.add), [c.ps_b[pb], tb[3]], [tmp_b[sl]])
                P.op("dve", lambda e, a=a, wq=wq, sl=sl: e.tensor_tensor(out=R_(mixedT[:, a, wq * 512:(wq + 1) * 512]), in0=tmp[sl], in1=uT[:, a, wq * 512:(wq + 1) * 512], op=ALU.mult),
                     [tmp_b[sl], uT_b], [mixed_b])
        outproj_partial(c, X, X_b, R_(mixedT), mixed_b, wout_d, [8 + 2 * g, 8 + 2 * g + 1], wo, wo_b)


def build_k2():
    c = Ctx("k2")
    P = c.P
    xT = c.din("xT", [128, KC, T])
    g_mix = c.din("g_mix", [128, KC]); g_ffn = c.din("g_ffn", [128, KC])
    wfm = c.din("wfm", [32, 128, KC * 128]); wtm = c.din("wtm", [12, 128, KC * 256])
    d = {"cosT": c.din("cosT", [128, 1024]), "sinT": c.din("sinT", [128, 1024]),
         "cos_tm": c.din("cos_tm", [128, 8, 128]), "sin_tm": c.din("sin_tm", [128, 8, 128]),
         "dmask": c.din("dmask", [128, 4, 128]), "xi": c.din("xi", [128, 4, 128]), "kz": c.din("kz", [128, 4]),
         "sw": c.din("sw", [128, 12]), "sprev": c.din("sprev", [3, 4, 2, 128, 256]),
         "lng_b": c.din("lng_b", [128, 1024]), "lnb_b": c.din("lnb_b", [128, 1024]),
         "wsT": c.din("wsT", [128, 4, 128]), "bsb": c.din("bsb", [128, 4, 128])}
    wout = c.din("wout", [D, D])
    wg = c.din("wg", [FC, 128, 2048]); wu = c.din("wu", [FC, 128, 2048]); wd = c.din("wd", [DFF, D])
    out = c.dout("out", [128, KC, T])
    zfm = c.dscr("zfm", [32, 128, 1024])
    ztm = c.dscr("ztm", [8, 128, 3072])
    X = c.sb("X", [128, KC, T]); X_b = P.bufs(2 * KC, "X")
    H = c.sb("H", [128, KC * T]); H_b = P.buf("H")
    WR = c.sb("WR", [128, 14336])
    FS = c.sb("FS", [128, 4096])
    Hr = H[:, :].bitcast(F32R).rearrange("p (k t) -> p k t", k=KC)
    gm_sb, gm_b = load_small(c, "gm_sb", g_mix, [128, KC])
    gf_sb, gf_b = load_small(c, "gf_sb", g_ffn, [128, KC])
    sq = FS[:, 0:1024]; sq_b = P.bufs(2, "sq")
    rstd = FS[:, 1024:2048]; rstd_b = P.bufs(2, "rstd")
    load_X(c, X, X_b, xT)
    rmsnorm_fm(c, X, X_b, Hr, H_b, gm_sb, gm_b, sq, sq_b, rstd, rstd_b)
    zfm_b, ztm_b = proj_phase(c, Hr, H_b, KC, wfm, 32, zfm, wtm, 12, ztm, WR, "k2")
    P.barrier()
    retention_phase(c, X, X_b, H, WR, FS, zfm, zfm_b, ztm, ztm_b, d, wout, sq, sq_b, rstd, rstd_b)
    P.barrier()
    sgu_phase(c, X, X_b, H, WR, FS, zfm, zfm_b, ztm, ztm_b, d, wout, 24, 2048)
    P.barrier()
    rmsnorm_fm(c, X, X_b, Hr, H_b, gf_sb, gf_b, sq, sq_b, rstd, rstd_b)
    ffn_phase(c, X, X_b, Hr, H_b, wg, wu, wd, WR, FS[:, 2048:3072])
    store_X(c, X, X_b, out)
    return c.finish(X_b)


def k2_tables(qtr):
    tb = np.arange(128)
    dmask = np.zeros((128, 4, 128), np.float64)
    xi = np.zeros((128, 4, 128), np.float64)
    kz = np.zeros((128, 4), np.float64)
    sw = np.zeros((128, 12), np.float64)
    ch = tb // 64
    for h in range(4):
        gm = RET_GAMMA[h]
        dmask[:, h, :] = (gm ** np.abs(tb[None, :] - tb[:, None])) * (ch[:, None] <= ch[None, :]) / 16.0
        xi[:, h, :] = (gm ** (tb + 1.0))[None, :]
        kz[:, h] = gm ** (127.0 - tb) / 16.0
        for i in range(3):
            if i < qtr:
                sw[:, i * 4 + h] = gm ** (1024.0 * (qtr - i - 1))
    return dmask.astype(np.float32), xi.astype(np.float32), kz.astype(np.float32), sw.astype(np.float32)


def hgrn_phase(c, X, X_b, H, WR, FS, zfm, zfm_b, ztm, ztm_b, d, wout_d, sq, sq_b, rstd, rstd_b, outputs, dbg=None, hstate=None, first=False):
    P = c.P
    qhat = H[:, 0:1024]; qhat_b = P.buf("qhat")
    khat = H[:, 1024:2048]; khat_b = P.buf("khat")
    eb = H[:, 2048:3072]; eb_b = P.buf("eb")
    enb = H[:, 3072:4096]; enb_b = P.buf("enb")
    vtm = H[:, 4096:5120].rearrange("p (j e) -> p j e", j=8); vtm_b = P.buf("hvtm")
    rawq = H[:, 5120:6144]; rawq_b = P.buf("rawq")
    rawf = H[:, 6144:7168]; rawf_b = P.buf("rawf")
    kk = H[:, 7168:7296]; kk_b = P.buf("kk")
    ktl = H[:, 7296:7424]; ktl_b = P.buf("ktl")
    am = [H[:, 7424 + s * 128:7424 + (s + 1) * 128] for s in range(2)]; am_b = P.bufs(2, "am")
    St = [H[:, 7680 + s * 128:7680 + (s + 1) * 128] for s in range(2)]; St_b = P.bufs(2, "St")
    outT = H[:, 7936:8960].rearrange("p (a t) -> p a t", a=1); outT_b = P.buf("houtT")
    mixedT = H[:, 8960:9984].rearrange("p (a t) -> p a t", a=1); mixed_b = P.buf("hmixedT")
    ident = H[:, 9984:10112]; ident_b = P.buf("ident")
    tS = H[:, 10112:10240]; tS_b = P.buf("htS")
    lbt = WR[:, 0:1024]; omlt = WR[:, 1024:2048]; lbt_b = P.buf("lbt")
    rawft = WR[:, 2048:3072].rearrange("p (j e) -> p j e", j=8); rawft_b = P.buf("rawft")
    l1t = WR[:, 3072:4096]
    wo = [R_(WR[:, 9216 + s * 2048:9216 + (s + 1) * 2048]) for s in range(2)]; wo_b = P.bufs(2, "wo")
    logf = FS[:, 2048:3072].rearrange("p (j e) -> p j e", j=8); logf_b = P.buf("logf")
    gate = FS[:, 3072:4096].rearrange("p (a t) -> p a t", a=1); gate_b = P.buf("hgate")
    ltri, ltri_b = load_small(c, "ltri_sb", d["ltri"], [128, 128])
    mtri, mtri_b = load_small(c, "mtri_sb", d["mtri"], [128, 128])
    P.dma("pool", R_(ident), d["ident"], [], [ident_b], None)
    hst_b = P.buf("hstate")
    lbf = c.sb("lbf", [128, 4, 8]); lbf_b = P.buf("lbf")
    P.dma("sp", lbf[:, 3, :], d["lb0f"], [], [lbf_b], None)
    P.dma("sp", lbf[:, 0, :], d["lb1f"], [], [lbf_b], None)
    P.op("dve", lambda e: e.tensor_tensor(out=lbf[:, 0, :], in0=lbf[:, 0, :], in1=lbf[:, 3, :], op=ALU.subtract), [lbf_b], [lbf_b])
    P.op("act", lambda e: e.activation(out=lbf[:, 0, :], in_=lbf[:, 0, :], func=AF.Sigmoid), [lbf_b], [lbf_b])
    P.op("dve", lambda e: e.tensor_scalar(out=lbf[:, 1, :], in0=lbf[:, 0, :], scalar1=-1.0, scalar2=1.0, op0=ALU.mult, op1=ALU.add), [lbf_b], [lbf_b])
    P.op("dve", lambda e: e.tensor_scalar(out=lbf[:, 2, :], in0=lbf[:, 1, :], scalar1=-1.0, scalar2=None, op0=ALU.mult), [lbf_b], [lbf_b])
    P.dma("pool", R_(lbt), d["lb1t"], [], [lbt_b], None)
    P.dma("pool", R_(l1t), d["lb0t"], [], [lbt_b], None)
    P.op("dve", lambda e: e.tensor_tensor(out=R_(lbt), in0=lbt, in1=l1t, op=ALU.subtract), [lbt_b], [lbt_b])
    P.op("act", lambda e: e.activation(out=R_(lbt), in_=lbt, func=AF.Sigmoid), [lbt_b], [lbt_b])
    P.op("dve", lambda e: e.tensor_scalar(out=R_(omlt), in0=lbt, scalar1=-1.0, scalar2=1.0, op0=ALU.mult, op1=ALU.add), [lbt_b], [lbt_b])
    fused = hstate is not None
    if outputs:
        ngain, ngain_b = load_small(c, "ngain_sb", d["ngain"], [128, 8])
        if not fused:
            aprev, aprev_b = load_small(c, "aprev_sb", d["aprev"], [128, 24])
    if not outputs and not fused:
        aprod = c.sb("aprod", [128, 8]); aprod_b = P.buf("aprod")
        P.op("dve", lambda e: e.memset(aprod[:], 1.0), [], [aprod_b])
    if not outputs:
        sout = c.sb("hsout", [128, 128]); sout_b = P.buf("hsout")

    for h in range(8):
        hs_ = slice(h * 128, (h + 1) * 128)
        if outputs:
            P.dma("pool", R_(rawq), zfm[h], [zfm_b], [rawq_b], rawq_b)
        P.dma("pool", R_(rawf), zfm[8 + h], [zfm_b], [rawf_b], rawf_b)
        P.dma("pool", R_(rawft), ztm[:, :, h * 128:(h + 1) * 128].rearrange("j p e -> p j e"), [ztm_b], [rawft_b], rawft_b)
        P.dma("pool", R_(vtm), ztm[:, :, 1024 + h * 128:1024 + (h + 1) * 128].rearrange("j p e -> p j e"), [ztm_b], [vtm_b], vtm_b)
        if outputs:
            P.dma("pool", R_(gate[:, 0, :]), zfm[16 + h], [zfm_b], [gate_b], gate_b)
        P.op("act", lambda e: e.activation(out=R_(rawft), in_=rawft, func=AF.Sigmoid), [rawft_b], [rawft_b])
        P.op("dve", lambda e, hs_=hs_: e.tensor_tensor(out=R_(rawft), in0=rawft, in1=omlt[:, hs_].rearrange("p (o e) -> p o e", o=1).to_broadcast([128, 8, 128]), op=ALU.mult),
             [rawft_b, lbt_b], [rawft_b])
        P.op("dve", lambda e, hs_=hs_: e.tensor_tensor(out=R_(rawft), in0=rawft, in1=lbt[:, hs_].rearrange("p (o e) -> p o e", o=1).to_broadcast([128, 8, 128]), op=ALU.add),
             [rawft_b, lbt_b], [rawft_b])
        P.op("act", lambda e: e.activation(out=logf, in_=rawft, func=AF.Ln), [rawft_b], [logf_b])
        for half in range(2):
            pb = c.nextps(0, 8)
            for jj in range(4):
                j = half * 4 + jj
                P.op("pe", lambda e, j=j, jj=jj, pb=pb: e.matmul(c.ps[pb][:, jj * 128:(jj + 1) * 128], logf[:, j, :], ltri[:], start=True, stop=True),
                     [logf_b, ltri_b], [c.ps_b[pb]])
            hsl = slice(half * 512, (half + 1) * 512)
            P.op("act", lambda e, hsl=hsl, pb=pb: e.activation(out=R_(eb[:, hsl]), in_=c.ps[pb][:], func=AF.Exp), [c.ps_b[pb]], [eb_b])
            P.op("act", lambda e, hsl=hsl, pb=pb: e.activation(out=R_(enb[:, hsl]), in_=c.ps[pb][:], func=AF.Exp, scale=-1.0), [c.ps_b[pb]], [enb_b])
        if outputs:
            P.op("act", lambda e: e.activation(out=R_(rawq), in_=rawq, func=AF.Silu), [rawq_b], [rawq_b])
            P.op("dve", lambda e: e.tensor_tensor(out=R_(qhat), in0=rawq, in1=eb, op=ALU.mult), [rawq_b, eb_b], [qhat_b])
        P.op("act", lambda e: e.activation(out=R_(rawf), in_=rawf, func=AF.Sigmoid), [rawf_b], [rawf_b])
        P.op("dve", lambda e, h=h: e.tensor_scalar(out=R_(rawf), in0=rawf, scalar1=lbf[:, 2, h:h + 1], scalar2=lbf[:, 1, h:h + 1], op0=ALU.mult, op1=ALU.add), [rawf_b, lbf_b], [rawf_b])
        P.op("dve", lambda e: e.tensor_tensor(out=R_(khat), in0=rawf, in1=enb, op=ALU.mult), [rawf_b, enb_b], [khat_b])
        if fused:
            if first:
                P.op("dve", lambda e: e.tensor_scalar(out=R_(St[0]), in0=ident, scalar1=0.0, scalar2=None, op0=ALU.mult), [ident_b], [St_b[0]])
            else:
                P.dma("pool", R_(St[0]), hstate[h], [hst_b], [St_b[0]], St_b[0])
        elif outputs:
            for i in range(3):
                P.dma("pool", R_(tS), d["sprev"][i, h], [], [tS_b], tS_b)
                if i == 0:
                    P.op("dve", lambda e: e.tensor_copy(out=R_(St[0]), in_=tS), [tS_b], [St_b[0]])
                else:
                    P.op("dve", lambda e, i=i, h=h: e.scalar_tensor_tensor(out=R_(St[0]), in0=St[0], scalar=aprev[:, i * 8 + h:i * 8 + h + 1], in1=tS, op0=ALU.mult, op1=ALU.add),
                         [St_b[0], aprev_b, tS_b], [St_b[0]])
        else:
            P.op("dve", lambda e: e.tensor_scalar(out=R_(St[0]), in0=ident, scalar1=0.0, scalar2=None, op0=ALU.mult), [ident_b], [St_b[0]])
        ktls = [ktl, tS] if fused else [ktl, ktl]
        ktls_b = [ktl_b, P.buf("ktl2")] if fused else [ktl_b, ktl_b]

        def stageA(j, h=h):
            js = slice(j * 128, (j + 1) * 128)
            ja = slice(j * 128, j * 128 + 64); jb = slice(j * 128 + 64, (j + 1) * 128)
            ca = j * 128 + 63; cb_ = j * 128 + 127
            kt, kt_b = ktls[j % 2], ktls_b[j % 2]
            P.op("dve", lambda e, ja=ja, ca=ca: e.tensor_scalar(out=R_(kk[:, 0:64]), in0=khat[:, ja], scalar1=eb[:, ca:ca + 1], scalar2=None, op0=ALU.mult), [khat_b, eb_b], [kk_b])
            P.op("dve", lambda e, jb=jb, cb_=cb_: e.tensor_scalar(out=R_(kk[:, 64:128]), in0=khat[:, jb], scalar1=eb[:, cb_:cb_ + 1], scalar2=None, op0=ALU.mult), [khat_b, eb_b], [kk_b])
            pt = c.nextps(0, 8)
            P.op("pe", lambda e, pt=pt: e.matmul(c.ps[pt][:, 0:128], R_(kk), R_(ident), start=True, stop=True), [kk_b, ident_b], [c.ps_b[pt]])
            P.op("act", lambda e, pt=pt, kt=kt: e.copy(out=R_(kt), in_=c.ps[pt][:, 0:128]), [c.ps_b[pt]], [kt_b])
            if outputs:
                pb = c.nextps(0, 8)
                P.op("pe", lambda e, js=js, pb=pb: e.matmul(c.ps[pb][:, 0:128], R_(khat[:, js]), R_(qhat[:, js]), start=True, stop=True), [khat_b, qhat_b], [c.ps_b[pb]])
                P.op("dve", lambda e, pb=pb, j=j: e.tensor_tensor(out=R_(am[j % 2]), in0=c.ps[pb][:, 0:128], in1=mtri[:], op=ALU.mult), [c.ps_b[pb], mtri_b], [am_b[j % 2]])

        def stageB(j, h=h):
            js = slice(j * 128, (j + 1) * 128)
            ja = slice(j * 128, j * 128 + 64); jb = slice(j * 128 + 64, (j + 1) * 128)
            ca = j * 128 + 63; cb_ = j * 128 + 127
            cur = j % 2
            kt, kt_b = ktls[j % 2], ktls_b[j % 2]
            pa = c.nextps(0, 8)
            P.op("pe", lambda e, j=j, pa=pa, kt=kt: e.matmul(c.ps[pa][:, 0:128], R_(kt[0:64, :]), R_(vtm[0:64, j, :]), start=True, stop=True), [kt_b, vtm_b], [c.ps_b[pa]])
            P.op("dve", lambda e, pa=pa, ca=ca: e.scalar_tensor_tensor(out=R_(St[1]), in0=St[0], scalar=eb[:, ca:ca + 1], in1=c.ps[pa][:, 0:128], op0=ALU.mult, op1=ALU.add),
                 [St_b[0], eb_b, c.ps_b[pa]], [St_b[1]])
            if outputs:
                po = c.nextps(0, 8)
                for (lo, qsl, stt) in ((0, ja, 0), (64, jb, 1)):
                    P.op("pe", lambda e, j=j, lo=lo, po=po, cur=cur: e.matmul(c.ps[po][:, lo:lo + 64], R_(vtm[:, j, :]), R_(am[cur][:, lo:lo + 64]), start=True, stop=False),
                         [vtm_b, am_b[cur]], [c.ps_b[po]])
                    P.op("pe", lambda e, lo=lo, po=po, qsl=qsl, stt=stt: e.matmul(c.ps[po][:, lo:lo + 64], R_(St[stt]), R_(qhat[:, qsl]), start=False, stop=True),
                         [St_b[stt], qhat_b], [c.ps_b[po]])
                P.op("act", lambda e, js=js, po=po: e.copy(out=R_(outT[:, 0, js]), in_=c.ps[po][:, 0:128]), [c.ps_b[po]], [outT_b])
            elif not fused:
                P.op("dve", lambda e, ca=ca, h=h: e.tensor_tensor(out=aprod[:, h:h + 1], in0=aprod[:, h:h + 1], in1=eb[:, ca:ca + 1], op=ALU.mult), [aprod_b, eb_b], [aprod_b])
                P.op("dve", lambda e, cb_=cb_, h=h: e.tensor_tensor(out=aprod[:, h:h + 1], in0=aprod[:, h:h + 1], in1=eb[:, cb_:cb_ + 1], op=ALU.mult), [aprod_b, eb_b], [aprod_b])
            if j < 7 or not outputs:
                pc = c.nextps(0, 8)
                P.op("pe", lambda e, j=j, pc=pc, kt=kt: e.matmul(c.ps[pc][:, 0:128], R_(kt[64:128, :]), R_(vtm[64:128, j, :]), start=True, stop=True), [kt_b, vtm_b], [c.ps_b[pc]])
                P.op("dve", lambda e, pc=pc, cb_=cb_: e.scalar_tensor_tensor(out=R_(St[0]), in0=St[1], scalar=eb[:, cb_:cb_ + 1], in1=c.ps[pc][:, 0:128], op0=ALU.mult, op1=ALU.add),
                     [St_b[1], eb_b, c.ps_b[pc]], [St_b[0]])

        stageA(0)
        for j in range(8):
            if j + 1 < 8:
                stageA(j + 1)
            stageB(j)
        if outputs:
            groupnorm_gate(c, outT, outT_b, 1, 128, gate, gate_b, ngain[:, h:h + 1], ngain_b, mixedT, mixed_b, sq, sq_b, rstd, rstd_b)
            if dbg is not None:
                P.dma("sp", dbg[h], mixedT[:, 0, :], [mixed_b], [], mixed_b)
            outproj_partial(c, X, X_b, R_(mixedT), mixed_b, wout_d, [h], wo, wo_b)
        else:
            P.op("dve", lambda e: e.tensor_copy(out=sout[:], in_=St[0]), [St_b[0]], [sout_b])
            if fused:
                P.dma("sp", hstate[h], sout[:], [sout_b], [hst_b], sout_b)
            else:
                P.dma("sp", d["sloc"][h], sout[:], [sout_b], [], sout_b)
    if not outputs and not fused:
        P.dma("sp", d["aout"], aprod[:], [aprod_b], [], aprod_b)
        return [sout_b, aprod_b]
    return []


def rstd_fm(c, src, src_b, nchunk, width, sq, sq_b, rstd, rstd_b):
    P = c.P
    for half in range(2):
        hs = slice(half * 512, (half + 1) * 512)
        pb = c.nextps(0, 8)
        for i in range(nchunk):
            sl = i % 2
            ss = slice(sl * 512, (sl + 1) * 512)
            P.op("act", lambda e, i=i, hs=hs, ss=ss: e.activation(out=sq[:, ss], in_=src[:, i, hs], func=AF.Square), [src_b], [sq_b[sl]])
            P.op("pe", lambda e, i=i, ss=ss, pb=pb: e.matmul(c.ps[pb][:], c.ones[:], sq[:, ss], start=(i == 0), stop=(i == nchunk - 1)),
                 [c.ones_b, sq_b[sl]], [c.ps_b[pb]])
        P.op("act", lambda e, hs=hs, pb=pb: e.activation(out=rstd[:, hs], in_=c.ps[pb][:], func=AF.Sqrt, bias=c.eps[:, 0:1], scale=1.0 / width),
             [c.ps_b[pb], c.eps_b], [rstd_b[half]])
        P.op("dve", lambda e, hs=hs: e.reciprocal(out=rstd[:, hs], in_=rstd[:, hs]), [rstd_b[half]], [rstd_b[half]])


L1_FM = 30
L1_TM = 10


def l1_common_inputs(c):
    d = {"lb0f": c.din("lb0f", [128, 8]), "lb1f": c.din("lb1f", [128, 8]), "lb0t": c.din("lb0t", [128, 1024]), "lb1t": c.din("lb1t", [128, 1024]),
         "ltri": c.din("ltri", [128, 128]), "mtri": c.din("mtri", [128, 128]), "ident": c.din("ident", [128, 128])}
    return d


def build_k3():
    c = Ctx("k3")
    P = c.P
    xT = c.din("xT", [128, KC, T])
    g_mix = c.din("g_mix", [128, KC])
    wfm = c.din("wfm", [L1_FM, 128, KC * 128]); wtm = c.din("wtm", [L1_TM, 128, KC * 256])
    d = l1_common_inputs(c)
    ckvg_f = c.din("ckvg_f", [128, 2]); ckvg_t = c.din("ckvg_t", [128, 256])
    d["sloc"] = c.dout("sloc", [8, 128, 128]); d["aout"] = c.dout("aout", [128, 8])
    kvtm_o = c.dout("kvtm", [8, 128, 256]); kvT_o = c.dout("kvT", [2, 128, 1024]); kixT_o = c.dout("kixT", [128, 1024])
    zfm = c.dscr("zfm1", [L1_FM, 128, 1024]); ztm = c.dscr("ztm1", [8, 128, L1_TM * 256])
    X = c.sb("X", [128, KC, T]); X_b = P.bufs(2 * KC, "X")
    H = c.sb("H", [128, KC * T]); H_b = P.buf("H")
    WR = c.sb("WR", [128, 14336])
    FS = c.sb("FS", [128, 4096])
    Hr = H[:, :].bitcast(F32R).rearrange("p (k t) -> p k t", k=KC)
    gm_sb, gm_b = load_small(c, "gm_sb", g_mix, [128, KC])
    sq = FS[:, 0:1024]; sq_b = P.bufs(2, "sq")
    rstd = FS[:, 1024:2048]; rstd_b = P.bufs(2, "rstd")
    load_X(c, X, X_b, xT)
    rmsnorm_fm(c, X, X_b, Hr, H_b, gm_sb, gm_b, sq, sq_b, rstd, rstd_b)
    zfm_b, ztm_b = proj_phase(c, Hr, H_b, KC, wfm, L1_FM, zfm, wtm, L1_TM, ztm, WR, "k3")
    P.barrier()
    fin = hgrn_phase(c, X, X_b, H, WR, FS, zfm, zfm_b, ztm, ztm_b, d, None, sq, sq_b, rstd, rstd_b, outputs=False)
    P.barrier()
    Xf = X[:, :, :].rearrange("p k t -> p (k t)")
    gf, gf_b = load_small(c, "ckvgf_sb", ckvg_f, [128, 2])
    gt, gt_b = load_small(c, "ckvgt_sb", ckvg_t, [128, 256])
    ckT = Xf[:, 0:2048].rearrange("p (a t) -> p a t", a=2); ckT_b = P.buf("ckT")
    for a in range(2):
        P.dma("sp", ckT[:, a, :], zfm[27 + a], [zfm_b], [ckT_b], ckT_b)
    rstd_fm(c, ckT, ckT_b, 2, 256, sq, sq_b, rstd, rstd_b)
    for a in range(2):
        P.op("dve", lambda e, a=a: e.scalar_tensor_tensor(out=ckT[:, a, :], in0=ckT[:, a, :], scalar=gf[:, a:a + 1], in1=rstd[:, :], op0=ALU.mult, op1=ALU.mult),
             [ckT_b, gf_b, rstd_b[0], rstd_b[1]], [ckT_b])
        P.dma("sp", kvT_o[a], ckT[:, a, :], [ckT_b], [], ckT_b)
    kx = Xf[:, 2048:3072]; kx_b = P.buf("kx")
    P.dma("sp", kx, zfm[29], [zfm_b], [kx_b], kx_b)
    P.dma("sp", kixT_o, kx, [kx_b], [], kx_b)
    ck = Xf[:, 4096:6144].rearrange("p (j r) -> p j r", j=8); ck_b = P.bufs(8, "ck")
    junk = Xf[:, 6144:6400]; junk_b = P.buf("ckjunk")
    st = c.sb("ck_st", [128, 8, 2]); st_b = P.bufs(8, "ckst")
    P.op("dve", lambda e: e.memset(st[:], 0.0), [], st_b)
    for j in range(8):
        P.dma("sp", ck[:, j, :], ztm[j, :, 2048:2304], [ztm_b], [ck_b[j]], ck_b[j])
        P.op("act", lambda e, j=j: e.activation(out=junk, in_=ck[:, j, :], func=AF.Square, accum_out=st[:, j, 0:1]), [ck_b[j], st_b[j], junk_b], [junk_b, st_b[j]])
        P.op("act", lambda e, j=j: e.activation(out=st[:, j, 1:2], in_=st[:, j, 0:1], func=AF.Sqrt, bias=c.eps[:, 0:1], scale=1.0 / 256), [st_b[j], c.eps_b], [st_b[j]])
        P.op("dve", lambda e, j=j: e.reciprocal(out=st[:, j, 1:2], in_=st[:, j, 1:2]), [st_b[j]], [st_b[j]])
        P.op("dve", lambda e, j=j: e.scalar_tensor_tensor(out=ck[:, j, :], in0=ck[:, j, :], scalar=st[:, j, 1:2], in1=gt[:], op0=ALU.mult, op1=ALU.mult),
             [ck_b[j], st_b[j], gt_b], [ck_b[j]])
        P.dma("sp", kvtm_o[j], ck[:, j, :], [ck_b[j]], [], ck_b[j])
    return c.finish(fin + [ckT_b, kx_b] + ck_b)


def l1_weights(w_in):
    pad_fm = np.zeros((D, 128), np.float32); pad_fm[:, 0:80] = w_in[:, 4736:4816]
    fm = np.concatenate([w_in[:, 0:2048], w_in[:, 3072:4096], w_in[:, 4096:4480], w_in[:, 4480:4736], pad_fm], 1)
    pad_tm = np.zeros((D, 256), np.float32); pad_tm[:, 0:80] = w_in[:, 4736:4816]
    tm = np.concatenate([w_in[:, 1024:3072], w_in[:, 4480:4736], pad_tm], 1)
    return tile_fm(np.ascontiguousarray(fm)), tile_tm(np.ascontiguousarray(tm))


def l1_tables():
    s = np.arange(128)
    same = (s[:, None] // 64) == (s[None, :] // 64)
    tri = (same & (s[:, None] <= s[None, :])).astype(np.float32)
    return tri.copy(), tri.copy(), np.eye(128, dtype=np.float32)


def bcast_rows(v, n=128):
    return np.ascontiguousarray(np.broadcast_to(np.asarray(v)[None], (n,) + np.asarray(v).shape))


def build_k4(dsa=True, dbg=False):
    c = Ctx("k4")
    P = c.P
    xT = c.din("xT", [128, KC, T])
    g_mix = c.din("g_mix", [128, KC]); g_ffn = c.din("g_ffn", [128, KC])
    wfm = c.din("wfm", [L1_FM, 128, KC * 128]); wtm = c.din("wtm", [L1_TM, 128, KC * 256])
    d = l1_common_inputs(c)
    d["ngain"] = c.din("ngain", [128, 8]); d["sprev"] = c.din("sprev", [3, 8, 128, 128]); d["aprev"] = c.din("aprev", [128, 24])
    wout = c.din("wout", [D, D])
    wg = c.din("wg", [FC, 128, 2048]); wu = c.din("wu", [FC, 128, 2048]); wd = c.din("wd", [DFF, D])
    out = c.dout("out", [128, KC, T])
    dbg_o = c.dout("dbg", [16, 128, 1024]) if dbg else None
    zfm = c.dscr("zfm1", [L1_FM, 128, 1024]); ztm = c.dscr("ztm1", [8, 128, L1_TM * 256])
    X = c.sb("X", [128, KC, T]); X_b = P.bufs(2 * KC, "X")
    H = c.sb("H", [128, KC * T]); H_b = P.buf("H")
    WR = c.sb("WR", [128, 14336])
    FS = c.sb("FS", [128, 4096])
    Hr = H[:, :].bitcast(F32R).rearrange("p (k t) -> p k t", k=KC)
    gm_sb, gm_b = load_small(c, "gm_sb", g_mix, [128, KC])
    gf_sb, gf_b = load_small(c, "gf_sb", g_ffn, [128, KC])
    sq = FS[:, 0:1024]; sq_b = P.bufs(2, "sq")
    rstd = FS[:, 1024:2048]; rstd_b = P.bufs(2, "rstd")
    load_X(c, X, X_b, xT)
    rmsnorm_fm(c, X, X_b, Hr, H_b, gm_sb, gm_b, sq, sq_b, rstd, rstd_b)
    zfm_b, ztm_b = proj_phase(c, Hr, H_b, KC, wfm, L1_FM, zfm, wtm, L1_TM, ztm, WR, "k4")
    P.barrier()
    hgrn_phase(c, X, X_b, H, WR, FS, zfm, zfm_b, ztm, ztm_b, d, wout, sq, sq_b, rstd, rstd_b, outputs=True, dbg=dbg_o)
    P.barrier()
    if dsa:
        dsa_phase(c, X, X_b, H, WR, FS, zfm, zfm_b, ztm, ztm_b, wout, sq, sq_b, rstd, rstd_b, dbg_o)
        P.barrier()
    rmsnorm_fm(c, X, X_b, Hr, H_b, gf_sb, gf_b, sq, sq_b, rstd, rstd_b)
    ffn_phase(c, X, X_b, Hr, H_b, wg, wu, wd, WR, FS[:, 2048:3072])
    store_X(c, X, X_b, out)
    return c.finish(X_b)


BIG = 1.0e30
NBIS = 24
REPL = -3.0e38


def dsa_phase(c, X, X_b, H, WR, FS, zfm, zfm_b, ztm, ztm_b, wout_d, sq, sq_b, rstd, rstd_b, dbg=None, keys=None, prep=None):
    P = c.P
    d = {} if keys is not None else {"kvT_all": c.din("kvT_all", [2, 128, 4096]), "kvtm_all": c.din("kvtm_all", [32, 128, 256]), "kixT_all": c.din("kixT_all", [128, 4096])}
    keys_b = P.buf("dsakeys")
    if keys is not None:
        d.update(keys)
    d.update({"slotadd": c.din("slotadd", [128, 3]), "slotok": c.din("slotok", [128, 3]),
         "admadd": c.din("admadd", [128, 128]), "admok": c.din("admok", [128, 128]),
         "wuq": c.din("wuq", [16, 128, 384]), "wqi": c.din("wqi", [8, 128, 384]),
         "cqg": c.din("cqg", [128, 3]), "qng": c.din("qng", [128, 2]), "wuv": c.din("wuv", [128, 8, 2, 128]),
         "bias_g": c.din("bias_g", [128, 8, 3, 128]), "crow": c.din("crow", [128, 8]), "ident2": c.din("ident2", [128, 128])})
    xpark = c.dscr("xpark", [128, KC, T])
    qT_s = c.dscr("qT_s", [16, 128, 1024]); qiT_s = c.dscr("qiT_s", [8, 128, 1024]); at_s = c.dscr("at_s", [8, 128, 1024])
    qT_sb = P.buf("qT_s"); qiT_sb = P.buf("qiT_s"); at_sb = P.buf("at_s"); xpark_b = P.buf("xpark")
    for k in range(KC):
        P.dma("sp", xpark[:, k, :], X[:, k, :], [X_b[2 * k], X_b[2 * k + 1]], [xpark_b], X_b[8 * (k // 4)])
    P.barrier()
    if prep is not None:
        prep()
        P.barrier()
    Xf = X[:, :, :].rearrange("p k t -> p (k t)")
    sc = Xf[:, 0:4096]; sc_b = P.buf("sc")
    maskT = Xf[:, 4096:8192].rearrange("p (k t) -> p k t", k=32); maskT_b = P.buf("maskT")
    biasT = Xf[:, 8192:11264].rearrange("p (h n t) -> p h n t", h=8, n=3); biasT_b = P.buf("biasT")
    wi = Xf[:, 11264:11392].rearrange("p (j h) -> p j h", j=8)
    aw = Xf[:, 11392:11520].rearrange("p (j h) -> p j h", j=8)
    sg = Xf[:, 11520:11648].rearrange("p (j h) -> p j h", j=8); wi_b = P.buf("wi")
    admadd = Xf[:, 11648:11776]; admok = Xf[:, 11776:11904]; adm_b = P.buf("adm")
    m8 = Xf[:, 11904:11912]; m8_b = P.buf("m8")
    bs = Xf[:, 11912:11920]; bs_b = P.buf("bisect")
    junk = Xf[:, 12288:16384]; junk_b = P.buf("bjunk")
    qraw = Xf[:, 12288:14336].rearrange("p (a t) -> p a t", a=2); qraw_b = P.buf("qraw")
    qout = Xf[:, 14336:16384].rearrange("p (a t) -> p a t", a=2); qout_b = P.buf("qout")
    kvT = H[:, 0:8192].rearrange("p (a s) -> p a s", a=2); kvT_b = P.buf("kvTall")
    kvtm = H[:, 8192:16384].rearrange("p (k r) -> p k r", k=32); kvtm_b = P.buf("kvtmall")
    kix = WR[:, 0:4096]; kix_b = P.buf("kixall")
    sel = WR[:, 4096:8192]; sel_b = P.buf("sel")
    E = [WR[:, 4096 + s * 512:4096 + (s + 1) * 512] for s in range(3)]; E_b = P.bufs(3, "E")
    onT = WR[:, 6144:7168].rearrange("p (a t) -> p a t", a=2); onT_b = P.buf("onT")
    qblk = WR[:, 8192:10240].rearrange("p (n t) -> p n t", n=16); qblk_b = P.buf("qblk")
    qiblk = WR[:, 10240:11264].rearrange("p (n t) -> p n t", n=8); qiblk_b = P.buf("qiblk")
    wuv = WR[:, 11264:13312].rearrange("p (h a v) -> p h a v", h=8, a=2); wuv_b = P.buf("wuv")
    ident = WR[:, 13312:13440]; onesR = WR[:, 13440:13568]; id_b = P.buf("ident2")
    cqT = WR[:, 8192:11264].rearrange("p (a t) -> p a t", a=3); cqT_b = P.buf("cqT")
    wq = [WR[:, 11264 + s * 384:11264 + (s + 1) * 384].rearrange("p (k j) -> p k j", k=3) for s in range(2)]; wq_b = P.bufs(2, "wq")
    rtmp = [FS[:, 2048 + s * 512:2048 + (s + 1) * 512] for s in range(2)]; rtmp_b = P.bufs(2, "rtmp")
    ltmp = [FS[:, 2560 + s * 512:2560 + (s + 1) * 512] for s in range(3)]; ltmp_b = P.bufs(3, "ltmp")
    rden = FS[:, 2048:2560]
    cqg, cqg_b = load_small(c, "cqg_sb", d["cqg"], [128, 3])
    qng, qng_b = load_small(c, "qng_sb", d["qng"], [128, 2])
    slotadd, sla_b = load_small(c, "slotadd_sb", d["slotadd"], [128, 3])
    slotok, slo_b = load_small(c, "slotok_sb", d["slotok"], [128, 3])
    crow, crow_b = load_small(c, "crow_sb", d["crow"], [128, 8])
    P.dma("pool", R_(ident), d["ident2"], [], [id_b], None)
    P.op("dve", lambda e: e.tensor_scalar(out=R_(onesR), in0=ident, scalar1=0.0, scalar2=1.0, op0=ALU.mult, op1=ALU.add), [id_b], [id_b])
    P.dma("sp", admadd, d["admadd"], [], [adm_b], None)
    P.dma("sp", admok, d["admok"], [], [adm_b], None)
    for a in range(3):
        P.dma("pool", R_(cqT[:, a, :]), zfm[24 + a], [zfm_b], [cqT_b], cqT_b)
    rstd_fm(c, cqT, cqT_b, 3, 384, sq, sq_b, rstd, rstd_b)
    for a in range(3):
        P.op("dve", lambda e, a=a: e.scalar_tensor_tensor(out=R_(cqT[:, a, :]), in0=cqT[:, a, :], scalar=cqg[:, a:a + 1], in1=rstd[:, :], op0=ALU.mult, op1=ALU.mult),
             [cqT_b, cqg_b, rstd_b[0], rstd_b[1]], [cqT_b])
    nw = 0

    def small_proj(w_d, n, dst, dst_b, act_eng):
        nonlocal nw
        s = nw % 2
        nw += 1
        P.dma("pool", R_(wq[s]), w_d[n].rearrange("p (k j) -> p k j", k=3), [], [wq_b[s]], wq_b[s])
        for half in range(2):
            hs = slice(half * 512, (half + 1) * 512)
            pb = c.nextps(0, 8)
            for k in range(3):
                P.op("pe", lambda e, s=s, k=k, hs=hs, pb=pb: e.matmul(c.ps[pb][:], R_(wq[s][:, k, :]), R_(cqT[:, k, hs]), start=(k == 0), stop=(k == 2)),
                     [wq_b[s], cqT_b], [c.ps_b[pb]])
            P.op(act_eng, (lambda e, hs=hs, pb=pb: e.copy(out=dst[:, hs], in_=c.ps[pb][:])) if act_eng == "act" else
                 (lambda e, hs=hs, pb=pb: e.tensor_copy(out=dst[:, hs], in_=c.ps[pb][:])), [c.ps_b[pb]], [dst_b])

    for h in range(8):
        for a in range(2):
            small_proj(d["wuq"], 2 * h + a, qraw[:, a, :], qraw_b, "act")
        rstd_fm(c, qraw, qraw_b, 2, 256, sq, sq_b, rstd, rstd_b)
        for a in range(2):
            P.op("dve", lambda e, a=a: e.scalar_tensor_tensor(out=qout[:, a, :], in0=qraw[:, a, :], scalar=qng[:, a:a + 1], in1=rstd[:, :], op0=ALU.mult, op1=ALU.mult),
                 [qraw_b, qng_b, rstd_b[0], rstd_b[1]], [qout_b])
            P.dma("sp", qT_s[2 * h + a], qout[:, a, :], [qout_b], [qT_sb], qout_b)
    for n in range(8):
        a = n % 2
        small_proj(d["wqi"], n, qout[:, a, :], qout_b, "dve")
        P.dma("sp", qiT_s[n], qout[:, a, :], [qout_b], [qiT_sb], qout_b)
    P.barrier()
    for a in range(2):
        P.dma("pool", R_(kvT[:, a, :]), d["kvT_all"][a], [keys_b], [kvT_b], None)
    for k4 in range(4):
        P.dma("pool", R_(kvtm[:, k4 * 8:(k4 + 1) * 8, :]), d["kvtm_all"][k4 * 8:(k4 + 1) * 8].rearrange("k p r -> p k r"), [keys_b], [kvtm_b], None)
    P.dma("pool", R_(kix), d["kixT_all"], [keys_b], [kix_b], None)
    P.dma("pool", R_(wuv), d["wuv"], [], [wuv_b], None)
    P.dma("sp", wi, ztm[:, :, 2304 + 64:2304 + 80].rearrange("j p h -> p j h"), [ztm_b], [wi_b], None)
    P.op("act", lambda e: e.activation(out=sg, in_=wi, func=AF.Sign), [wi_b], [wi_b])
    P.op("dve", lambda e: e.tensor_tensor(out=aw, in0=wi, in1=sg, op=ALU.mult), [wi_b], [wi_b])
    P.op("dve", lambda e: e.tensor_scalar(out=aw, in0=aw, scalar1=1.0 / 32.0, scalar2=None, op0=ALU.mult), [wi_b], [wi_b])
    P.dma("sp", biasT, d["bias_g"], [], [biasT_b], None)
    for h in range(8):
        P.op("dve", lambda e, h=h: e.tensor_scalar(out=biasT[:, h], in0=biasT[:, h], scalar1=crow[:, h:h + 1], scalar2=16.0, op0=ALU.subtract, op1=ALU.mult),
             [biasT_b, crow_b], [biasT_b])
    for qb in range(8):
        qs_ = slice(qb * 128, (qb + 1) * 128)
        nown = (qb + 1) * 128
        P.dma("pool", R_(qblk), qT_s[:, :, qs_].rearrange("n p t -> p n t"), [qT_sb], [qblk_b], qblk_b)
        P.dma("pool", R_(qiblk), qiT_s[:, :, qs_].rearrange("n p t -> p n t"), [qiT_sb], [qiblk_b], qiblk_b)
        tiles = [(lo, min(lo + 512, nown)) for lo in range(0, nown, 512)] + [(lo, lo + 512) for lo in range(1024, 4096, 512)]
        cnt = 0
        for (lo, hi_) in tiles:
            n = hi_ - lo
            for hd in range(16):
                pr = slice((hd % 2) * 64, (hd % 2) * 64 + 64)
                pb = c.nextps(3, 5)
                P.op("pe", lambda e, hd=hd, pr=pr, lo=lo, hi_=hi_, n=n, pb=pb: e.matmul(c.ps[pb][:, 0:n], R_(qiblk[pr, hd // 2, :]), R_(kix[pr, lo:hi_]), start=True, stop=True),
                     [qiblk_b, kix_b], [c.ps_b[pb]])
                s = cnt % 2
                cnt += 1
                P.op("act", lambda e, hd=hd, n=n, pb=pb, s=s, qb=qb: e.activation(out=rtmp[s][:, 0:n], in_=c.ps[pb][:, 0:n], func=AF.Relu, scale=aw[:, qb, hd:hd + 1]),
                     [c.ps_b[pb], wi_b], [rtmp_b[s]])
                if hd == 0:
                    P.op("dve", lambda e, lo=lo, hi_=hi_, n=n, s=s, qb=qb, hd=hd: e.tensor_scalar(out=sc[:, lo:hi_], in0=rtmp[s][:, 0:n], scalar1=sg[:, qb, hd:hd + 1], scalar2=None, op0=ALU.mult),
                         [rtmp_b[s], wi_b], [sc_b])
                else:
                    P.op("dve", lambda e, lo=lo, hi_=hi_, n=n, s=s, qb=qb, hd=hd: e.scalar_tensor_tensor(out=sc[:, lo:hi_], in0=rtmp[s][:, 0:n], scalar=sg[:, qb, hd:hd + 1], in1=sc[:, lo:hi_],
                                                                                                  op0=ALU.mult, op1=ALU.add), [rtmp_b[s], wi_b, sc_b], [sc_b])
        dsl = slice(qb * 128, (qb + 1) * 128)
        if nown < 1024:
            P.op("dve", lambda e, nown=nown: e.memset(sc[:, nown:1024], 0.0), [sc_b], [sc_b])
        P.op("dve", lambda e: e.tensor_reduce(out=bs[:, 5:6], in_=sc, axis=AX.X, op=ALU.max, apply_absolute_value=True), [sc_b, bs_b], [bs_b])
        P.op("dve", lambda e: e.tensor_scalar(out=bs[:, 0:1], in0=bs[:, 5:6], scalar1=-1.0, scalar2=-1.0, op0=ALU.mult, op1=ALU.add), [bs_b], [bs_b])
        P.op("dve", lambda e: e.tensor_scalar(out=bs[:, 1:2], in0=bs[:, 5:6], scalar1=2.0, scalar2=2.0, op0=ALU.mult, op1=ALU.add), [bs_b], [bs_b])
        P.op("dve", lambda e, dsl=dsl: e.tensor_tensor(out=sc[:, dsl], in0=sc[:, dsl], in1=admadd, op=ALU.add), [sc_b, adm_b], [sc_b])
        if nown < 1024:
            P.op("dve", lambda e, nown=nown: e.memset(sc[:, nown:1024], -BIG), [sc_b], [sc_b])
        for i in range(3):
            P.op("dve", lambda e, i=i: e.tensor_scalar(out=sc[:, 1024 * (i + 1):1024 * (i + 2)], in0=sc[:, 1024 * (i + 1):1024 * (i + 2)], scalar1=slotadd[:, i:i + 1], scalar2=None, op0=ALU.add),
                 [sc_b, sla_b], [sc_b])
        for it in range(NBIS):
            P.op("dve", lambda e: e.tensor_scalar(out=bs[:, 1:2], in0=bs[:, 1:2], scalar1=0.5, scalar2=None, op0=ALU.mult), [bs_b], [bs_b])
            P.op("dve", lambda e: e.tensor_tensor(out=bs[:, 2:3], in0=bs[:, 0:1], in1=bs[:, 1:2], op=ALU.add), [bs_b], [bs_b])
            P.op("dve", lambda e: e.tensor_scalar(out=junk, in0=sc, scalar1=bs[:, 2:3], scalar2=0.0, op0=ALU.is_ge, op1=ALU.add, accum_out=bs[:, 3:4]), [sc_b, bs_b, junk_b], [junk_b, bs_b])
            P.op("dve", lambda e: e.tensor_scalar(out=bs[:, 4:5], in0=bs[:, 3:4], scalar1=255.5, scalar2=None, op0=ALU.is_ge), [bs_b], [bs_b])
            P.op("dve", lambda e: e.scalar_tensor_tensor(out=bs[:, 0:1], in0=bs[:, 4:5], scalar=bs[:, 1:2], in1=bs[:, 0:1], op0=ALU.mult, op1=ALU.add), [bs_b], [bs_b])
        P.barrier()
        P.op("dve", lambda e: e.tensor_scalar(out=R_(sel), in0=sc, scalar1=bs[:, 0:1], scalar2=None, op0=ALU.is_ge), [sc_b, bs_b], [sel_b])
        P.op("dve", lambda e, dsl=dsl: e.tensor_tensor(out=R_(sel[:, dsl]), in0=sel[:, dsl], in1=admok, op=ALU.mult), [sel_b, adm_b], [sel_b])
        if nown < 1024:
            P.op("dve", lambda e, nown=nown: e.tensor_scalar(out=R_(sel[:, nown:1024]), in0=sel[:, nown:1024], scalar1=0.0, scalar2=None, op0=ALU.mult), [sel_b], [sel_b])
        for i in range(3):
            P.op("dve", lambda e, i=i: e.tensor_scalar(out=R_(sel[:, 1024 * (i + 1):1024 * (i + 2)]), in0=sel[:, 1024 * (i + 1):1024 * (i + 2)], scalar1=slotok[:, i:i + 1], scalar2=None, op0=ALU.mult),
                 [sel_b, slo_b], [sel_b])
        kbs = list(range(qb + 1)) + list(range(8, 32))
        for kb in kbs:
            pb = c.nextps(3, 5)
            P.op("pe", lambda e, kb=kb, pb=pb: e.matmul(c.ps[pb][:, 0:128], R_(sel[:, kb * 128:(kb + 1) * 128]), R_(ident), start=True, stop=True), [sel_b, id_b], [c.ps_b[pb]])
            P.op("dve", lambda e, kb=kb, pb=pb: e.tensor_scalar(out=maskT[:, kb, :], in0=c.ps[pb][:, 0:128], scalar1=-1.0, scalar2=BIG, op0=ALU.add, op1=ALU.mult), [c.ps_b[pb]], [maskT_b])
        P.barrier()
        near = {}
        for g in (0, -1, -2):
            kb = qb + g
            near[kb if kb >= 0 else 16 + kb] = 2 + g
        NS = 3
        for hg in range(2):
            def emit_logits(ki, kb, hg=hg):
                ks = slice(kb * 128, (kb + 1) * 128)
                pb = c.nextps(3, 5)
                for a in range(2):
                    P.op("pe", lambda e, a=a, ks=ks, pb=pb, hg=hg: e.matmul(c.ps[pb][:], R_(kvT[:, a, ks]), R_(qblk[:, 8 * hg + a:8 * hg + 8:2, :]), start=(a == 0), stop=(a == 1)),
                         [kvT_b, qblk_b], [c.ps_b[pb]])
                return pb

            def emit_rest(ki, kb, pb, hg=hg):
                s = ki % NS
                lt3 = ltmp[s].rearrange("p (h t) -> p h t", h=4)
                P.op("dve", lambda e, kb=kb, pb=pb, lt3=lt3: e.tensor_tensor(out=lt3, in0=c.ps[pb][:].rearrange("p (h t) -> p h t", h=4), in1=maskT[:, kb:kb + 1, :].to_broadcast([128, 4, 128]), op=ALU.add),
                     [c.ps_b[pb], maskT_b], [ltmp_b[s]])
                if kb in near:
                    nb = near[kb]
                    P.op("dve", lambda e, lt3=lt3, nb=nb, hg=hg: e.tensor_tensor(out=lt3, in0=lt3, in1=biasT[:, 4 * hg:4 * hg + 4, nb, :], op=ALU.add), [ltmp_b[s], biasT_b], [ltmp_b[s]])
                P.op("act", lambda e, s=s: e.activation(out=R_(E[s]), in_=ltmp[s], func=AF.Exp, scale=1.0 / 16.0), [ltmp_b[s]], [E_b[s]])
                first, last = (ki == 0), (ki == len(kbs) - 1)
                for a in range(2):
                    P.op("pe", lambda e, a=a, kb=kb, s=s, first=first, last=last: e.matmul(c.ps[a][:], R_(kvtm[:, kb, a * 128:(a + 1) * 128]), R_(E[s]), start=first, stop=last),
                         [kvtm_b, E_b[s]], [c.ps_b[a]])
                P.op("pe", lambda e, s=s, first=first, last=last: e.matmul(c.ps[2][:], R_(onesR), R_(E[s]), start=first, stop=last), [id_b, E_b[s]], [c.ps_b[2]])

            pend = [emit_logits(0, kbs[0])]
            if len(kbs) > 1:
                pend.append(emit_logits(1, kbs[1]))
            for ki, kb in enumerate(kbs):
                if ki + 2 < len(kbs):
                    pend.append(emit_logits(ki + 2, kbs[ki + 2]))
                emit_rest(ki, kb, pend[ki])
            P.op("dve", lambda e: e.reciprocal(out=rden, in_=c.ps[2][:]), [c.ps_b[2], rtmp_b[0]], [rtmp_b[0]])
            for a in range(2):
                P.op("dve", lambda e, a=a: e.tensor_tensor(out=R_(onT[:, a, :]), in0=c.ps[a][:], in1=rden, op=ALU.mult), [c.ps_b[a], rtmp_b[0]], [onT_b])
            for hh in range(4):
                h = 4 * hg + hh
                pb = c.nextps(3, 5)
                for a in range(2):
                    P.op("pe", lambda e, a=a, h=h, hh=hh, pb=pb: e.matmul(c.ps[pb][:, 0:128], R_(wuv[:, h, a, :]), R_(onT[:, a, hh * 128:(hh + 1) * 128]), start=(a == 0), stop=(a == 1)),
                         [wuv_b, onT_b], [c.ps_b[pb]])
                s2 = hh % 2
                P.op("act", lambda e, pb=pb, s2=s2: e.copy(out=ltmp[s2][:, 0:128], in_=c.ps[pb][:, 0:128]), [c.ps_b[pb]], [ltmp_b[s2]])
                P.dma("sp", at_s[h, :, qs_], ltmp[s2][:, 0:128], [ltmp_b[s2]], [at_sb], ltmp_b[s2])
        P.barrier()
    P.barrier()
    for k in range(KC):
        P.dma("sp", X[:, k, :], xpark[:, k, :], [xpark_b], [X_b[2 * k], X_b[2 * k + 1]], None)
    mixedT = H[:, 0:2048].rearrange("p (a t) -> p a t", a=2); mixed_b = P.buf("dmixedT")
    wo = [R_(WR[:, 9216 + s * 2048:9216 + (s + 1) * 2048]) for s in range(2)]; wo_b = P.bufs(2, "wo")
    for hp in range(4):
        for a in range(2):
            P.dma("pool", R_(mixedT[:, a, :]), at_s[2 * hp + a], [at_sb], [mixed_b], mixed_b)
            if dbg is not None:
                P.dma("sp", dbg[8 + 2 * hp + a], mixedT[:, a, :], [mixed_b], [], mixed_b)
        outproj_partial(c, X, X_b, R_(mixedT), mixed_b, wout_d, [8 + 2 * hp, 8 + 2 * hp + 1], wo, wo_b)


def rel_bucket_np(rel):
    rel = np.asarray(rel, np.int32)
    ret = np.where(rel > 0, 16, 0)
    n = np.abs(rel)
    nf = np.maximum(n, 1).astype(np.float32)
    large = 8 + (np.log(nf / np.float32(8)) / np.float32(math.log(256 / 8)) * np.float32(8)).astype(np.int32)
    large = np.minimum(large, 15)
    return ret + np.where(n < 8, n, large)


def dsa_host_inputs(inp, k3res, b, q):
    kvT_all = np.zeros((2, 128, 4096), np.float32)
    kvtm_all = np.zeros((32, 128, 256), np.float32)
    kix_all = np.zeros((128, 4096), np.float32)
    slotadd = np.zeros((128, 3), np.float32)
    slotok = np.ones((128, 3), np.float32)
    srcs = [q] + [q - 1 - i for i in range(3)]
    for sl, sq_ in enumerate(srcs):
        if sq_ < 0:
            slotadd[:, sl - 1] = -BIG
            slotok[:, sl - 1] = 0.0
            continue
        r = k3res[b * 4 + sq_]
        kvT_all[:, :, sl * 1024:(sl + 1) * 1024] = r["kvT"]
        kvtm_all[sl * 8:(sl + 1) * 8] = r["kvtm"]
        kix_all[0:64, sl * 1024:(sl + 1) * 1024] = r["kixT"][0:64]
        kix_all[64:128, sl * 1024:(sl + 1) * 1024] = r["kixT"][0:64]
    t = np.arange(128)
    ok = (t[None, :] // 64) <= (t[:, None] // 64)
    admok = ok.astype(np.float32)
    admadd = np.where(ok, 0.0, -BIG).astype(np.float32)
    s_ = np.arange(128)[:, None, None]; nb = np.arange(3)[None, :, None]; tt = np.arange(128)[None, None, :]
    rel = (nb - 2) * 128 + s_ - tt
    bk = rel_bucket_np(rel)
    rb = inp["rel_bias"]
    bias_g = np.ascontiguousarray(rb[bk].transpose(0, 3, 1, 2))
    wuv = inp["dsa_w_uv"][0]
    return {"kvT_all": kvT_all, "kvtm_all": kvtm_all, "kixT_all": kix_all, "slotadd": slotadd, "slotok": slotok, "admadd": admadd, "admok": admok,
            "wuq": tile_fm(inp["dsa_w_uq"][0]), "wqi": tile_fm(inp["dsa_w_qidx"][0]), "cqg": vec_fm(inp["dsa_cq_g"][0]), "qng": vec_fm(inp["dsa_qnorm_g"][0]),
            "wuv": np.ascontiguousarray(wuv.reshape(8, 2, 128, 128).transpose(2, 0, 1, 3)), "bias_g": bias_g.astype(np.float32), "crow": bcast_rows(rb[15]),
            "ident2": np.eye(128, dtype=np.float32)}


_PROGS = {}


def _prog(name, fn):
    if name not in _PROGS:
        _PROGS[name] = fn()
    return _PROGS[name]


def _run(nc, ims):
    return run_bass_kernel_spmd(nc, ims, core_ids=list(range(NCORES))).results


def kernel_unfused(x, ln_mix_g, ln_ffn_g, w_ffn_gate, w_ffn_up, w_ffn_down, rel_bias,
           ev_w_in, ev_w_out, sgu_ln_g, sgu_ln_b, sgu_w_s, sgu_b_s,
           od_w_in, od_w_out, hgrn_lb, hgrn_norm_g, dsa_cq_g, dsa_ckv_g,
           dsa_w_uq, dsa_qnorm_g, dsa_w_qidx, dsa_w_uv):
    f = lambda a: np.ascontiguousarray(np.asarray(a, dtype=np.float32))
    x = f(x)
    inp = {"rel_bias": f(rel_bias), "dsa_w_uq": f(dsa_w_uq), "dsa_w_qidx": f(dsa_w_qidx), "dsa_cq_g": f(dsa_cq_g),
           "dsa_qnorm_g": f(dsa_qnorm_g), "dsa_w_uv": f(dsa_w_uv)}
    cores = [(c // 4, c % 4) for c in range(NCORES)]
    xs = [x_to_fm(x[b, q * T:(q + 1) * T]) for b, q in cores]
    ropes = [rope_tables(q) for q in range(4)]
    w_in = f(ev_w_in)[0]
    gm0 = vec_fm(f(ln_mix_g)[0])
    wtm1 = tile_tm(np.ascontiguousarray(w_in[:, 1024:3072]))
    kz1 = k1_tables()
    ims = [{"xT": xs[c], "g": gm0, "wtm": wtm1, "cos_tm": ropes[q][1][0], "sin_tm": ropes[q][1][1], "kz1": kz1} for c, (b, q) in enumerate(cores)]
    r1 = _run(_prog("k1", build_k1), ims)
    del wtm1
    wfm = tile_fm(np.ascontiguousarray(np.concatenate([w_in[:, 0:2048], w_in[:, 3072:5120]], 1)))
    wtm = tile_tm(np.ascontiguousarray(np.concatenate([w_in[:, 1024:3072], w_in[:, 5120:6144]], 1)))
    common = {"g_mix": gm0, "g_ffn": vec_fm(f(ln_ffn_g)[0]), "wfm": wfm, "wtm": wtm,
              "lng_b": bcast_rows(f(sgu_ln_g)[0]), "lnb_b": bcast_rows(f(sgu_ln_b)[0]),
              "wsT": np.ascontiguousarray(f(sgu_w_s)[0].transpose(2, 0, 1)), "bsb": bcast_rows(f(sgu_b_s)[0]),
              "wout": f(ev_w_out)[0], "wg": tile_fm(f(w_ffn_gate)[0]), "wu": tile_fm(f(w_ffn_up)[0]), "wd": f(w_ffn_down)[0]}
    ims = []
    for c, (b, q) in enumerate(cores):
        dmask, xi, kz, sw = k2_tables(q)
        im = dict(common)
        im.update({"xT": xs[c], "cosT": ropes[q][0][0], "sinT": ropes[q][0][1], "cos_tm": ropes[q][1][0], "sin_tm": ropes[q][1][1],
                   "dmask": dmask, "xi": xi, "kz": kz, "sw": sw, "sprev": np.stack([r1[b * 4 + i]["sloc"] for i in range(3)])})
        ims.append(im)
    r2 = _run(_prog("k2", build_k2), ims)
    del common, ims, wfm, wtm
    x1s = [r["out"] for r in r2]
    wfm, wtm = l1_weights(f(od_w_in)[0])
    ltri, mtri, ident = l1_tables()
    lb = f(hgrn_lb)
    common = {"g_mix": vec_fm(f(ln_mix_g)[1]), "wfm": wfm, "wtm": wtm, "lb0f": vec_fm(lb[0]), "lb1f": vec_fm(lb[1]),
              "lb0t": bcast_rows(lb[0]), "lb1t": bcast_rows(lb[1]), "ltri": ltri, "mtri": mtri, "ident": ident}
    ckvg = f(dsa_ckv_g)[0]
    ims = []
    for c in range(NCORES):
        im = dict(common)
        im.update({"xT": x1s[c], "ckvg_f": vec_fm(ckvg), "ckvg_t": bcast_rows(ckvg)})
        ims.append(im)
    r3 = _run(_prog("k3", build_k3), ims)
    common.update({"g_ffn": vec_fm(f(ln_ffn_g)[1]), "ngain": vec_fm(f(hgrn_norm_g)[0]), "wout": f(od_w_out)[0],
                   "wg": tile_fm(f(w_ffn_gate)[1]), "wu": tile_fm(f(w_ffn_up)[1]), "wd": f(w_ffn_down)[1]})
    ims = []
    for c, (b, q) in enumerate(cores):
        sprev = np.zeros((3, 8, 128, 128), np.float32)
        aprev = np.ones((128, 3, 8), np.float32)
        for i in range(3):
            if i < q:
                sprev[i] = r3[b * 4 + i]["sloc"]
                aprev[:, i, :] = r3[b * 4 + i]["aout"]
        im = dict(common)
        im.update({"xT": x1s[c], "sprev": sprev, "aprev": np.ascontiguousarray(aprev.reshape(128, 24))})
        im.update(dsa_host_inputs(inp, r3, b, q))
        ims.append(im)
    r4 = _run(_prog("k4", lambda: build_k4(dsa=True, dbg=False)), ims)
    out = np.zeros((2, 4096, D), np.float32)
    for c, (b, q) in enumerate(cores):
        out[b, q * T:(q + 1) * T] = fm_to_x(r4[c]["out"])
    return out


def dsa_prep(c, X, zfm, zfm_b, ztm, ztm_b, sq, sq_b, rstd, rstd_b, ckvg_f, ckvg_t, kvT_all, kvtm_all, kix_all, slot):
    P = c.P
    keys_b = P.buf("dsakeys")
    cs = slice(slot * 1024, (slot + 1) * 1024)
    Xf = X[:, :, :].rearrange("p k t -> p (k t)")
    gf, gf_b = load_small(c, "ckvgf_sb", ckvg_f, [128, 2])
    gt, gt_b = load_small(c, "ckvgt_sb", ckvg_t, [128, 256])
    ckT = Xf[:, 0:2048].rearrange("p (a t) -> p a t", a=2); ckT_b = P.buf("ckT")
    for a in range(2):
        P.dma("sp", ckT[:, a, :], zfm[27 + a], [zfm_b], [ckT_b], None)
    rstd_fm(c, ckT, ckT_b, 2, 256, sq, sq_b, rstd, rstd_b)
    for a in range(2):
        P.op("dve", lambda e, a=a: e.scalar_tensor_tensor(out=ckT[:, a, :], in0=ckT[:, a, :], scalar=gf[:, a:a + 1], in1=rstd[:, :], op0=ALU.mult, op1=ALU.mult),
             [ckT_b, gf_b, rstd_b[0], rstd_b[1]], [ckT_b])
        P.dma("sp", kvT_all[a][:, cs], ckT[:, a, :], [ckT_b], [keys_b], None)
    kx = Xf[:, 2048:3072]; kx_b = P.buf("kx")
    P.dma("sp", kx, zfm[29], [zfm_b], [kx_b], None)
    P.dma("sp", kix_all[0:64, cs], kx[0:64, :], [kx_b], [keys_b], None)
    P.dma("sp", kix_all[64:128, cs], kx[0:64, :], [kx_b], [keys_b], None)
    ck = Xf[:, 4096:6144].rearrange("p (j r) -> p j r", j=8); ck_b = P.buf("ck")
    junk = Xf[:, 6144:6400]; junk_b = P.buf("ckjunk")
    st = c.sb("ck_st", [128, 8, 2]); st_b = P.buf("ckst")
    P.op("dve", lambda e: e.memset(st[:], 0.0), [], [st_b])
    P.dma("sp", ck, ztm[:, :, 2048:2304].rearrange("j p r -> p j r"), [ztm_b], [ck_b], None)
    for j in range(8):
        P.op("act", lambda e, j=j: e.activation(out=junk, in_=ck[:, j, :], func=AF.Square, accum_out=st[:, j, 0:1]), [ck_b, st_b, junk_b], [junk_b, st_b])
        P.op("act", lambda e, j=j: e.activation(out=st[:, j, 1:2], in_=st[:, j, 0:1], func=AF.Sqrt, bias=c.eps[:, 0:1], scale=1.0 / 256), [st_b, c.eps_b], [st_b])
        P.op("dve", lambda e, j=j: e.reciprocal(out=st[:, j, 1:2], in_=st[:, j, 1:2]), [st_b], [st_b])
        P.op("dve", lambda e, j=j: e.scalar_tensor_tensor(out=ck[:, j, :], in0=ck[:, j, :], scalar=st[:, j, 1:2], in1=gt[:], op0=ALU.mult, op1=ALU.mult),
             [ck_b, st_b, gt_b], [ck_b])
    P.dma("sp", kvtm_all[slot * 8:(slot + 1) * 8].rearrange("k p r -> p k r"), ck, [ck_b], [keys_b], None)


def build_fused():
    c = Ctx("fused")
    P = c.P
    x_slots = c.din("x_slots", [4, 128, KC, T])
    out = c.dout("out", [128, KC, T])
    g_mix0 = c.din("g_mix0", [128, KC]); g_ffn0 = c.din("g_ffn0", [128, KC])
    g_mix1 = c.din("g_mix1", [128, KC]); g_ffn1 = c.din("g_ffn1", [128, KC])
    wfm0 = c.din("wfm0", [32, 128, KC * 128]); wtm0 = c.din("wtm0", [12, 128, KC * 256])
    wfm1 = c.din("wfm1", [L1_FM, 128, KC * 128]); wtm1 = c.din("wtm1", [L1_TM, 128, KC * 256])
    cosT = c.din("cosT", [4, 128, 1024]); sinT = c.din("sinT", [4, 128, 1024])
    cos_tm = c.din("cos_tm", [4, 128, 8, 128]); sin_tm = c.din("sin_tm", [4, 128, 8, 128])
    d0 = {"dmask": c.din("dmask", [128, 4, 128]), "xi": c.din("xi", [128, 4, 128]), "kz": c.din("kz", [128, 4]),
          "lng_b": c.din("lng_b", [128, 1024]), "lnb_b": c.din("lnb_b", [128, 1024]),
          "wsT": c.din("wsT", [128, 4, 128]), "bsb": c.din("bsb", [128, 4, 128])}
    wout0 = c.din("wout0", [D, D]); wout1 = c.din("wout1", [D, D])
    wg0 = c.din("wg0", [FC, 128, 2048]); wu0 = c.din("wu0", [FC, 128, 2048]); wd0 = c.din("wd0", [DFF, D])
    wg1 = c.din("wg1", [FC, 128, 2048]); wu1 = c.din("wu1", [FC, 128, 2048]); wd1 = c.din("wd1", [DFF, D])
    d1 = l1_common_inputs(c)
    d1["ngain"] = c.din("ngain", [128, 8])
    ckvg_f = c.din("ckvg_f", [128, 2]); ckvg_t = c.din("ckvg_t", [128, 256])
    zfm0 = c.dscr("zfm0", [32, 128, 1024]); ztm0 = c.dscr("ztm0", [8, 128, 3072])
    zfm1 = c.dscr("zfm1", [L1_FM, 128, 1024]); ztm1 = c.dscr("ztm1", [8, 128, L1_TM * 256])
    sstate = c.dscr("sstate", [4, 2, 128, 256]); hstate = c.dscr("hstate", [8, 128, 128])
    keys = {"kvT_all": c.dscr("kvT_all_s", [2, 128, 4096]), "kvtm_all": c.dscr("kvtm_all_s", [32, 128, 256]), "kixT_all": c.dscr("kix_all_s", [128, 4096])}
    X = c.sb("X", [128, KC, T]); X_b = P.bufs(2 * KC, "X")
    H = c.sb("H", [128, KC * T]); H_b = P.buf("H")
    WR = c.sb("WR", [128, 14336])
    FS = c.sb("FS", [128, 4096])
    Hr = H[:, :].bitcast(F32R).rearrange("p (k t) -> p k t", k=KC)
    gm0, gm0_b = load_small(c, "gm0_sb", g_mix0, [128, KC]); gf0, gf0_b = load_small(c, "gf0_sb", g_ffn0, [128, KC])
    gm1, gm1_b = load_small(c, "gm1_sb", g_mix1, [128, KC]); gf1, gf1_b = load_small(c, "gf1_sb", g_ffn1, [128, KC])
    sq = FS[:, 0:1024]; sq_b = P.bufs(2, "sq")
    rstd = FS[:, 1024:2048]; rstd_b = P.bufs(2, "rstd")
    ftmp = FS[:, 2048:3072]
    ftmp2 = [FS[:, 3072:3584], FS[:, 3584:4096]]
    pre_fm = list(range(8, 16)) + [27, 28, 29]
    pre_tm = list(range(0, 9))
    for p in range(4):
        P.barrier()
        load_X(c, X, X_b, x_slots[p])
        rmsnorm_fm(c, X, X_b, Hr, H_b, gm0, gm0_b, sq, sq_b, rstd, rstd_b)
        zfm_b, ztm_b = proj_phase(c, Hr, H_b, KC, wfm0, 32, zfm0, wtm0, 12, ztm0, WR, "p0")
        P.barrier()
        dd = dict(d0); dd.update({"cosT": cosT[p], "sinT": sinT[p], "cos_tm": cos_tm[p], "sin_tm": sin_tm[p]})
        retention_phase(c, X, X_b, H, WR, FS, zfm0, zfm_b, ztm0, ztm_b, dd, wout0, sq, sq_b, rstd, rstd_b, sstate=sstate, first=(p == 0))
        P.barrier()
        sgu_phase(c, X, X_b, H, WR, FS, zfm0, zfm_b, ztm0, ztm_b, dd, wout0, 24, 2048)
        P.barrier()
        rmsnorm_fm(c, X, X_b, Hr, H_b, gf0, gf0_b, sq, sq_b, rstd, rstd_b)
        ffn_phase(c, X, X_b, Hr, H_b, wg0, wu0, wd0, WR, ftmp, ftmp2)
        P.barrier()
        rmsnorm_fm(c, X, X_b, Hr, H_b, gm1, gm1_b, sq, sq_b, rstd, rstd_b)
        if p < 3:
            zfm_b, ztm_b = proj_phase(c, Hr, H_b, KC, wfm1, L1_FM, zfm1, wtm1, L1_TM, ztm1, WR, "p1", fm_list=pre_fm, tm_list=pre_tm)
            P.barrier()
            hgrn_phase(c, X, X_b, H, WR, FS, zfm1, zfm_b, ztm1, ztm_b, d1, None, sq, sq_b, rstd, rstd_b, outputs=False, hstate=hstate, first=(p == 0))
            P.barrier()
            dsa_prep(c, X, zfm1, zfm_b, ztm1, ztm_b, sq, sq_b, rstd, rstd_b, ckvg_f, ckvg_t, keys["kvT_all"], keys["kvtm_all"], keys["kixT_all"], 3 - p)
        else:
            zfm_b, ztm_b = proj_phase(c, Hr, H_b, KC, wfm1, L1_FM, zfm1, wtm1, L1_TM, ztm1, WR, "p1")
            P.barrier()
            hgrn_phase(c, X, X_b, H, WR, FS, zfm1, zfm_b, ztm1, ztm_b, d1, wout1, sq, sq_b, rstd, rstd_b, outputs=True, hstate=hstate, first=False)
            P.barrier()
            prep = lambda: dsa_prep(c, X, zfm1, zfm_b, ztm1, ztm_b, sq, sq_b, rstd, rstd_b, ckvg_f, ckvg_t, keys["kvT_all"], keys["kvtm_all"], keys["kixT_all"], 0)
            dsa_phase(c, X, X_b, H, WR, FS, zfm1, zfm_b, ztm1, ztm_b, wout1, sq, sq_b, rstd, rstd_b, None, keys=keys, prep=prep)
            P.barrier()
            rmsnorm_fm(c, X, X_b, Hr, H_b, gf1, gf1_b, sq, sq_b, rstd, rstd_b)
            ffn_phase(c, X, X_b, Hr, H_b, wg1, wu1, wd1, WR, ftmp, ftmp2)
            store_X(c, X, X_b, out)
    print("fused program:", {e: len(v) for e, v in P.q.items()}, "lanes", len(P.alllanes))
    return c.finish(X_b)


def kernel(x, ln_mix_g, ln_ffn_g, w_ffn_gate, w_ffn_up, w_ffn_down, rel_bias,
           ev_w_in, ev_w_out, sgu_ln_g, sgu_ln_b, sgu_w_s, sgu_b_s,
           od_w_in, od_w_out, hgrn_lb, hgrn_norm_g, dsa_cq_g, dsa_ckv_g,
           dsa_w_uq, dsa_qnorm_g, dsa_w_qidx, dsa_w_uv):
    f = lambda a: np.ascontiguousarray(np.asarray(a, dtype=np.float32))
    x = f(x)
    w_in0 = f(ev_w_in)[0]
    wfm1, wtm1 = l1_weights(f(od_w_in)[0])
    ltri, mtri, ident = l1_tables()
    lb = f(hgrn_lb)
    ckvg = f(dsa_ckv_g)[0]
    rb = f(rel_bias)
    dmask, xi, kz, _ = k2_tables(0)
    common = {
        "g_mix0": vec_fm(f(ln_mix_g)[0]), "g_ffn0": vec_fm(f(ln_ffn_g)[0]), "g_mix1": vec_fm(f(ln_mix_g)[1]), "g_ffn1": vec_fm(f(ln_ffn_g)[1]),
        "wfm0": tile_fm(np.ascontiguousarray(np.concatenate([w_in0[:, 0:2048], w_in0[:, 3072:5120]], 1))),
        "wtm0": tile_tm(np.ascontiguousarray(np.concatenate([w_in0[:, 1024:3072], w_in0[:, 5120:6144]], 1))),
        "wfm1": wfm1, "wtm1": wtm1, "dmask": dmask, "xi": xi, "kz": kz,
        "lng_b": bcast_rows(f(sgu_ln_g)[0]), "lnb_b": bcast_rows(f(sgu_ln_b)[0]),
        "wsT": np.ascontiguousarray(f(sgu_w_s)[0].transpose(2, 0, 1)), "bsb": bcast_rows(f(sgu_b_s)[0]),
        "wout0": f(ev_w_out)[0], "wout1": f(od_w_out)[0],
        "wg0": tile_fm(f(w_ffn_gate)[0]), "wu0": tile_fm(f(w_ffn_up)[0]), "wd0": f(w_ffn_down)[0],
        "wg1": tile_fm(f(w_ffn_gate)[1]), "wu1": tile_fm(f(w_ffn_up)[1]), "wd1": f(w_ffn_down)[1],
        "lb0f": vec_fm(lb[0]), "lb1f": vec_fm(lb[1]), "lb0t": bcast_rows(lb[0]), "lb1t": bcast_rows(lb[1]),
        "ltri": ltri, "mtri": mtri, "ident": ident, "ngain": vec_fm(f(hgrn_norm_g)[0]),
        "ckvg_f": vec_fm(ckvg), "ckvg_t": bcast_rows(ckvg),
        "wuq": tile_fm(f(dsa_w_uq)[0]), "wqi": tile_fm(f(dsa_w_qidx)[0]), "cqg": vec_fm(f(dsa_cq_g)[0]), "qng": vec_fm(f(dsa_qnorm_g)[0]),
        "wuv": np.ascontiguousarray(f(dsa_w_uv)[0].reshape(8, 2, 128, 128).transpose(2, 0, 1, 3)),
        "crow": bcast_rows(rb[15]), "ident2": np.eye(128, dtype=np.float32),
    }
    t = np.arange(128)
    ok = (t[None, :] // 64) <= (t[:, None] // 64)
    common["admok"] = ok.astype(np.float32)
    common["admadd"] = np.where(ok, 0.0, -BIG).astype(np.float32)
    s_ = np.arange(128)[:, None, None]; nb = np.arange(3)[None, :, None]; tt = np.arange(128)[None, None, :]
    bk = rel_bucket_np((nb - 2) * 128 + s_ - tt)
    common["bias_g"] = np.ascontiguousarray(rb[bk].transpose(0, 3, 1, 2)).astype(np.float32)
    ropes = [rope_tables(q) for q in range(4)]
    ims = []
    for c in range(NCORES):
        b, q = c // 4, c % 4
        xs = np.zeros((4, 128, KC, T), np.float32)
        slotadd = np.zeros((128, 3), np.float32); slotok = np.ones((128, 3), np.float32)
        qs = [max(q - 3 + p, 0) for p in range(4)]
        for p in range(4):
            qq = q - 3 + p
            if qq >= 0:
                xs[p] = x_to_fm(x[b, qq * T:(qq + 1) * T])
            else:
                slotadd[:, 3 - p - 1] = -BIG
                slotok[:, 3 - p - 1] = 0.0
        im = dict(common)
        im.update({"x_slots": xs, "slotadd": slotadd, "slotok": slotok,
                   "cosT": np.stack([ropes[k][0][0] for k in qs]), "sinT": np.stack([ropes[k][0][1] for k in qs]),
                   "cos_tm": np.stack([ropes[k][1][0] for k in qs]), "sin_tm": np.stack([ropes[k][1][1] for k in qs])})
        ims.append(im)
    res = _run(_prog("fused", build_fused), ims)
    out = np.zeros((2, 4096, D), np.float32)
    for c in range(NCORES):
        b, q = c // 4, c % 4
        out[b, q * T:(q + 1) * T] = fm_to_x(res[c]["out"])
    return out
```
